# Optimizing a Trainium2 kernel written in Bass

```python
import jax, jax.numpy as jnp
from jax import lax
import numpy as np

D_MODEL = 1024
BATCH = 16
SEQ = 256
DEPTH = 4
DEC_BATCH = 8
DEC_SEQ = 2048
PAST_LEN = 256

GRID_W = 64
HEAD_DIM = 64
EPS = 1e-6
NA_HEADS = 8
NA_WIDTH = NA_HEADS * HEAD_DIM
NA_KR = 8
NA_KC = 16
GLA_HEADS = 4
GLA_DK = 64
GLA_DV = 128
GLA_KW = GLA_HEADS * GLA_DK
GLA_VW = GLA_HEADS * GLA_DV
GLA_RANK = 16
GLA_GATE_NORM = 16.0
GLA_CHUNK = 64
GQA_HEADS = 16
GQA_KV_HEADS = 4
GQA_GROUP = GQA_HEADS // GQA_KV_HEADS
C_QW = GQA_HEADS * HEAD_DIM
C_KW = GQA_KV_HEADS * HEAD_DIM
C_IN = C_QW + 2 * C_KW
ROPE_THETA = 10000.0
ROPE_AXIS_DIM = HEAD_DIM // 2
Q_BLOCK = 128
D_FF = 2816
CONV_W = 3
N_AB = (DEPTH + 1) // 2
N_C = DEPTH // 2
AB_IN = 3 * NA_WIDTH + 2 * GLA_KW + 2 * GLA_VW + 2 * GLA_RANK
AB_OUT = NA_WIDTH + GLA_VW
AB_SPLITS = (NA_WIDTH, 2 * NA_WIDTH, 3 * NA_WIDTH,
             3 * NA_WIDTH + GLA_KW, 3 * NA_WIDTH + 2 * GLA_KW,
             3 * NA_WIDTH + 2 * GLA_KW + GLA_VW, 3 * NA_WIDTH + 2 * GLA_KW + 2 * GLA_VW,
             3 * NA_WIDTH + 2 * GLA_KW + 2 * GLA_VW + GLA_RANK)

kernel_name = "hybrid_na_gla_gqa_prefix_dit_step"


def rms_norm(x, g):
    xf = x.astype(jnp.float32)
    xf = xf * lax.rsqrt(jnp.mean(xf * xf, axis=-1, keepdims=True) + EPS)
    return (xf * g.astype(jnp.float32)).astype(x.dtype)


def heads(a, nh):
    b, n, w = a.shape
    return a.reshape(b, n, nh, w // nh).transpose(0, 2, 1, 3)


def merge(a):
    b, h, n, d = a.shape
    return a.transpose(0, 2, 1, 3).reshape(b, n, h * d)


def modulation(cvec, w, bias):
    mod = (jax.nn.silu(cvec) @ w + bias).reshape(-1, 1, 6 * D_MODEL)
    return jnp.split(mod, 6, axis=-1)


def block_attention(q, k, v):
    b, kvh, g, nq, d = q.shape
    nb = nq // Q_BLOCK
    qb = jnp.moveaxis(q.reshape(b, kvh, g, nb, Q_BLOCK, d), 3, 0)
    scale = d ** -0.5

    def one(qblk):
        s = jnp.einsum('bhgqd,bhkd->bhgqk', qblk, k).astype(jnp.float32) * scale
        p = jax.nn.softmax(s, axis=-1).astype(v.dtype)
        return jnp.einsum('bhgqk,bhkd->bhgqd', p, v)

    o = lax.map(one, qb)
    return jnp.moveaxis(o, 0, 3).reshape(b, kvh, g, nq, d)


def grid_angles(n):
    t = jnp.arange(n)
    row = (t // GRID_W).astype(jnp.float32)
    col = (t % GRID_W).astype(jnp.float32)
    inv = ROPE_THETA ** (-jnp.arange(0, ROPE_AXIS_DIM, 2, dtype=jnp.float32) / ROPE_AXIS_DIM)
    return row[:, None] * inv, col[:, None] * inv


def rope_half(x, ang):
    x1, x2 = jnp.split(x, 2, axis=-1)
    cos, sin = jnp.cos(ang).astype(x.dtype), jnp.sin(ang).astype(x.dtype)
    return jnp.concatenate([x1 * cos - x2 * sin, x2 * cos + x1 * sin], axis=-1)


def rope_2d(x, ang_r, ang_c):
    xr, xc = jnp.split(x, 2, axis=-1)
    return jnp.concatenate([rope_half(xr, ang_r), rope_half(xc, ang_c)], axis=-1)


def neighbourhood_attention(q, k, v, k_ctx, v_ctx, rpb):
    b, h, n, d = q.shape
    rows = n // GRID_W
    kr = min(NA_KR, rows)
    r = jnp.arange(rows)
    rs = jnp.clip(r - kr // 2, 0, rows - kr)
    key_rows = rs[:, None] + jnp.arange(kr)[None, :]
    col = jnp.arange(GRID_W)
    cs = jnp.clip(col - NA_KC // 2, 0, GRID_W - NA_KC)
    col_ok = (col[None, :] >= cs[:, None]) & (col[None, :] < cs[:, None] + NA_KC)
    mask = jnp.broadcast_to(col_ok[:, None, :], (GRID_W, kr, GRID_W)).reshape(GRID_W, kr * GRID_W)
    qg = q.reshape(b, h, rows, GRID_W, d)
    kg = k.reshape(b, h, rows, GRID_W, d)[:, :, key_rows].reshape(b, h, rows, kr * GRID_W, d)
    vg = v.reshape(b, h, rows, GRID_W, d)[:, :, key_rows].reshape(b, h, rows, kr * GRID_W, d)
    dr = key_rows - r[:, None] + NA_KR - 1
    dc = jnp.clip(col[None, :] - col[:, None] + NA_KC - 1, 0, 2 * NA_KC - 2)
    bias = rpb.astype(jnp.float32)[:, dr[:, None, :, None], dc[None, :, None, :]]
    bias = bias.reshape(h, rows, GRID_W, kr * GRID_W)
    scale = d ** -0.5
    s_loc = jnp.einsum('bhrqd,bhrkd->bhrqk', qg, kg).astype(jnp.float32) * scale + bias
    s_loc = jnp.where(mask, s_loc, -jnp.inf)
    s_ctx = jnp.einsum('bhrqd,bhld->bhrql', qg, k_ctx).astype(jnp.float32) * scale
    p = jax.nn.softmax(jnp.concatenate([s_loc, s_ctx], axis=-1), axis=-1).astype(v.dtype)
    nk = kr * GRID_W
    o = (jnp.einsum('bhrqk,bhrkd->bhrqd', p[..., :nk], vg)
         + jnp.einsum('bhrql,bhld->bhrqd', p[..., nk:], v_ctx))
    return o.reshape(b, h, n, d)


def gla_scan(q, k, v, g, s0):
    b, h, t, dk = q.shape
    dv = v.shape[-1]
    nc = t // GLA_CHUNK

    def chunks(a):
        return jnp.moveaxis(a.reshape(b, h, nc, GLA_CHUNK, a.shape[-1]), 2, 0)

    causal = jnp.tril(jnp.ones((GLA_CHUNK, GLA_CHUNK), dtype=bool))

    def step(s, inp):
        qc, kc, vc, gc = inp
        bcum = jnp.cumsum(gc.astype(jnp.float32), axis=2)
        o_inter = jnp.einsum('bhid,bhde->bhie', qc * jnp.exp(bcum).astype(qc.dtype), s)
        diff = bcum[:, :, :, None, :] - bcum[:, :, None, :, :]
        decay = jnp.exp(jnp.where(causal[:, :, None], diff, -jnp.inf)).astype(qc.dtype)
        att = jnp.einsum('bhid,bhjd,bhijd->bhij', qc, kc, decay)
        o = o_inter + jnp.einsum('bhij,bhje->bhie', att, vc)
        blast = bcum[:, :, -1:, :]
        k_dec = kc * jnp.exp(blast - bcum).astype(kc.dtype)
        s_new = (jnp.exp(blast[:, :, 0, :])[..., None].astype(s.dtype) * s
                 + jnp.einsum('bhjd,bhje->bhde', k_dec, vc)).astype(s.dtype)
        return s_new, o

    s_fin, o = lax.scan(step, s0, (chunks(q), chunks(k), chunks(v), chunks(g)))
    return jnp.moveaxis(o, 0, 2).reshape(b, h, t, dv), s_fin


def gla_bidir(q, k, v, g_f, g_b, s0_f, s0_b):
    o_f, s_f = gla_scan(q, k, v, g_f, s0_f)
    flip = lambda a: jnp.flip(a, axis=2)
    o_b, s_b = gla_scan(flip(q), flip(k), flip(v), flip(g_b), s0_b)
    return o_f + flip(o_b), s_f, s_b


def ab_mixer(h, w_in, w_out, rpb, w2_f, b2_f, w2_b, b2_b, gla_g, ctx):
    b, n, _ = h.shape
    qa, ka, va, qg, kg, vg, rg, lr_f, lr_b = jnp.split(h @ w_in, AB_SPLITS, axis=-1)
    qa, ka, va = heads(qa, NA_HEADS), heads(ka, NA_HEADS), heads(va, NA_HEADS)
    qg = heads(qg, GLA_HEADS) * (GLA_DK ** -0.5)
    kg, vg = heads(kg, GLA_HEADS), heads(vg, GLA_HEADS)
    gf = heads(jax.nn.log_sigmoid(lr_f @ w2_f + b2_f) / GLA_GATE_NORM, GLA_HEADS)
    gb = heads(jax.nn.log_sigmoid(lr_b @ w2_b + b2_b) / GLA_GATE_NORM, GLA_HEADS)
    if ctx is None:
        oa = block_attention(qa[:, :, None], ka, va)[:, :, 0]
        zeros = jnp.zeros((b, GLA_HEADS, GLA_DK, GLA_DV), h.dtype)
        og, s_f, s_b = gla_bidir(qg, kg, vg, gf, gb, zeros, zeros)
        new = (ka, va, s_f, s_b)
    else:
        k_ctx, v_ctx, s0_f, s0_b = ctx
        oa = neighbourhood_attention(qa, ka, va, k_ctx, v_ctx, rpb)
        og, _, _ = gla_bidir(qg, kg, vg, gf, gb, s0_f, s0_b)
        new = None
    og = merge(rms_norm(og, gla_g.reshape(GLA_HEADS, 1, GLA_DV))) * jax.nn.silu(rg)
    out = jnp.concatenate([merge(oa), og], axis=-1) @ w_out
    return out, new


def c_mixer(h, w_in, w_out, qn_g, kn_g, ctx):
    b, n, _ = h.shape
    q, k, v = jnp.split(h @ w_in, (C_QW, C_QW + C_KW), axis=-1)
    q = rms_norm(heads(q, GQA_HEADS), qn_g)
    k = rms_norm(heads(k, GQA_KV_HEADS), kn_g)
    v = heads(v, GQA_KV_HEADS)
    if ctx is None:
        o = block_attention(q.reshape(b, GQA_KV_HEADS, GQA_GROUP, n, HEAD_DIM), k, v)
        new = (k, v)
    else:
        k_ctx, v_ctx = ctx
        ang_r, ang_c = grid_angles(n)
        q = rope_2d(q, ang_r, ang_c)
        k = rope_2d(k, ang_r, ang_c)
        k_all = jnp.concatenate([k, k_ctx], axis=2)
        v_all = jnp.concatenate([v, v_ctx], axis=2)
        o = block_attention(q.reshape(b, GQA_KV_HEADS, GQA_GROUP, n, HEAD_DIM), k_all, v_all)
        new = None
    return merge(o.reshape(b, GQA_HEADS, n, HEAD_DIM)) @ w_out, new


def conv_ffn(h, w_up, conv_w, conv_b, w_down):
    u = h @ w_up
    up = jnp.pad(u, ((0, 0), (1, 1), (0, 0)))
    u = up[:, :-2] * conv_w[0] + up[:, 1:-1] * conv_w[1] + up[:, 2:] * conv_w[2] + conv_b
    val, gate = jnp.split(u, 2, axis=-1)
    return (jax.nn.silu(gate) * val) @ w_down


def setup_inputs(seed: int = 0) -> dict:
    key = jax.random.key(seed)
    ks = jax.random.split(key, 40)
    nrm = lambda i, shape, s=1.0: jax.random.normal(ks[i], shape, jnp.float32) * s
    D = D_MODEL
    return {
        "x_prompt": nrm(0, (BATCH, SEQ, D)),
        "x_sample": nrm(1, (DEC_BATCH, DEC_SEQ, D)),
        "cache_na_k": nrm(2, (DEC_BATCH, N_AB, NA_HEADS, PAST_LEN, HEAD_DIM)),
        "cache_na_v": nrm(3, (DEC_BATCH, N_AB, NA_HEADS, PAST_LEN, HEAD_DIM)),
        "state_gla_fwd": nrm(4, (DEC_BATCH, N_AB, GLA_HEADS, GLA_DK, GLA_DV)),
        "state_gla_bwd": nrm(5, (DEC_BATCH, N_AB, GLA_HEADS, GLA_DK, GLA_DV)),
        "cache_gqa_k": nrm(6, (DEC_BATCH, N_C, GQA_KV_HEADS, PAST_LEN, HEAD_DIM)),
        "cache_gqa_v": nrm(7, (DEC_BATCH, N_C, GQA_KV_HEADS, PAST_LEN, HEAD_DIM)),
        "c": nrm(8, (DEC_BATCH, D)),
        "c_ctx": nrm(9, (D,)),
        "ada_w": nrm(10, (DEPTH, D, 6 * D), 0.5 * D ** -0.5),
        "ada_b": nrm(11, (DEPTH, 6 * D), 0.02),
        "norm_mix_g": 1.0 + nrm(12, (DEPTH, D), 0.1),
        "norm_ffn_g": 1.0 + nrm(13, (DEPTH, D), 0.1),
        "ab_w_in": nrm(14, (N_AB, D, AB_IN), D ** -0.5),
        "ab_w_out": nrm(15, (N_AB, AB_OUT, D), AB_OUT ** -0.5),
        "na_rpb": nrm(16, (N_AB, NA_HEADS, 2 * NA_KR - 1, 2 * NA_KC - 1), 0.2),
        "gla_w2_fwd": nrm(17, (N_AB, GLA_RANK, GLA_KW), GLA_RANK ** -0.5),
        "gla_b2_fwd": nrm(18, (N_AB, GLA_KW), 0.5),
        "gla_w2_bwd": nrm(19, (N_AB, GLA_RANK, GLA_KW), GLA_RANK ** -0.5),
        "gla_b2_bwd": nrm(20, (N_AB, GLA_KW), 0.5),
        "gla_norm_g": 1.0 + nrm(21, (N_AB, GLA_VW), 0.1),
        "gqa_w_in": nrm(22, (N_C, D, C_IN), D ** -0.5),
        "gqa_w_out": nrm(23, (N_C, C_QW, D), C_QW ** -0.5),
        "gqa_q_norm_g": 1.0 + nrm(24, (N_C, HEAD_DIM), 0.1),
        "gqa_k_norm_g": 1.0 + nrm(25, (N_C, HEAD_DIM), 0.1),
        "ffn_w_up": nrm(26, (DEPTH, D, 2 * D_FF), D ** -0.5),
        "ffn_conv_w": nrm(27, (DEPTH, CONV_W, 2 * D_FF), CONV_W ** -0.5),
        "ffn_conv_b": nrm(28, (DEPTH, 2 * D_FF), 0.02),
        "ffn_w_down": nrm(29, (DEPTH, D_FF, D), D_FF ** -0.5),
        "final_norm_g": 1.0 + nrm(30, (D,), 0.1),
    }


def reference(x_prompt, x_sample, cache_na_k, cache_na_v, state_gla_fwd, state_gla_bwd,
              cache_gqa_k, cache_gqa_v, c, c_ctx, ada_w, ada_b, norm_mix_g, norm_ffn_g,
              ab_w_in, ab_w_out, na_rpb, gla_w2_fwd, gla_b2_fwd, gla_w2_bwd, gla_b2_bwd, gla_norm_g,
              gqa_w_in, gqa_w_out, gqa_q_norm_g, gqa_k_norm_g,
              ffn_w_up, ffn_conv_w, ffn_conv_b, ffn_w_down, final_norm_g):
    xp, xs = x_prompt, x_sample
    na_k, na_v, gla_f, gla_b, gq_k, gq_v = [], [], [], [], [], []
    for i in range(DEPTH):
        j = i // 2
        sh1p, sc1p, g1p, sh2p, sc2p, g2p = modulation(c_ctx, ada_w[i], ada_b[i])
        sh1s, sc1s, g1s, sh2s, sc2s, g2s = modulation(c, ada_w[i], ada_b[i])
        hp = rms_norm(xp, norm_mix_g[i]) * (1.0 + sc1p) + sh1p
        hs = rms_norm(xs, norm_mix_g[i]) * (1.0 + sc1s) + sh1s
        if i % 2 == 0:
            prm = (ab_w_in[j], ab_w_out[j], na_rpb[j], gla_w2_fwd[j], gla_b2_fwd[j],
                   gla_w2_bwd[j], gla_b2_bwd[j], gla_norm_g[j])
            op, (ka, va, sf, sb) = ab_mixer(hp, *prm, ctx=None)
            os_, _ = ab_mixer(hs, *prm, ctx=(cache_na_k[:, j], cache_na_v[:, j],
                                             state_gla_fwd[:, j], state_gla_bwd[:, j]))
            na_k.append(ka)
            na_v.append(va)
            gla_f.append(sf)
            gla_b.append(sb)
        else:
            prm = (gqa_w_in[j], gqa_w_out[j], gqa_q_norm_g[j], gqa_k_norm_g[j])
            op, (kc, vc) = c_mixer(hp, *prm, ctx=None)
            os_, _ = c_mixer(hs, *prm, ctx=(cache_gqa_k[:, j], cache_gqa_v[:, j]))
            gq_k.append(kc)
            gq_v.append(vc)
        xp = xp + g1p * op
        xs = xs + g1s * os_
        ffn = (ffn_w_up[i], ffn_conv_w[i], ffn_conv_b[i], ffn_w_down[i])
        xp = xp + g2p * conv_ffn(rms_norm(xp, norm_ffn_g[i]) * (1.0 + sc2p) + sh2p, *ffn)
        xs = xs + g2s * conv_ffn(rms_norm(xs, norm_ffn_g[i]) * (1.0 + sc2s) + sh2s, *ffn)
    y_prompt = rms_norm(xp, final_norm_g)
    y_sample = rms_norm(xs, final_norm_g)
    new_na_k = jnp.stack(na_k, axis=1)
    new_na_v = jnp.stack(na_v, axis=1)
    new_gla_fwd = jnp.stack(gla_f, axis=1)
    new_gla_bwd = jnp.stack(gla_b, axis=1)
    new_gqa_k = jnp.stack(gq_k, axis=1)
    new_gqa_v = jnp.stack(gq_v, axis=1)
    return (y_prompt, y_sample, new_na_k, new_na_v, new_gla_fwd, new_gla_bwd, new_gqa_k, new_gqa_v)
```

```python
import concourse.bass as bass
import concourse.mybir as mybir
from contextlib import ExitStack

F32 = mybir.dt.float32
BF16 = mybir.dt.bfloat16
AF = mybir.ActivationFunctionType
ALU = mybir.AluOpType

ENGS = ("pe", "act", "dve", "pool", "sp")
DMA_RING = 8


class _Op:
    __slots__ = ("eng", "fn", "deps", "dma", "signal", "seq", "ring", "ringval", "idx")

    def __init__(self, eng, fn, dma):
        self.eng = eng
        self.fn = fn
        self.dma = dma
        self.deps = []
        self.signal = False
        self.seq = 0
        self.ring = None
        self.ringval = 0


class Prog:
    def __init__(self, nc):
        self.nc = nc
        self.ops = []
        self.state = {}
        self.ndma = {e: 0 for e in ENGS}
        self.last_op = {e: None for e in ENGS}
        self.dma_ops = {e: [] for e in ENGS}
        self._cap = None

    def capture(self, thunk):
        assert self._cap is None
        self._cap = []
        thunk()
        lst, self._cap = self._cap, None
        return lst

    @staticmethod
    def merge(la, lb):
        out = []
        ia = ib = 0
        while ia < len(la) or ib < len(lb):
            fa = ia / max(1, len(la))
            fb = ib / max(1, len(lb))
            if ib >= len(lb) or (ia < len(la) and fa <= fb):
                out.append(la[ia])
                ia += 1
            else:
                out.append(lb[ib])
                ib += 1
        return out

    def replay(self, lst):
        for it in lst:
            self.op(*it)

    def replay_merged(self, la, lb):
        self.replay(self.merge(la, lb))

    @staticmethod
    def _conf(a, b):
        n = min(len(a), len(b))
        return a[:n] == b[:n]

    def _collect(self, op, keys, is_write):
        for k in keys:
            name, sub = k[0], tuple(k[1:])
            tab = self.state.setdefault(name, {})
            for s2, st in tab.items():
                if self._conf(sub, s2):
                    if st[0] is not None:
                        op.deps.append(st[0])
                    if is_write:
                        op.deps.extend(st[1])

    def _update(self, op, reads, writes):
        for k in reads:
            name, sub = k[0], tuple(k[1:])
            tab = self.state[name]
            st = tab.setdefault(sub, [None, []])
            st[1].append(op)
        for k in writes:
            name, sub = k[0], tuple(k[1:])
            tab = self.state[name]
            for s2 in [s for s in tab if len(s) >= len(sub) and s[:len(sub)] == sub]:
                del tab[s2]
            tab[sub] = [op, []]

    def op(self, eng, fn, R=(), W=(), dma=False, extra=()):
        if self._cap is not None:
            self._cap.append((eng, fn, list(R), list(W), dma, list(extra)))
            return None
        o = _Op(eng, fn, dma)
        o.idx = len(self.ops)
        W = list(W) + [k for k in R if k[0] == "ps"]
        R = [k for k in R if k[0] != "ps"]
        self._collect(o, R, False)
        self._collect(o, W, True)
        o.deps.extend(extra)
        self._update(o, R, W)
        if dma:
            j = self.ndma[eng]
            self.ndma[eng] += 1
            o.ring = j % DMA_RING
            o.ringval = 16 * (j // DMA_RING + 1)
            if j >= DMA_RING:
                o.deps.append(self.dma_ops[eng][j - DMA_RING])
            self.dma_ops[eng].append(o)
        self.ops.append(o)
        self.last_op[eng] = o
        return o

    def dma(self, eng, out, in_, R=(), W=()):
        return self.op(eng, lambda e: e.dma_start(out=out, in_=in_), R, W, dma=True)

    def barrier(self):
        pend = [o for o in self.last_op.values() if o is not None]
        for e in ENGS:
            pend.extend(self.dma_ops[e][-DMA_RING:])
        for e in ENGS:
            if e == "sp" and False:
                continue
            self.op(e, None, extra=pend)

    def emit(self):
        nc = self.nc
        for o in self.ops:
            for d in o.deps:
                if not d.dma:
                    if d.eng == "pe" and o.eng == "pe" and not o.dma:
                        continue
                    d.signal = True
        cnt = {e: 0 for e in ENGS}
        for o in self.ops:
            if o.dma:
                continue
            if o.fn is None:
                continue
            if o.signal:
                cnt[o.eng] += 1
                o.seq = cnt[o.eng]
        with ExitStack() as es:
            esem = {e: es.enter_context(nc.semaphore("s_" + e)) for e in ENGS}
            rsem = {e: [es.enter_context(nc.semaphore("r_%s%d" % (e, i))) for i in range(DMA_RING)]
                    for e in ENGS if self.ndma[e] > 0}
            block = es.enter_context(nc.Block())
            per = {e: [o for o in self.ops if o.eng == e] for e in ENGS}
            handles = {"pe": "tensor", "act": "scalar", "dve": "vector", "pool": "gpsimd", "sp": "sync"}

            def make(e):
                def body(eng):
                    waited = {}
                    for o in per[e]:
                        for d in o.deps:
                            if d.dma:
                                key = ("r", d.eng, d.ring)
                                sem, val = rsem[d.eng][d.ring], d.ringval
                            else:
                                if d.eng == "pe" and e == "pe" and not o.dma:
                                    continue
                                if d.fn is None:
                                    continue
                                key = ("e", d.eng)
                                sem, val = esem[d.eng], d.seq
                            if waited.get(key, 0) >= val:
                                continue
                            waited[key] = val
                            eng.wait_ge(sem, val)
                        if o.fn is None:
                            continue
                        ins = o.fn(eng)
                        if o.dma:
                            ins.then_inc(rsem[e][o.ring], 16)
                        elif o.signal:
                            ins.then_inc(esem[e], 1)
                    for d in self.dma_ops[e][-DMA_RING:]:
                        key = ("r", e, d.ring)
                        if waited.get(key, 0) >= d.ringval:
                            continue
                        waited[key] = d.ringval
                        eng.wait_ge(rsem[e][d.ring], d.ringval)
                return body

            for e in ENGS:
                if per[e]:
                    getattr(block, handles[e])(make(e))


import numpy as np
import ml_dtypes
from concourse.bass_utils import run_bass_kernel_spmd

D = 1024
NT = 2560
NTILE = 20
DEPTH = 4
DFF = 2816
NJ = 22
EPS = 1e-6
SEQS = [(0, 256), (256, 512), (512, 2560)]
SEQ_STARTS = {0, 256, 512}
SEQ_ENDS = {256, 512, 2560}
ARENA_BYTES = 112 * 1024


def _prod(s):
    r = 1
    for v in s:
        r *= v
    return r


class Arena:
    def __init__(self, t, nbytes):
        self.t = t
        self.cap = nbytes
        self.off = 0
        self.gen = 0

    def reset(self):
        self.off = 0
        self.gen += 1

    def alloc(self, name, shape, dt):
        esz = 2 if dt == BF16 else 4
        nb = _prod(shape) * esz
        nb_al = (nb + 31) // 32 * 32
        assert self.off + nb_al <= self.cap, (name, self.off, nb_al, self.cap)
        ap = self.t[:, self.off // 2:(self.off + nb) // 2]
        self.off += nb_al
        if dt == F32:
            ap = ap.bitcast(F32)
        if len(shape) == 2:
            ap = ap.rearrange("p (a b) -> p a b", a=shape[0])
        elif len(shape) == 3:
            ap = ap.rearrange("p (a b c) -> p a b c", a=shape[0], b=shape[1])
        elif len(shape) == 4:
            ap = ap.rearrange("p (a b c d) -> p a b c d", a=shape[0], b=shape[1], c=shape[2])
        return ap, "%s@%d" % (name, self.gen)


def vof(t):
    return 0 if t < 512 else 1


def split_blocks(t0, t1, maxw=512):
    out = []
    a = t0
    while a < t1:
        b = min(t1, a + maxw)
        if a < 512 < b:
            b = 512
        out.append((a, b - a))
        a = b
    return out


def build(stop=None, sub=None):
    nc = bass.Bass("TRN2", target_bir_lowering=False)
    din = lambda n, s: nc.dram_tensor(n, list(s), F32, kind="ExternalInput").ap()
    dout = lambda n, s: nc.dram_tensor(n, list(s), F32, kind="ExternalOutput").ap()
    I = {}
    for n, s in [("xT", (D, NT)), ("cv", (128, 8, 2)), ("adaW", (4, 24, 128, 8, 256)), ("adaB", (128, 4, 48, 2)),
                 ("gmix", (128, 4, 8)), ("gffn", (128, 4, 8)), ("gfin", (128, 8)),
                 ("wA", (2, 3, 4, 128, 8, 128)), ("wG", (2, 2, 128, 8, 800)), ("w2x", (2, 128, 2, 256)), ("b2x", (128, 2, 2, 2)),
                 ("glag", (128, 2, 4)), ("wOab", (2, 8, 128, 8, 128)), ("T2", (2, 4, 128, 2, 7, 128)), ("T2i", (2, 4, 128, 2, 7, 128)),
                 ("cnk", (2, 4, 128, 256)), ("cnv", (2, 4, 128, 2, 2, 64)), ("sgf", (2, 128, 2, 128)), ("sgb", (2, 128, 2, 128)),
                 ("wQ", (2, 8, 128, 8, 128)), ("wK", (2, 4, 128, 8, 128)), ("wV", (2, 4, 128, 8, 64)), ("wOc", (2, 8, 128, 8, 128)),
                 ("qng", (128, 2)), ("kng", (128, 2)), ("cgk", (2, 4, 128, 256)), ("cgv", (2, 4, 128, 2, 64)),
                 ("cosT", (128, 2048)), ("sinT", (128, 2048)),
                 ("wUp", (4, NJ, 2, 128, 8, 128)), ("wDn", (4, 8, 128, NJ, 128)), ("convw", (128, 4, 44, 4)),
                 ("consts", (128, 5, 128)), ("masks", (128, 2, 4, 128))]:
        I[n] = din(n, s)
    O = {}
    for n, s in [("yT", (D, NT)), ("o_nak", (2, 4, 128, 512)), ("o_nav", (2, 4, 4, 128, 128)),
                 ("o_gf", (2, 2, 128, 2, 128)), ("o_gb", (2, 2, 128, 2, 128)),
                 ("o_gk", (2, 4, 64, 512)), ("o_gv", (2, 4, 4, 128, 64))]:
        O[n] = dout(n, s)

    with ExitStack() as es:
        sb = lambda n, s, d: es.enter_context(nc.sbuf_tensor(n, list(s), d))
        xT = sb("xT_sb", (128, 8, NT), F32)
        arena_t = sb("arena", (128, ARENA_BYTES // 2), BF16)
        cbf = sb("cbf", (128, 5, 128), BF16)
        mbf = sb("mbf", (128, 2, 4, 128), BF16)
        modT = sb("modT", (128, 4, 48, 2), F32)
        adab = sb("adab", (128, 4, 48, 2), F32)
        gmix = sb("gmix_sb", (128, 4, 8), F32)
        gffn = sb("gffn_sb", (128, 4, 8), F32)
        gfin = sb("gfin_sb", (128, 8), F32)
        gsc = sb("gsc", (128, 9, 8, 2), F32)
        cvs = sb("cvs", (128, 8, 2), F32)
        cvb = sb("cvb", (128, 8, 2), BF16)
        convw = sb("convw_sb", (128, 4, 44, 4), F32)
        glag = sb("glag_sb", (128, 2, 4), F32)
        qng = sb("qng_sb", (128, 2), F32)
        kng = sb("kng_sb", (128, 2), F32)
        w2x = sb("w2x_sb", (128, 2, 2, 256), BF16)
        negb = sb("negb", (128, 2, 2, 2), F32)
        zero1 = sb("zero1", (128, 1), F32)
        psum = es.enter_context(nc.psum_tensor("psum", [128, 8, 512], F32))
        P = Prog(nc)
        A = Arena(arena_t, ARENA_BYTES)
        ident = cbf[:, 0, :]
        ones = cbf[:, 1, :]
        blockones = cbf[:, 2, :]
        rperm = cbf[:, 3, :]

        def PS(b):
            return ("ps", b)

        P.dma("sp", xT[:], I["xT"].rearrange("(k p) t -> p k t", p=128), W=[("xT",)])
        P.dma("pool", cbf[:], I["consts"], W=[("cbf",)])
        P.dma("pool", mbf[:], I["masks"], W=[("mbf",)])
        P.dma("sp", adab[:], I["adaB"], W=[("adab",)])
        P.dma("sp", gmix[:], I["gmix"], W=[("gmix",)])
        P.dma("sp", gffn[:], I["gffn"], W=[("gffn",)])
        P.dma("sp", gfin[:], I["gfin"], W=[("gfin",)])
        P.dma("sp", cvs[:], I["cv"], W=[("cvs",)])
        P.dma("sp", convw[:], I["convw"], W=[("convw",)])
        P.dma("sp", glag[:], I["glag"], W=[("glag",)])
        P.dma("sp", qng[:], I["qng"], W=[("qng",)])
        P.dma("sp", kng[:], I["kng"], W=[("kng",)])
        P.dma("pool", w2x[:], I["w2x"].rearrange("l r d c -> r l d c"), W=[("w2x",)])
        P.dma("sp", negb[:], I["b2x"], W=[("negb",)])
        P.op("dve", lambda e: e.tensor_scalar(out=negb[:], in0=negb[:], scalar1=-1.0, scalar2=None, op0=ALU.mult), R=[("negb",)], W=[("negb",)])
        P.op("pool", lambda e: e.memset(zero1[:], 0.0), W=[("zero1",)])
        P.op("act", lambda e: e.activation(out=cvb[:], in_=cvs[:], func=AF.Silu), R=[("cvs",)], W=[("cvb",)])

        def mod_dma(l, pc, buf):
            P.dma("pool", buf[0], I["adaW"][l, pc], W=[(buf[1],)])

        def mod_compute(l, pc, buf, bank):
            ab, abk = buf
            for fi in range(2):
                for k in range(8):
                    P.op("pe", lambda e, fi=fi, k=k: e.matmul(
                        psum[:, bank, fi * 2:fi * 2 + 2], lhsT=ab[:, k, fi * 128:(fi + 1) * 128], rhs=cvb[:, k, :],
                        start=(k == 0), stop=(k == 7)), R=[(abk,), ("cvb",)], W=[PS(bank)])
            f0 = pc * 2
            P.op("dve", lambda e: e.tensor_tensor(
                out=modT[:, l, f0:f0 + 2, :], in0=psum[:, bank, 0:4].rearrange("p (a b) -> p a b", a=2),
                in1=adab[:, l, f0:f0 + 2, :], op=ALU.add), R=[PS(bank), ("adab",)], W=[("modT", l, pc)])

        def mod_finish(l):
            for which in range(2):
                g = gmix if which == 0 else gffn
                sc0 = 8 + 24 * which
                for v in range(2):
                    P.op("dve", lambda e, which=which, g=g, sc0=sc0, v=v: e.scalar_tensor_tensor(
                        out=gsc[:, l * 2 + which, :, v], in0=modT[:, l, sc0:sc0 + 8, v], scalar=1.0, in1=g[:, l, :],
                        op0=ALU.add, op1=ALU.mult), R=[("modT", l), ("gmix",), ("gffn",)], W=[("gsc", l, which, v)])

        A.reset()
        NB_ADA = 3
        adabuf = [A.alloc("ada%d" % i, (8, 256), BF16) for i in range(NB_ADA)]
        for pc in range(24):
            mod_dma(0, pc, adabuf[pc % NB_ADA])
            mod_compute(0, pc, adabuf[pc % NB_ADA], pc % 2)
        mod_finish(0)
        for v in range(2):
            P.op("dve", lambda e, v=v: e.tensor_copy(out=gsc[:, 8, :, v], in_=gfin[:]), R=[("gfin",)], W=[("gsc", 8, 0, v)])
        P.barrier()

        def norm_block(nidx, l, which, t0, w, dst, dcol, dkey, tmp, final=False, xsrc=None, xkey=None):
            v = vof(t0)
            sqb, sqk = tmp["sq"]
            rs, rsk = tmp["rs"]
            tt, ttk = tmp["tt"]
            bank = tmp["bank"]
            if xsrc is None:
                xsrc = lambda k: xT[:, k, t0:t0 + w]
            xk_ = (lambda k: ("xT", k)) if xkey is None else (lambda k: xkey)
            for k in range(8):
                P.op("act", lambda e, k=k: e.activation(out=sqb[:, k % 2, 0:w], in_=xsrc(k), func=AF.Square),
                     R=[xk_(k)], W=[(sqk, k % 2)])
                P.op("pe", lambda e, k=k: e.matmul(psum[:, bank, 0:w], lhsT=ones, rhs=sqb[:, k % 2, 0:w], start=(k == 0), stop=(k == 7)),
                     R=[(sqk, k % 2), ("cbf",)], W=[PS(bank)])
            P.op("act", lambda e: e.activation(out=rs[:, 0:w], in_=psum[:, bank, 0:w], func=AF.Ln, scale=1.0 / D, bias=zero1_eps[:]),
                 R=[PS(bank), ("eps",)], W=[(rsk,)])
            P.op("act", lambda e: e.activation(out=rs[:, 0:w], in_=rs[:, 0:w], func=AF.Exp, scale=-0.5), R=[(rsk,)], W=[(rsk,)])
            for k in range(8):
                if final:
                    P.op("dve", lambda e, k=k: e.scalar_tensor_tensor(out=dst[:, k, dcol:dcol + w], in0=xsrc(k),
                         scalar=gsc[:, nidx, k, v:v + 1], in1=rs[:, 0:w], op0=ALU.mult, op1=ALU.mult),
                         R=[xk_(k), (rsk,), ("gsc",)], W=[(dkey, k)])
                else:
                    P.op("dve", lambda e, k=k: e.scalar_tensor_tensor(out=tt[:, k % 2, 0:w], in0=xsrc(k),
                         scalar=gsc[:, nidx, k, v:v + 1], in1=rs[:, 0:w], op0=ALU.mult, op1=ALU.mult),
                         R=[xk_(k), (rsk,), ("gsc",)], W=[(ttk, k % 2)])
                    sh = modT[:, l, 24 * which + k, v:v + 1]
                    P.op("act", lambda e, k=k, sh=sh: e.activation(out=dst[:, k, dcol:dcol + w], in_=tt[:, k % 2, 0:w], func=AF.Identity, bias=sh, scale=1.0),
                         R=[(ttk, k % 2), ("modT", l)], W=[(dkey, k)])

        eps_t = sb("eps_t", (128, 1), F32)
        eps64_t = sb("eps64_t", (128, 1), F32)
        zero1_eps = eps_t
        P.op("pool", lambda e: e.memset(eps_t[:], EPS), W=[("eps",)])

        def norm_tmp(bank):
            return {"sq": A.alloc("nsq", (2, 512), BF16), "rs": A.alloc("nrs", (512,), F32),
                    "tt": A.alloc("ntt", (2, 512), F32), "bank": bank}

        def out_proj_residual(l, wsrc_fn, attn, attn_key, nk, gate_f0):
            wb = [A.alloc("wo%d" % i, (nk, 128), BF16) for i in range(2)]
            bi = 0
            P.dma("pool", wb[0][0], wsrc_fn(0), W=[(wb[0][1],)])
            for d in range(8):
                w, wk = wb[d % 2]
                if d + 1 < 8:
                    P.dma("pool", wb[(d + 1) % 2][0], wsrc_fn(d + 1), W=[(wb[(d + 1) % 2][1],)])
                for (t0, wd) in split_blocks(0, NT):
                    bank = 6 + bi % 2
                    bi += 1
                    for k in range(nk):
                        P.op("pe", lambda e, w=w, k=k, t0=t0, wd=wd, bank=bank: e.matmul(psum[:, bank, 0:wd], lhsT=w[:, k, :], rhs=attn[:, k, t0:t0 + wd],
                             start=(k == 0), stop=(k == nk - 1)), R=[(wk,), (attn_key,)], W=[PS(bank)])
                    v = vof(t0)
                    xk = [("xT", d, tt) for tt in range(t0 // 128, (t0 + wd + 127) // 128)]
                    P.op("dve", lambda e, d=d, t0=t0, wd=wd, bank=bank, v=v: e.scalar_tensor_tensor(
                        out=xT[:, d, t0:t0 + wd], in0=psum[:, bank, 0:wd], scalar=modT[:, l, gate_f0 + d, v:v + 1], in1=xT[:, d, t0:t0 + wd],
                        op0=ALU.mult, op1=ALU.add), R=[PS(bank), ("modT", l)] + xk, W=xk)

        def softmax_finish(bank, w, dst_ap, dkeys, rr, rrk):
            P.op("dve", lambda e: e.reciprocal(out=rr[64:128, 0:w], in_=psum[64:128, bank, 0:w]), R=[PS(bank)], W=[(rrk,)])
            P.op("dve", lambda e: e.tensor_tensor(out=dst_ap, in0=psum[0:64, bank, 0:w], in1=rr[64:128, 0:w], op=ALU.mult),
                 R=[PS(bank), (rrk,)], W=dkeys)

        def prompt_attention(qT, qkey, kT, kkey, vtm, vkey, base, dst_fn, tmpP, rrs, cnt, qkeyf=None, k128=False):
            for sq in range(2):
                c0 = sq * 256
                u = cnt[0]
                cnt[0] += 1
                sbank = [0 + 3 * (u % 2), 1 + 3 * (u % 2)]
                obank = 2 + 3 * (u % 2)
                pt, ptk = tmpP[u % 2]
                for kt in range(2):
                    kc = c0 + kt * 128
                    r0_, r1_ = (0, 128) if k128 else (base, base + 64)
                    P.op("pe", lambda e, kc=kc, c0=c0, b=sbank[kt], r0_=r0_, r1_=r1_: e.matmul(psum[:, b, 0:256], lhsT=kT[r0_:r1_, kc:kc + 128],
                         rhs=qT[r0_:r1_, c0:c0 + 256], start=True, stop=True), R=[(kkey,), ((qkey,) if qkeyf is None else qkeyf(c0))], W=[PS(sbank[kt])])
                    P.op("act", lambda e, kt=kt, b=sbank[kt], pt=pt: e.activation(out=pt[:, kt, 0:256], in_=psum[:, b, 0:256], func=AF.Exp, scale=0.125),
                         R=[PS(sbank[kt])], W=[(ptk, kt)])
                for kt in range(2):
                    P.op("pe", lambda e, kt=kt, sq=sq, pt=pt, obank=obank: e.matmul(psum[:, obank, 0:256], lhsT=vtm(sq * 2 + kt), rhs=pt[:, kt, 0:256],
                         start=(kt == 0), stop=(kt == 1)), R=[(ptk, kt), (vkey,)], W=[PS(obank)])
                rr, rrk = rrs[u % 2]
                dst, dkeys = dst_fn(c0, 256)
                softmax_finish(obank, 256, dst, dkeys, rr, rrk)

        def ab_layer(l):
            j = l // 2
            A.reset()
            hT, hk = A.alloc("hT", (8, NT), BF16)
            mark = A.off
            nt = norm_tmp(7)
            for (t0, w) in split_blocks(0, NT):
                norm_block(l * 2, l, 0, t0, w, hT, t0, hk, nt)
            P.barrier()
            if sub is not None and sub <= 1:
                return
            tiles_of = lambda a_, b_: list(range(a_ // 128, b_ // 128))
            psb = psum[:, 7, :].bitcast(BF16)
            def gla_part(pc):
                A.off = mark
                A.gen += 1
                og, ogk = A.alloc("og", (2, NT), BF16)
                wg, wgk = A.alloc("wg", (8, 800), BF16)
                P.dma("pool", wg, I["wG"][j, pc], W=[(wgk,)])
                Sbs, Sbsk = A.alloc("Sbs", (NTILE, 128), BF16)
                S32 = [A.alloc("S32_%d" % d, (128,), F32) for d in range(2)]
                Sfb, Sfbk = A.alloc("Sfb", (128,), BF16)
                lrx, lrxk = A.alloc("lrx", (128,), BF16)
                spb, spk = A.alloc("sp", (2, 128), F32)
                cb, cbk = A.alloc("cc", (2, 128), F32)
                En, Enk = A.alloc("En", (2, 128), F32)
                Ep, Epk = A.alloc("Ep", (2, 128), F32)
                ktm, ktmk = A.alloc("ktm", (128,), BF16)
                m1, m1k = A.alloc("m1", (2, 128), F32)
                m2, m2k = A.alloc("m2", (2, 128), F32)
                att, attk = A.alloc("att", (2, 128), BF16)
                osq, osqk = A.alloc("osq", (256,), BF16)
                ors, orsk = A.alloc("ors", (256,), F32)
                ot, otk = A.alloc("ot", (256,), F32)
                stmp, stmpk = A.alloc("stmp", (128,), F32)
                stg, stgk = A.alloc("stg", (128,), F32)
                ebt2 = [A.alloc("eb%d" % p_, (2,), F32) for p_ in range(2)]
                kt2 = [A.alloc("kt%d" % p_, (2, 128), BF16) for p_ in range(2)]
                vtm2 = [A.alloc("vtm%d" % p_, (256,), BF16) for p_ in range(2)]
                srg2 = [A.alloc("srg%d" % p_, (2, 128), F32) for p_ in range(2)]
                qpad2 = [[[A.alloc("qp%d%d%d" % (p_, d_, i_), (128,), BF16) for i_ in range(2)] for d_ in range(2)] for p_ in range(2)]
                for p_ in range(2):
                    for d_ in range(2):
                        for i_ in range(2):
                            P.op("pool", lambda e, p_=p_, d_=d_, i_=i_: e.memset(qpad2[p_][d_][i_][0][(1 - i_) * 64:(2 - i_) * 64, :], 0.0), W=[("qpz", p_, d_, i_)])

                def gate_stage(c0, dirs, pp):
                    ebt, ebk = ebt2[pp]
                    for k in range(8):
                        P.op("pe", lambda e, k=k: e.matmul(psum[:, 1, 0:128], lhsT=wg[:, k, 672:800], rhs=hT[:, k, c0:c0 + 128],
                             start=(k == 0), stop=(k == 7)), R=[(wgk,), (hk, k)], W=[PS(1)])
                    P.op("act", lambda e: e.activation(out=lrx, in_=psum[:, 1, 0:128], func=AF.Identity), R=[PS(1)], W=[(lrxk,)])
                    for d in dirs:
                        P.op("pe", lambda e, d=d: e.matmul(psum[:, 2, d * 128:(d + 1) * 128], lhsT=w2x[:, j, d, pc * 128:(pc + 1) * 128],
                             rhs=lrx, start=True, stop=True), R=[(lrxk,), ("w2x",)], W=[PS(2)])
                    for d in dirs:
                        P.op("act", lambda e, d=d: e.activation(out=spb[:, d, :], in_=psum[:, 2, d * 128:(d + 1) * 128], func=AF.Exp, scale=-1.0, bias=negb[:, j, d, pc:pc + 1]),
                             R=[PS(2), ("negb",)], W=[(spk, d)])
                        P.op("act", lambda e, d=d: e.activation(out=spb[:, d, :], in_=spb[:, d, :], func=AF.Ln, bias=1.0, scale=1.0), R=[(spk, d)], W=[(spk, d)])
                        P.op("dve", lambda e, d=d: e.tensor_tensor_scan(out=cb[:, d, :], data0=ones_f[:, 0:128], data1=spb[:, d, :],
                             initial=0.0, op0=ALU.mult, op1=ALU.add), R=[(spk, d), ("ones_f",)], W=[(cbk, d)])
                        P.op("act", lambda e, d=d: e.activation(out=ebt[:, d:d + 1], in_=cb[:, d, 127:128], func=AF.Exp, scale=-1.0 / 16), R=[(cbk, d)], W=[(ebk, d)])
                        if d == 1:
                            P.op("dve", lambda e: e.scalar_tensor_tensor(out=cb[:, 1, :], in0=cb[:, 1, :], scalar=cb[:, 1, 127:128],
                                 in1=spb[:, 1, :], op0=ALU.subtract, op1=ALU.subtract), R=[(cbk, 1), (spk, 1)], W=[(cbk, 1)])

                def v_stage(c0, pp):
                    vtm, vtmk = vtm2[pp]
                    for k in range(8):
                        P.op("pe", lambda e, k=k: e.matmul(psum[:, 3, 0:256], lhsT=hT[:, k, c0:c0 + 128], rhs=wg[:, k, 256:512], start=(k == 0), stop=(k == 7)),
                             R=[(wgk,), (hk, k)], W=[PS(3)])
                    P.op("act", lambda e: e.activation(out=vtm, in_=psum[:, 3, 0:256], func=AF.Identity), R=[PS(3)], W=[(vtmk,)])

                def state_update(d, pp):
                    S, Sk = S32[d]
                    ebt, ebk = ebt2[pp]
                    kt, ktk = kt2[pp]
                    vtm, vtmk = vtm2[pp]
                    P.op("pe", lambda e: e.transpose(out=psb[:, 0:128], in_=kt[:, d, :], identity=ident), R=[(ktk, d), ("cbf",)], W=[PS(7)])
                    P.op("act", lambda e: e.activation(out=ktm, in_=psb[:, 0:128], func=AF.Identity), R=[PS(7)], W=[(ktmk,)])
                    P.op("pe", lambda e: e.matmul(psum[:, 5, 256:512], lhsT=ktm, rhs=vtm, start=True, stop=True), R=[(ktmk,), (vtmk,)], W=[PS(5)])
                    for i in range(2):
                        r0 = i * 64
                        P.op("dve", lambda e, i=i, r0=r0: e.tensor_tensor(out=stmp[r0:r0 + 64, :], in0=psum[r0:r0 + 64, 5, 256 + i * 128:256 + (i + 1) * 128],
                             in1=S[r0:r0 + 64, :], op=ALU.add), R=[PS(5), (Sk,)], W=[(stmpk, i)])
                        P.op("dve", lambda e, i=i, r0=r0: e.tensor_scalar(out=S[r0:r0 + 64, :], in0=stmp[r0:r0 + 64, :], scalar1=ebt[r0:r0 + 64, d:d + 1], scalar2=None,
                             op0=ALU.mult), R=[(stmpk, i), (ebk, d)], W=[(Sk,)])

                def state_init(d, si):
                    S, Sk = S32[d]
                    if si < 2:
                        P.op("pool", lambda e: e.memset(S, 0.0), W=[(Sk,)])
                    else:
                        P.dma("sp", S, I["sgb" if d == 1 else "sgf"][j, :, pc, :], W=[(Sk,)])

                def state_out(d, si):
                    S, Sk = S32[d]
                    if si < 2:
                        P.op("act", lambda e: e.activation(out=stg, in_=S, func=AF.Identity), R=[(Sk,)], W=[(stgk,)])
                        P.dma("sp", O["o_gb" if d == 1 else "o_gf"][j, si, :, pc, :], stg, R=[(stgk,)])

                flatB = []
                for si, (s0, s1) in enumerate(SEQS):
                    ts_ = list(reversed(tiles_of(s0, s1)))
                    for q_, t in enumerate(ts_):
                        flatB.append((si, t, q_ == 0, q_ == len(ts_) - 1))

                def b_stage1(idx):
                    si, t, first, last = flatB[idx]
                    pp, c0 = idx % 2, t * 128
                    kt, ktk = kt2[pp]
                    for k in range(8):
                        P.op("pe", lambda e, k=k: e.matmul(psum[:, 0, 0:128], lhsT=wg[:, k, 128:256], rhs=hT[:, k, c0:c0 + 128],
                             start=(k == 0), stop=(k == 7)), R=[(wgk,), (hk, k)], W=[PS(0)])
                    gate_stage(c0, [1], pp)
                    P.op("act", lambda e: e.activation(out=En[:, 1, :], in_=cb[:, 1, :], func=AF.Exp, scale=-1.0 / 16), R=[(cbk, 1)], W=[(Enk, 1)])
                    P.op("dve", lambda e: e.tensor_tensor(out=kt[:, 1, :], in0=psum[:, 0, 0:128], in1=En[:, 1, :], op=ALU.mult),
                         R=[PS(0), (Enk, 1)], W=[(ktk, 1)])
                    v_stage(c0, pp)

                def b_stage2(idx):
                    si, t, first, last = flatB[idx]
                    pp = idx % 2
                    S, Sk = S32[1]
                    if first:
                        state_init(1, si)
                    P.op("act", lambda e: e.activation(out=Sbs[:, t, :], in_=S, func=AF.Identity), R=[(Sk,)], W=[(Sbsk, t)])
                    state_update(1, pp)
                    if last:
                        state_out(1, si)

                for idx in range(len(flatB) + 1):
                    la = P.capture(lambda: b_stage1(idx)) if idx < len(flatB) else []
                    lb = P.capture(lambda: b_stage2(idx - 1)) if idx >= 1 else []
                    P.replay_merged(la, lb)
                if sub is not None and sub <= 2:
                    P.barrier()
                    return
                flatF = []
                for si, (s0, s1) in enumerate(SEQS):
                    ts_ = tiles_of(s0, s1)
                    for q_, t in enumerate(ts_):
                        flatF.append((si, t, q_ == 0, q_ == len(ts_) - 1))

                def f_stage1(idx):
                    si, t, first, last = flatF[idx]
                    pp, c0 = idx % 2, t * 128
                    kt, ktk = kt2[pp]
                    srg, srgk = srg2[pp]
                    qp = qpad2[pp]
                    for qk in range(2):
                        for k in range(8):
                            P.op("pe", lambda e, k=k, qk=qk: e.matmul(psum[:, 0, qk * 128:(qk + 1) * 128],
                                 lhsT=wg[:, k, qk * 128:(qk + 1) * 128], rhs=hT[:, k, c0:c0 + 128], start=(k == 0), stop=(k == 7)),
                                 R=[(wgk,), (hk, k)], W=[PS(0)])
                    gate_stage(c0, [0, 1], pp)
                    P.op("act", lambda e: e.activation(out=En, in_=cb, func=AF.Exp, scale=-1.0 / 16), R=[(cbk,)], W=[(Enk,)])
                    P.op("act", lambda e: e.activation(out=Ep, in_=cb, func=AF.Exp, scale=1.0 / 16), R=[(cbk,)], W=[(Epk,)])
                    qps = psum[:, 0, 0:128]
                    kps = psum[:, 0, 128:256]
                    for i_ in range(2):
                        r0 = i_ * 64
                        P.op("dve", lambda e, i_=i_, r0=r0: e.scalar_tensor_tensor(out=qp[0][i_][0][r0:r0 + 64, :], in0=qps[r0:r0 + 64, :], scalar=0.125, in1=En[r0:r0 + 64, 0, :],
                             op0=ALU.mult, op1=ALU.mult), R=[PS(0), (Enk,)], W=[(qp[0][i_][1],)])
                    P.op("dve", lambda e: e.tensor_tensor(out=kt[:, 0, :], in0=kps, in1=Ep[:, 0, :], op=ALU.mult), R=[PS(0), (Epk,)], W=[(ktk, 0)])
                    for i_ in range(2):
                        r0 = i_ * 64
                        P.op("dve", lambda e, i_=i_, r0=r0: e.scalar_tensor_tensor(out=qp[1][i_][0][r0:r0 + 64, :], in0=qps[r0:r0 + 64, :], scalar=0.125, in1=Ep[r0:r0 + 64, 1, :],
                             op0=ALU.mult, op1=ALU.mult), R=[PS(0), (Epk,)], W=[(qp[1][i_][1],)])
                    P.op("dve", lambda e: e.tensor_tensor(out=kt[:, 1, :], in0=kps, in1=En[:, 1, :], op=ALU.mult), R=[PS(0), (Enk,)], W=[(ktk, 1)])
                    v_stage(c0, pp)
                    for i in range(2):
                        for k in range(8):
                            P.op("pe", lambda e, k=k, i=i: e.matmul(psum[:, 4, i * 128:(i + 1) * 128], lhsT=wg[:, k, 512 + i * 128:512 + (i + 1) * 128],
                                 rhs=hT[:, k, c0:c0 + 128], start=(k == 0), stop=(k == 7)), R=[(wgk,), (hk, k)], W=[PS(4)])
                    P.op("act", lambda e: e.activation(out=srg, in_=psum[:, 4, 0:256].rearrange("p (a b) -> p a b", a=2), func=AF.Exp, scale=-1.0), R=[PS(4)], W=[(srgk,)])
                    P.op("act", lambda e: e.activation(out=srg, in_=srg, func=AF.Ln, bias=1.0, scale=1.0), R=[(srgk,)], W=[(srgk,)])
                    P.op("act", lambda e: e.activation(out=srg, in_=srg, func=AF.Exp, scale=-1.0), R=[(srgk,)], W=[(srgk,)])
                    P.op("dve", lambda e: e.tensor_tensor(out=srg, in0=psum[:, 4, 0:256].rearrange("p (a b) -> p a b", a=2), in1=srg, op=ALU.mult), R=[PS(4), (srgk,)], W=[(srgk,)])

                def f_stage2(idx):
                    si, t, first, last = flatF[idx]
                    pp, c0 = idx % 2, t * 128
                    kt, ktk = kt2[pp]
                    srg, srgk = srg2[pp]
                    qp = qpad2[pp]
                    vtm, vtmk = vtm2[pp]
                    S, Sk = S32[0]
                    if first:
                        state_init(0, si)
                    P.op("act", lambda e: e.activation(out=Sfb, in_=S, func=AF.Identity), R=[(Sk,)], W=[(Sfbk,)])
                    for i in range(2):
                        P.op("pe", lambda e, i=i: e.matmul(psum[:, 5, i * 128:(i + 1) * 128], lhsT=kt[:, 0, :], rhs=qp[0][i][0],
                             start=True, stop=True), R=[(qp[0][i][1],), ("qpz",), (ktk, 0)], W=[PS(5)])
                        P.op("pe", lambda e, i=i: e.matmul(psum[:, 6, i * 128:(i + 1) * 128], lhsT=kt[:, 1, :], rhs=qp[1][i][0],
                             start=True, stop=True), R=[(qp[1][i][1],), ("qpz",), (ktk, 1)], W=[PS(6)])
                    P.op("dve", lambda e: e.tensor_tensor(out=m1, in0=psum[:, 5, 0:256].rearrange("p (a b) -> p a b", a=2), in1=mbf[:, 0, 0:2, :], op=ALU.mult), R=[PS(5), ("mbf",)], W=[(m1k,)])
                    P.op("dve", lambda e: e.tensor_tensor(out=m2, in0=psum[:, 6, 0:256].rearrange("p (a b) -> p a b", a=2), in1=mbf[:, 1, 0:2, :], op=ALU.mult), R=[PS(6), ("mbf",)], W=[(m2k,)])
                    P.op("pool", lambda e: e.tensor_tensor(out=att, in0=m1, in1=m2, op=ALU.add), R=[(m1k,), (m2k,)], W=[(attk,)])
                    for i in range(2):
                        oc = 256 + i * 128
                        P.op("pe", lambda e, i=i, oc=oc: e.matmul(psum[:, 7, oc:oc + 128], lhsT=vtm[:, i * 128:(i + 1) * 128], rhs=att[:, i, :], start=True, stop=False),
                             R=[(vtmk,), (attk,)], W=[PS(7)])
                        P.op("pe", lambda e, i=i, oc=oc: e.matmul(psum[:, 7, oc:oc + 128], lhsT=Sfb, rhs=qp[0][i][0], start=False, stop=False),
                             R=[(Sfbk,), (qp[0][i][1],)], W=[PS(7)])
                        P.op("pe", lambda e, i=i, oc=oc: e.matmul(psum[:, 7, oc:oc + 128], lhsT=Sbs[:, t, :], rhs=qp[1][i][0], start=False, stop=True),
                             R=[(Sbsk, t), (qp[1][i][1],)], W=[PS(7)])
                    P.op("act", lambda e: e.activation(out=osq, in_=psum[:, 7, 256:512], func=AF.Square), R=[PS(7)], W=[(osqk,)])
                    P.op("pe", lambda e: e.matmul(psum[:, 6, 256:512], lhsT=ones, rhs=osq, start=True, stop=True), R=[(osqk,), ("cbf",)], W=[PS(6)])
                    P.op("act", lambda e: e.activation(out=ors, in_=psum[:, 6, 256:512], func=AF.Ln, scale=1.0 / 128, bias=eps_t[:]), R=[PS(6), ("eps",)], W=[(orsk,)])
                    P.op("act", lambda e: e.activation(out=ors, in_=ors, func=AF.Exp, scale=-0.5), R=[(orsk,)], W=[(orsk,)])
                    P.op("dve", lambda e: e.tensor_tensor(out=ot, in0=psum[:, 7, 256:512], in1=ors, op=ALU.mult), R=[PS(7), (orsk,)], W=[(otk,)])
                    for i in range(2):
                        hh = 2 * pc + i
                        P.op("dve", lambda e, i=i, hh=hh: e.scalar_tensor_tensor(out=og[:, i, c0:c0 + 128], in0=ot[:, i * 128:(i + 1) * 128], scalar=glag[:, j, hh:hh + 1],
                             in1=srg[:, i, :], op0=ALU.mult, op1=ALU.mult), R=[(otk,), (srgk,), ("glag",)], W=[(ogk, i)])
                    state_update(0, pp)
                    if last:
                        state_out(0, si)

                for idx in range(len(flatF) + 1):
                    la = P.capture(lambda: f_stage1(idx)) if idx < len(flatF) else []
                    lb = P.capture(lambda: f_stage2(idx - 1)) if idx >= 1 else []
                    P.replay_merged(la, lb)
                P.barrier()
                out_proj_residual(l, lambda d: I["wOab"][j, d, :, 4 + 2 * pc:6 + 2 * pc, :], og, ogk, 2, 16)
                P.barrier()

            for pc_ in range(2):
                gla_part(pc_)
                if sub is not None and sub <= 3 + pc_:
                    return
            A.off = mark
            A.gen += 1
            oa, oak = A.alloc("oa", (1, NT), BF16)
            wq3 = [A.alloc("wa%d" % i, (8, 128), BF16) for i in range(3)]
            qA, qAk = A.alloc("qA", (NT,), BF16)
            kAh = [A.alloc("kA%d" % i, (NT,), BF16) for i in range(2)]
            kAk = "kAall"
            vA, vAk = A.alloc("vA", (NTILE, 2, 128), BF16)
            kst, kstk = A.alloc("kst", (512,), F32)
            vst, vstk = A.alloc("vst", (4, 128), F32)
            T2b, T2k = A.alloc("T2b", (2, 7, 128), BF16)
            T2ib, T2ik = A.alloc("T2ib", (2, 7, 128), BF16)
            kcxh = [A.alloc("kcx%d" % i, (256,), BF16) for i in range(2)]
            kcxk = "kcxall"
            vcx, vcxk = A.alloc("vcx", (2, 2, 128), BF16)
            ptI = [A.alloc("ptI%d" % i, (5, 128), BF16) for i in range(2)]
            ptE = [A.alloc("ptE%d" % i, (4, 128), BF16) for i in range(2)]
            ptC = [A.alloc("ptC%d" % i, (2, 128), BF16) for i in range(2)]
            ptP = [A.alloc("ptP%d" % i, (2, 256), BF16) for i in range(2)]
            tb = [A.alloc("tb%d" % i, (5, 128), F32) for i in range(2)]
            rrs = [A.alloc("rr%d" % i, (256,), F32) for i in range(2)]
            wmark = A.off
            for i in range(2):
                P.op("pool", lambda e, i=i: e.memset(ptI[i][0], 0.0), W=[(ptI[i][1],)])
            P.op("pool", lambda e: e.memset(vA[:, :, :, 64:128], 1.0), W=[(vAk, "ones")])
            P.op("pool", lambda e: e.memset(kAh[0][0][64:128, :], 0.0), W=[("kAz", 0)])
            P.op("pool", lambda e: e.memset(kAh[1][0][0:64, :], 0.0), W=[("kAz", 1)])
            P.op("pool", lambda e: e.memset(kcxh[0][0][64:128, :], 0.0), W=[("kAz", 2)])
            P.op("pool", lambda e: e.memset(kcxh[1][0][0:64, :], 0.0), W=[("kAz", 3)])
            P.op("pool", lambda e: e.memset(vcx[:, :, :, 64:128], 1.0), W=[(vcxk, "ones")])
            cnt = [0]
            for hp in range(4):
                for wi in range(3):
                    P.dma("pool", wq3[wi][0], I["wA"][j, wi, hp], W=[(wq3[wi][1],)])
                import os
                CUT = os.environ.get("CUT", "")
                if "a" not in CUT:
                    P.dma("pool", T2b, I["T2"][j, hp], W=[(T2k,)])
                    P.dma("pool", T2ib, I["T2i"][j, hp], W=[(T2ik,)])
                    P.dma("pool", kcxh[0][0][0:64, :], I["cnk"][j, hp, 0:64, :], W=[(kcxk, 0)])
                    P.dma("pool", kcxh[1][0][64:128, :], I["cnk"][j, hp, 64:128, :], W=[(kcxk, 1)])
                    P.dma("pool", vcx[:, :, :, 0:64], I["cnv"][j, hp], W=[(vcxk, "v")])
                bi = 0
                for wi, (dst, dk_) in enumerate([(qA, qAk), (None, kAk)] if "b" not in CUT else []):
                    for (t0, wd) in split_blocks(0, NT):
                        bank = 6 + (bi % 2)
                        bi += 1
                        for k in range(8):
                            P.op("pe", lambda e, wi=wi, k=k, t0=t0, wd=wd, bank=bank: e.matmul(psum[:, bank, 0:wd], lhsT=wq3[wi][0][:, k, :], rhs=hT[:, k, t0:t0 + wd],
                                 start=(k == 0), stop=(k == 7)), R=[(wq3[wi][1],), (hk, k)], W=[PS(bank)])
                        if wi == 0:
                            P.op("act", lambda e, dst=dst, t0=t0, wd=wd, bank=bank: e.activation(out=dst[:, t0:t0 + wd], in_=psum[:, bank, 0:wd], func=AF.Identity),
                                 R=[PS(bank)], W=[(dk_, t0 // 512)])
                        else:
                            for hf in range(2):
                                r0 = hf * 64
                                P.op("act", lambda e, hf=hf, r0=r0, t0=t0, wd=wd, bank=bank: e.activation(out=kAh[hf][0][r0:r0 + 64, t0:t0 + wd], in_=psum[r0:r0 + 64, bank, 0:wd], func=AF.Identity),
                                     R=[PS(bank)], W=[(dk_, t0 // 512, hf)])
                        if wi == 1 and t0 == 0 and "d" not in CUT:
                            P.op("dve", lambda e, bank=bank: e.tensor_copy(out=kst, in_=psum[:, bank, 0:512]), R=[PS(bank)], W=[(kstk,)])
                            if "e" not in CUT:
                                P.dma("sp", O["o_nak"][j, hp], kst, R=[(kstk,)])
                for t in range(NTILE if "c" not in CUT else 0):
                    bank = 6 + (t % 2)
                    for k in range(8):
                        P.op("pe", lambda e, k=k, t=t, bank=bank: e.matmul(psum[:, bank, 0:128], lhsT=hT[:, k, t * 128:(t + 1) * 128], rhs=wq3[2][0][:, k, :],
                             start=(k == 0), stop=(k == 7)), R=[(wq3[2][1],), (hk, k)], W=[PS(bank)])
                    P.op("act", lambda e, t=t, bank=bank: e.activation(out=vA[:, t, :, 0:64], in_=psum[:, bank, 0:128].rearrange("p (a b) -> p a b", a=2), func=AF.Identity),
                         R=[PS(bank)], W=[(vAk, "v", t)])
                    if t < 4:
                        P.op("dve", lambda e, t=t, bank=bank: e.tensor_copy(out=vst[:, t, :], in_=psum[:, bank, 0:128]), R=[PS(bank)], W=[(vstk, t)])
                P.dma("sp", O["o_nav"][j, hp].rearrange("t p c -> p t c"), vst, R=[(vstk,)])
                if sub == 5:
                    P.barrier()
                    return
                def na_head(i):
                    base = i * 64
                    kAi = kAh[i][0]
                    kci = kcxh[i][0]
                    prompt_attention(qA, qAk, kAi, kAk, lambda tt, i=i: vA[:, tt, i, :], vAk, base,
                                     lambda c0, w, base=base: (oa[base:base + 64, 0, c0:c0 + w], [(oak, base, c0)]), ptP, rrs, cnt, k128=True)
                    nn = 16 if sub is None else {6: 0, 7: 1, 8: 3, 9: 16}.get(sub, 16)
                    u0 = cnt[0]
                    cnt[0] += nn

                    def info(n):
                        u = u0 + n
                        par = u % 2
                        interior = 2 <= n <= 13
                        if interior:
                            ms = list(range(n - 2, n + 3))
                        elif n < 2:
                            ms = [0, 1, 2, 3]
                        else:
                            ms = [12, 13, 14, 15]
                        e0 = (2 * (ms[0] - n) + 7 - 1) // 2
                        return par, 0 + 3 * par, 1 + 3 * par, 2 + 3 * par, 512 + n * 128, interior, ms, e0

                    def st1(n):
                        par, sb_, cb_, ob_, q0, interior, ms, e0 = info(n)
                        nm = len(ms)
                        for ci, m in enumerate(ms):
                            k0 = 512 + m * 128
                            if ci < 4:
                                P.op("pe", lambda e, ci=ci, k0=k0: e.matmul(psum[:, sb_, ci * 128:(ci + 1) * 128], lhsT=kAi[:, k0:k0 + 128],
                                     rhs=qA[:, q0:q0 + 128], start=True, stop=True), R=[(kAk,), ("kAz",), (qAk,)], W=[PS(sb_)])
                            else:
                                P.op("pe", lambda e, k0=k0: e.matmul(psum[:, cb_, 256:384], lhsT=kAi[:, k0:k0 + 128],
                                     rhs=qA[:, q0:q0 + 128], start=True, stop=True), R=[(kAk,), ("kAz",), (qAk,)], W=[PS(cb_)])
                        for cc in range(2):
                            P.op("pe", lambda e, cc=cc: e.matmul(psum[:, cb_, cc * 128:(cc + 1) * 128], lhsT=kci[:, cc * 128:(cc + 1) * 128],
                                 rhs=qA[:, q0:q0 + 128], start=True, stop=True), R=[(kcxk,), ("kAz",), (qAk,)], W=[PS(cb_)])
                        tbb, tbk = tb[par]
                        n4 = min(nm, 4)
                        Tt, Ttk = (T2ib, T2ik) if interior else (T2b, T2k)
                        P.op("dve", lambda e: e.scalar_tensor_tensor(out=tbb[:, 0:n4, :], in0=psum[:, sb_, 0:n4 * 128].rearrange("p (a b) -> p a b", a=n4),
                             scalar=0.125, in1=Tt[:, i, e0:e0 + n4, :], op0=ALU.mult, op1=ALU.add), R=[PS(sb_), (Ttk,)], W=[(tbk, 0)])
                        if nm == 5:
                            P.op("dve", lambda e: e.scalar_tensor_tensor(out=tbb[:, 4, :], in0=psum[:, cb_, 256:384],
                                 scalar=0.125, in1=Tt[:, i, e0 + 4, :], op0=ALU.mult, op1=ALU.add), R=[PS(cb_), (Ttk,)], W=[(tbk, 1)])
                        pC, pCk = ptC[par]
                        P.op("act", lambda e: e.activation(out=pC, in_=psum[:, cb_, 0:256].rearrange("p (a b) -> p a b", a=2), func=AF.Exp, scale=0.125),
                             R=[PS(cb_)], W=[(pCk,)])
                        if interior:
                            pI, pIk = ptI[par]
                            P.op("act", lambda e: e.activation(out=pI, in_=tbb, func=AF.Exp), R=[(tbk,)], W=[(pIk,)])
                        else:
                            pE, pEk = ptE[par]
                            P.op("act", lambda e: e.activation(out=pE, in_=tbb[:, 0:4, :], func=AF.Exp), R=[(tbk,)], W=[(pEk,)])

                    def st2(n):
                        par, sb_, cb_, ob_, q0, interior, ms, e0 = info(n)
                        pl, plk = ptI[par] if interior else ptE[par]
                        pC, pCk = ptC[par]
                        for ci, m in enumerate(ms):
                            tt = 4 + m
                            P.op("pe", lambda e, ci=ci, tt=tt: e.matmul(psum[:, ob_, 0:128], lhsT=vA[:, tt, i, :], rhs=pl[:, ci, :], start=(ci == 0), stop=False),
                                 R=[(vAk,), (plk,)], W=[PS(ob_)])
                        for cc in range(2):
                            P.op("pe", lambda e, cc=cc: e.matmul(psum[:, ob_, 0:128], lhsT=vcx[:, cc, i, :], rhs=pC[:, cc, :], start=False, stop=(cc == 1)),
                                 R=[(vcxk,), (pCk,)], W=[PS(ob_)])
                        rr, rrk = rrs[par]
                        softmax_finish(ob_, 128, oa[base:base + 64, 0, q0:q0 + 128], [(oak, base, q0)], rr, rrk)

                    for n in range(nn + 1):
                        if n < nn:
                            st1(n)
                        if n >= 1:
                            st2(n - 1)
                for i_ in range(2):
                    na_head(i_)
                if sub is not None and sub >= 6:
                    P.barrier()
                    return
                A.off = wmark
                out_proj_residual(l, lambda d, hp=hp: I["wOab"][j, d, :, hp:hp + 1, :], oa, oak, 1, 16)
            P.barrier()

        def gqa_layer(l):
            j = l // 2
            A.reset()
            hT, hk = A.alloc("hT", (8, NT), BF16)
            oT, oTk = A.alloc("attn", (2, NT), BF16)
            nt = norm_tmp(7)
            for (t0, w) in split_blocks(0, NT):
                norm_block(l * 2, l, 0, t0, w, hT, t0, hk, nt)
            wq2 = [A.alloc("wq%d" % i, (8, 128), BF16) for i in range(2)]
            wkb, wkk = A.alloc("wk", (8, 128), BF16)
            wvb, wvk = A.alloc("wv", (8, 64), BF16)
            kGh = [A.alloc("kG%d" % i, (NT + 256,), BF16) for i in range(2)]
            kGk = "kGall"
            vG, vGk = A.alloc("vG", (NTILE + 2, 128), BF16)
            cosb = [A.alloc("cos%d" % i, (512,), F32) for i in range(2)]
            sinb = [A.alloc("sin%d" % i, (512,), F32) for i in range(2)]
            sqb, sqk = nt["sq"]
            rsb, rsk = A.alloc("grs", (2, 512), F32)
            qnb, qnk = nt["tt"]
            qbb, qbk = A.alloc("gqnb", (2, 512), BF16)
            t1_, t1k_ = A.alloc("gt1", (512,), F32)
            t2_, t2k_ = A.alloc("gt2", (512,), F32)
            kst, kstk = t1_, t1k_
            vst, vstk = A.alloc("vst", (4, 64), F32)
            ptS = [A.alloc("ptS%d" % i, (2, 512), BF16) for i in range(3)]
            ptP = [A.alloc("ptP%d" % i, (2, 256), BF16) for i in range(2)]
            rrs = [A.alloc("rr%d" % i, (512,), F32) for i in range(1)] * 2
            P.op("pool", lambda e: e.memset(vG[:, :, 64:128], 1.0), W=[(vGk, "ones")])
            P.op("pool", lambda e: e.memset(kGh[0][0][64:128, :], 0.0), W=[("kGz", 0)])
            P.op("pool", lambda e: e.memset(kGh[1][0][0:64, :], 0.0), W=[("kGz", 1)])
            cnt = [0]
            blocks = split_blocks(0, NT)

            def group(g):
                for c2 in range(2):
                    P.dma("pool", wq2[c2][0], I["wQ"][j, 2 * g + c2], W=[(wq2[c2][1],)])
                P.dma("pool", wkb, I["wK"][j, g], W=[(wkk,)])
                P.dma("pool", wvb, I["wV"][j, g], W=[(wvk,)])
                P.dma("pool", kGh[0][0][0:64, NT:NT + 256], I["cgk"][j, g, 0:64, :], W=[(kGk, "ctx", 0)])
                P.dma("pool", kGh[1][0][64:128, NT:NT + 256], I["cgk"][j, g, 64:128, :], W=[(kGk, "ctx", 1)])
                P.dma("pool", vG[:, NTILE:NTILE + 2, 0:64], I["cgv"][j, g], W=[(vGk, "ctx")])
                units = [(bi, which) for bi in range(len(blocks)) for which in range(3)]
                nu = len(units)

                def uinfo(u):
                    bi, which = units[u]
                    t0, wd = blocks[bi]
                    if which < 2:
                        return t0, wd, wq2[which], qng, oT[:, which, t0:t0 + wd], [(oTk, which, t0 // 512)], which
                    return t0, wd, (wkb, wkk), kng, None, [(kGk, "tok", t0 // 512)], which

                def stA(u):
                    t0, wd, (ws, wsk), gvec, dst, dkeys, which = uinfo(u)
                    pb, s = u % 3, u % 2
                    if which == 0 and t0 >= 512:
                        bsel = (t0 // 512) % 2
                        P.dma("sp", cosb[bsel][0], I["cosT"][:, t0 - 512:t0], W=[(cosb[bsel][1],)])
                        P.dma("sp", sinb[bsel][0], I["sinT"][:, t0 - 512:t0], W=[(sinb[bsel][1],)])
                    for k in range(8):
                        P.op("pe", lambda e, k=k: e.matmul(psum[:, pb, 0:wd], lhsT=ws[:, k, :], rhs=hT[:, k, t0:t0 + wd],
                             start=(k == 0), stop=(k == 7)), R=[(wsk,), (hk, k)], W=[PS(pb)])
                    P.op("act", lambda e: e.activation(out=sqb[:, s, 0:wd], in_=psum[:, pb, 0:wd], func=AF.Square), R=[PS(pb)], W=[(sqk, s)])

                def stB(u):
                    t0, wd, (ws, wsk), gvec, dst, dkeys, which = uinfo(u)
                    pb, s = u % 3, u % 2
                    P.op("pe", lambda e: e.matmul(psum[:, 3, 0:wd], lhsT=blockones, rhs=sqb[:, s, 0:wd], start=True, stop=True), R=[(sqk, s), ("cbf",)], W=[PS(3)])
                    P.op("act", lambda e: e.activation(out=rsb[:, s, 0:wd], in_=psum[:, 3, 0:wd], func=AF.Ln, scale=1.0 / 64, bias=eps_t[:]), R=[PS(3), ("eps",)], W=[(rsk, s)])
                    P.op("act", lambda e: e.activation(out=rsb[:, s, 0:wd], in_=rsb[:, s, 0:wd], func=AF.Exp, scale=-0.5), R=[(rsk, s)], W=[(rsk, s)])
                    if t0 < 512:
                        if which < 2:
                            P.op("dve", lambda e: e.scalar_tensor_tensor(out=dst, in0=psum[:, pb, 0:wd], scalar=gvec[:, j:j + 1], in1=rsb[:, s, 0:wd], op0=ALU.mult, op1=ALU.mult),
                                 R=[PS(pb), (rsk, s), ("qkng",)], W=dkeys)
                        else:
                            for hf in range(2):
                                r0 = hf * 64
                                P.op("dve", lambda e, hf=hf, r0=r0: e.scalar_tensor_tensor(out=kGh[hf][0][r0:r0 + 64, t0:t0 + wd], in0=psum[r0:r0 + 64, pb, 0:wd], scalar=gvec[r0:r0 + 64, j:j + 1],
                                     in1=rsb[r0:r0 + 64, s, 0:wd], op0=ALU.mult, op1=ALU.mult), R=[PS(pb), (rsk, s), ("qkng",)], W=dkeys)
                        if which == 2:
                            P.op("dve", lambda e: e.scalar_tensor_tensor(out=kst[:, 0:wd], in0=psum[:, pb, 0:wd], scalar=gvec[:, j:j + 1], in1=rsb[:, s, 0:wd], op0=ALU.mult, op1=ALU.mult),
                                 R=[PS(pb), (rsk, s), ("qkng",)], W=[(kstk,)])
                            P.dma("sp", O["o_gk"][j, g], kst[0:64, :], R=[(kstk,)])
                        return
                    P.op("dve", lambda e: e.scalar_tensor_tensor(out=qnb[:, s, 0:wd], in0=psum[:, pb, 0:wd], scalar=gvec[:, j:j + 1], in1=rsb[:, s, 0:wd], op0=ALU.mult, op1=ALU.mult),
                         R=[PS(pb), (rsk, s), ("qkng",)], W=[(qnk, s)])
                    P.op("act", lambda e: e.activation(out=qbb[:, s, 0:wd], in_=qnb[:, s, 0:wd], func=AF.Identity), R=[(qnk, s)], W=[(qbk, s)])

                def stC(u):
                    t0, wd, (ws, wsk), gvec, dst, dkeys, which = uinfo(u)
                    if t0 < 512:
                        return
                    s = u % 2
                    bsel = (t0 // 512) % 2
                    cs, csk = cosb[bsel]
                    sn, snk = sinb[bsel]
                    P.op("pe", lambda e: e.matmul(psum[:, 4, 0:wd], lhsT=rperm, rhs=qbb[:, s, 0:wd], start=True, stop=True), R=[(qbk, s), ("cbf",)], W=[PS(4)])
                    P.op("dve", lambda e: e.tensor_tensor(out=t1_[:, 0:wd], in0=qnb[:, s, 0:wd], in1=cs[:, 0:wd], op=ALU.mult), R=[(qnk, s), (csk,)], W=[(t1k_,)])
                    P.op("dve", lambda e: e.tensor_tensor(out=t2_[:, 0:wd], in0=psum[:, 4, 0:wd], in1=sn[:, 0:wd], op=ALU.mult), R=[PS(4), (snk,)], W=[(t2k_,)])
                    if which < 2:
                        P.op("pool", lambda e: e.tensor_tensor(out=dst, in0=t1_[:, 0:wd], in1=t2_[:, 0:wd], op=ALU.add), R=[(t1k_,), (t2k_,)], W=dkeys)
                    else:
                        for hf in range(2):
                            r0 = hf * 64
                            P.op("pool", lambda e, hf=hf, r0=r0: e.tensor_tensor(out=kGh[hf][0][r0:r0 + 64, t0:t0 + wd], in0=t1_[r0:r0 + 64, 0:wd], in1=t2_[r0:r0 + 64, 0:wd], op=ALU.add),
                                 R=[(t1k_,), (t2k_,)], W=dkeys)

                for i in range(nu + 2):
                    if i < nu:
                        stA(i)
                    if 0 <= i - 1 < nu:
                        stB(i - 1)
                    if 0 <= i - 2 < nu:
                        stC(i - 2)
                for t in range(NTILE):
                    pb = 6 + (t % 2)
                    for k in range(8):
                        P.op("pe", lambda e, k=k, t=t, pb=pb: e.matmul(psum[:, pb, 0:64], lhsT=hT[:, k, t * 128:(t + 1) * 128], rhs=wvb[:, k, :],
                             start=(k == 0), stop=(k == 7)), R=[(wvk,), (hk, k)], W=[PS(pb)])
                    P.op("act", lambda e, t=t, pb=pb: e.activation(out=vG[:, t, 0:64], in_=psum[:, pb, 0:64], func=AF.Identity), R=[PS(pb)], W=[(vGk, "v", t)])
                    if t < 4:
                        P.op("dve", lambda e, t=t, pb=pb: e.tensor_copy(out=vst[:, t, :], in_=psum[:, pb, 0:64]), R=[PS(pb)], W=[(vstk, t)])
                P.dma("sp", O["o_gv"][j, g].rearrange("t p c -> p t c"), vst, R=[(vstk,)])
                for hq in range(4):
                    ch = hq // 2
                    base = (hq % 2) * 64
                    qv = oT[:, ch, :]
                    prompt_attention(qv, oTk, kGh[hq % 2][0], kGk, lambda tt: vG[:, tt, :], vGk, base,
                                     lambda c0, w, ch=ch, base=base: (oT[base:base + 64, ch, c0:c0 + w], [(oTk, ch, 0, base, c0)]), ptP, rrs, cnt,
                                     qkeyf=lambda c0, ch=ch, base=base: (oTk, ch, 0), k128=True)
                chunks = [(512 + m * 128, 4 + m) for m in range(16)] + [(NT, NTILE), (NT + 128, NTILE + 1)]
                items = [(hq, qb, pi) for hq in range(4) for qb in range(4) for pi in range(9)]
                LA = 2

                def emit_qk(kk):
                    hq, qb, pi = items[kk]
                    ch, base, q0 = hq // 2, (hq % 2) * 64, 512 + qb * 512
                    slot = kk % 3
                    pt, ptk = ptS[kk % 3]
                    for e_ in range(2):
                        kc, vt = chunks[2 * pi + e_]
                        P.op("pe", lambda e, e_=e_, kc=kc: e.matmul(psum[:, 2 * slot + e_, :], lhsT=kGh[hq % 2][0][:, kc:kc + 128],
                             rhs=oT[:, ch, q0:q0 + 512], start=True, stop=True), R=[(kGk,), ("kGz",), (oTk, ch, qb + 1)], W=[PS(2 * slot + e_)])
                    P.op("act", lambda e: e.activation(out=pt, in_=psum[:, 2 * slot:2 * slot + 2, :], func=AF.Exp, scale=0.125),
                         R=[PS(2 * slot), PS(2 * slot + 1)], W=[(ptk,)])

                def emit_pv(kk):
                    hq, qb, pi = items[kk]
                    ch, base, q0 = hq // 2, (hq % 2) * 64, 512 + qb * 512
                    u = hq * 4 + qb
                    ob_ = 6 + (u % 2)
                    pt, ptk = ptS[kk % 3]
                    for e_ in range(2):
                        kc, vt = chunks[2 * pi + e_]
                        P.op("pe", lambda e, e_=e_, vt=vt: e.matmul(psum[:, ob_, :], lhsT=vG[:, vt, :], rhs=pt[:, e_, :], start=(pi == 0 and e_ == 0), stop=(pi == 8 and e_ == 1)),
                             R=[(vGk,), (ptk,)], W=[PS(ob_)])
                    if pi == 8:
                        rr, rrk = rrs[u % 2]
                        softmax_finish(ob_, 512, oT[base:base + 64, ch, q0:q0 + 512], [(oTk, ch, qb + 1, base)], rr, rrk)

                for kk in range(len(items) + LA):
                    if kk < len(items):
                        emit_qk(kk)
                    if kk >= LA:
                        emit_pv(kk - LA)
                gmark = A.off
                out_proj_residual(l, lambda d, g=g: I["wOc"][j, d, :, 2 * g:2 * g + 2, :], oT, oTk, 2, 16)
                A.off = gmark

            for g_ in range(4):
                group(g_)
            P.barrier()

        def ffn_layer(l):
            A.reset()
            NG = 4
            GW = NT // NG
            hG2 = [A.alloc("hG%d" % i, (8, GW + 2), BF16) for i in range(2)]
            act, actk = A.alloc("act", (NJ, GW), BF16)
            wup = [A.alloc("wup%d" % i, (2, 8, 128), BF16) for i in range(3)]
            wdn = [A.alloc("wdn%d" % i, (NJ, 128), BF16) for i in range(2)]
            usb = [[A.alloc("u%d%d" % (i, vg), (GW + 2,), F32) for vg in range(2)] for i in range(2)]
            Ab = [[A.alloc("A%d%d" % (i, vg), (GW,), F32) for vg in range(2)] for i in range(2)]
            Bg, Bgk = A.alloc("Bg", (GW,), F32)
            nt = norm_tmp(7)
            adaF = [A.alloc("adaF%d" % i, (8, 256), BF16) for i in range(2)] if l + 1 < DEPTH else None
            xsave, xsk = A.alloc("xsave", (8, 4), F32)
            for g in range(1, NG):
                P.op("dve", lambda e, g=g: e.tensor_copy(out=xsave[:, :, g:g + 1], in_=xT[:, :, g * GW - 1:g * GW]), R=[("xT",)], W=[(xsk, g)])
            pbi = [0]
            up_items = [(g, jj) for g in range(NG) for jj in range(NJ)]
            dn_items = [(g, d) for g in range(NG) for d in range(8)]
            up_next = [0]
            dn_next = [0]

            def up_prefetch(upto):
                while up_next[0] <= min(upto, len(up_items) - 1):
                    i = up_next[0]
                    wu, wuk = wup[i % 3]
                    P.dma("pool", wu, I["wUp"][l, up_items[i][1]].rearrange("v p k c -> p v k c"), W=[(wuk,)])
                    up_next[0] += 1

            def dn_prefetch(upto):
                while dn_next[0] <= min(upto, len(dn_items) - 1):
                    i = dn_next[0]
                    wd_, wdk = wdn[i % 2]
                    P.dma("pool", wd_, I["wDn"][l, dn_items[i][1]], W=[(wdk,)])
                    dn_next[0] += 1

            def ffn_norm(g):
                hG, hGk = hG2[g % 2]
                t0g, t1g = g * GW, (g + 1) * GW
                lo = t0g - (0 if t0g in SEQ_STARTS else 1)
                hi = t1g + (0 if t1g in SEQ_ENDS else 1)
                if lo < t0g:
                    norm_block(l * 2 + 1, l, 1, lo, 1, hG, 0, hGk, nt, xsrc=lambda k: xsave[:, k, g:g + 1], xkey=(xsk, g))
                for (t0, w) in split_blocks(t0g, hi):
                    norm_block(l * 2 + 1, l, 1, t0, w, hG, t0 - lo, hGk, nt)

            def ffn_group(g):
                hG, hGk = hG2[g % 2]
                t0g, t1g = g * GW, (g + 1) * GW
                lo = t0g - (0 if t0g in SEQ_STARTS else 1)
                hi = t1g + (0 if t1g in SEQ_ENDS else 1)
                segs = []
                for (s0, s1) in SEQS:
                    a, b = max(s0, t0g), min(s1, t1g)
                    if a < b:
                        segs.append((a, b, s0, s1))
                blocks = split_blocks(lo, hi, 321)

                def ffn_tail(jq):
                    av, avk = Ab[jq % 2][0]
                    ag, agk = Ab[jq % 2][1]
                    P.op("act", lambda e: e.activation(out=Bg, in_=ag, func=AF.Silu), R=[(agk,)], W=[(Bgk,)])
                    P.op("pool", lambda e: e.tensor_tensor(out=act[:, jq, :], in0=av, in1=Bg, op=ALU.mult), R=[(avk,), (Bgk,)], W=[(actk, jq)])

                for jj in range(NJ):
                    ui = g * NJ + jj
                    if adaF is not None:
                        if ui % 3 == 0 and ui // 3 < 24:
                            mod_dma(l + 1, ui // 3, adaF[(ui // 3) % 2])
                        if ui % 3 == 2 and ui // 3 < 24:
                            mod_compute(l + 1, ui // 3, adaF[(ui // 3) % 2], 6)
                    up_prefetch(ui + 2)
                    if jj == NJ - 3:
                        dn_prefetch(g * 8)
                    wu, wuk = wup[ui % 3]
                    for vg in range(2):
                        ub, ubk = usb[jj % 2][vg]
                        ab, abk = Ab[jj % 2][vg]
                        chn = vg * NJ + jj
                        for (t0, w) in blocks:
                            bank = pbi[0] % 6
                            pbi[0] += 1
                            for k in range(8):
                                P.op("pe", lambda e, wu=wu, vg=vg, k=k, t0=t0, w=w, bank=bank: e.matmul(psum[:, bank, 0:w], lhsT=wu[:, vg, k, :], rhs=hG[:, k, t0 - lo:t0 - lo + w],
                                     start=(k == 0), stop=(k == 7)), R=[(wuk,), (hGk, k)], W=[PS(bank)])
                            P.op("act", lambda e, ub=ub, t0=t0, w=w, bank=bank: e.activation(out=ub[:, t0 - lo:t0 - lo + w], in_=psum[:, bank, 0:w], func=AF.Identity),
                                 R=[PS(bank)], W=[(ubk, t0)])
                            a0, a1 = max(t0, t0g), min(t0 + w, t1g)
                            if a0 < a1 and vg == 1:
                                P.op("act", lambda e, ab=ab, a0=a0, a1=a1, t0=t0, bank=bank, chn=chn: e.activation(out=ab[:, a0 - t0g:a1 - t0g], in_=psum[:, bank, a0 - t0:a1 - t0],
                                     func=AF.Identity, scale=convw[:, l, chn, 1:2], bias=convw[:, l, chn, 3:4]), R=[PS(bank), ("convw",)], W=[(abk, a0)])
                        if vg == 0:
                            P.op("pool", lambda e, ab=ab, ub=ub, chn=chn: e.tensor_scalar(out=ab[:, 0:GW], in0=ub[:, t0g - lo:t0g - lo + GW],
                                 scalar1=convw[:, l, chn, 1:2], scalar2=convw[:, l, chn, 3:4], op0=ALU.mult, op1=ALU.add), R=[(ubk,), ("convw",)], W=[(abk,)])
                        segs_ = segs
                        if g == 0 and segs[0][:2] == (0, 256) and segs[1][:2] == (256, 512):
                            A3 = ab[:, 0:512].rearrange("p (s c) -> p s c", s=2)
                            U3 = ub[:, 0:512].rearrange("p (s c) -> p s c", s=2)
                            P.op("dve", lambda e, A3=A3, U3=U3, chn=chn: e.scalar_tensor_tensor(out=A3[:, :, 1:256], in0=U3[:, :, 0:255],
                                 scalar=convw[:, l, chn, 0:1], in1=A3[:, :, 1:256], op0=ALU.mult, op1=ALU.add), R=[(ubk,), (abk,), ("convw",)], W=[(abk,)])
                            P.op("dve", lambda e, A3=A3, U3=U3, chn=chn: e.scalar_tensor_tensor(out=A3[:, :, 0:255], in0=U3[:, :, 1:256],
                                 scalar=convw[:, l, chn, 2:3], in1=A3[:, :, 0:255], op0=ALU.mult, op1=ALU.add), R=[(ubk,), (abk,), ("convw",)], W=[(abk,)])
                            segs_ = segs[2:]
                        for (a, b, s0, s1) in segs_:
                            al = a + 1 if a == s0 else a
                            if al < b:
                                P.op("dve", lambda e, ab=ab, ub=ub, al=al, b=b, chn=chn: e.scalar_tensor_tensor(out=ab[:, al - t0g:b - t0g], in0=ub[:, al - 1 - lo:b - 1 - lo],
                                     scalar=convw[:, l, chn, 0:1], in1=ab[:, al - t0g:b - t0g], op0=ALU.mult, op1=ALU.add), R=[(ubk,), (abk,), ("convw",)], W=[(abk,)])
                            br = b - 1 if b == s1 else b
                            if a < br:
                                P.op("dve", lambda e, ab=ab, ub=ub, a=a, br=br, chn=chn: e.scalar_tensor_tensor(out=ab[:, a - t0g:br - t0g], in0=ub[:, a + 1 - lo:br + 1 - lo],
                                     scalar=convw[:, l, chn, 2:3], in1=ab[:, a - t0g:br - t0g], op0=ALU.mult, op1=ALU.add), R=[(ubk,), (abk,), ("convw",)], W=[(abk,)])
                    if jj >= 1:
                        ffn_tail(jj - 1)
                ffn_tail(NJ - 1)
                if g + 1 < NG:
                    ffn_norm(g + 1)
                for d in range(8):
                    di = g * 8 + d
                    dn_prefetch(di + 1)
                    wd_, wdk = wdn[di % 2]
                    for (t0, w) in split_blocks(t0g, t1g, 320):
                        bank = pbi[0] % 6
                        pbi[0] += 1
                        for jj in range(NJ):
                            P.op("pe", lambda e, wd_=wd_, jj=jj, t0=t0, w=w, bank=bank: e.matmul(psum[:, bank, 0:w], lhsT=wd_[:, jj, :], rhs=act[:, jj, t0 - t0g:t0 - t0g + w],
                                 start=(jj == 0), stop=(jj == NJ - 1)), R=[(wdk,), (actk, jj)], W=[PS(bank)])
                        v = vof(t0)
                        xk = [("xT", d, tt) for tt in range(t0 // 128, (t0 + w + 127) // 128)]
                        P.op("dve", lambda e, d=d, t0=t0, w=w, bank=bank, v=v: e.scalar_tensor_tensor(out=xT[:, d, t0:t0 + w], in0=psum[:, bank, 0:w],
                             scalar=modT[:, l, 40 + d, v:v + 1], in1=xT[:, d, t0:t0 + w], op0=ALU.mult, op1=ALU.add), R=[PS(bank), ("modT", l)] + xk, W=xk)

            ffn_norm(0)
            for g_ in range(NG):
                ffn_group(g_)
            if adaF is not None:
                mod_finish(l + 1)
            P.barrier()

        ones_f = sb("ones_f", (128, 128), F32)
        P.op("pool", lambda e: e.memset(ones_f[:], 1.0), W=[("ones_f",)])

        phases = []
        for l in range(DEPTH):
            phases.append(("mix", l))
            phases.append(("ffn", l))
        nph = len(phases) if stop is None else stop
        for (kind, l) in phases[:nph]:
            if kind == "ffn":
                ffn_layer(l)
            elif l % 2 == 0:
                ab_layer(l)
            else:
                gqa_layer(l)
        A.reset()
        yst = [A.alloc("yst%d" % i, (8, 512), F32) for i in range(2)]
        nt = norm_tmp(7)
        for bi, (t0, w) in enumerate(split_blocks(0, NT)):
            ys, ysk = yst[bi % 2]
            norm_block(8, 0, 0, t0, w, ys, 0, ysk, nt, final=True)
            P.dma("sp", O["yT"].rearrange("(k p) t -> p k t", p=128)[:, :, t0:t0 + w], ys[:, :, 0:w], R=[(ysk,)])
        P.emit()
    return nc


def _consts():
    c = np.zeros((128, 5, 128), np.float32)
    c[:, 0, :] = np.eye(128, dtype=np.float32)
    c[:, 1, :] = 1.0
    c[0:64, 2, 0:64] = 1.0
    c[64:128, 2, 64:128] = 1.0
    for m in range(128):
        r = m % 32
        k = m + 16 if r < 16 else m - 16
        c[k, 3, m] = 1.0
    jj = np.arange(128)[:, None]
    ii = np.arange(128)[None, :]
    mk = np.zeros((128, 2, 4, 128), np.float32)
    mk[:, 0] = (jj <= ii).astype(np.float32)[:, None, :]
    mk[:, 1] = (jj >= ii).astype(np.float32)[:, None, :]
    p = np.arange(128)
    dd = p % 64
    axis = (dd >= 32).astype(np.int64)
    r = dd % 32
    fi = r % 16
    inv = (np.float32(10000.0) ** (-(np.arange(0, 32, 2, dtype=np.float32)) / np.float32(32))).astype(np.float32)
    t = np.arange(2048)
    pos = np.where(axis[:, None] == 0, (t // 64)[None, :], (t % 64)[None, :]).astype(np.float32)
    ang = (pos * inv[fi][:, None]).astype(np.float32)
    cosT = np.cos(ang).astype(np.float32)
    sinT = np.sin(ang).astype(np.float32)
    sinT = np.where((r < 16)[:, None], -sinT, sinT).astype(np.float32)
    return c, mk, cosT, sinT


def _t2_index():
    p = np.arange(128)
    kb, kc = p // 64, p % 64
    qq = np.arange(128)
    qrow, qc = qq // 64, qq % 64
    e = np.arange(7)
    d = 2 * e + 1
    dr = d[None, :, None] + kb[:, None, None] - qrow[None, None, :]
    dc = kc[:, None] - qc[None, :] + 15
    cs = np.clip(qc - 8, 0, 48)
    ok = (kc[:, None] >= cs[None, :]) & (kc[:, None] < cs[None, :] + 16)
    dcc = np.clip(dc, 0, 30)
    dr_b = np.broadcast_to(dr, (128, 7, 128))
    dc_b = np.broadcast_to(dcc[:, None, :], (128, 7, 128))
    ok_b = np.broadcast_to(ok[:, None, :], (128, 7, 128))
    return dr_b, dc_b, ok_b


def _chunkw(w, ncols_per_chunk=128):
    K, C = w.shape
    return np.ascontiguousarray(w.reshape(K // 128, 128, C // ncols_per_chunk, ncols_per_chunk).transpose(2, 1, 0, 3))


def _prep_shared(inp):
    f = lambda a: np.asarray(a, dtype=np.float32)
    S = {}
    ada_w = f(inp["ada_w"])
    S["adaW"] = np.ascontiguousarray(ada_w.reshape(4, 8, 128, 24, 256).transpose(0, 3, 2, 1, 4))
    ada_b = f(inp["ada_b"])
    ab = ada_b.reshape(4, 48, 128).transpose(2, 0, 1)
    S["adaB"] = np.ascontiguousarray(np.repeat(ab[:, :, :, None], 2, axis=3))
    S["gmix"] = np.ascontiguousarray(f(inp["norm_mix_g"]).reshape(4, 8, 128).transpose(2, 0, 1))
    S["gffn"] = np.ascontiguousarray(f(inp["norm_ffn_g"]).reshape(4, 8, 128).transpose(2, 0, 1))
    S["gfin"] = np.ascontiguousarray(f(inp["final_norm_g"]).reshape(8, 128).T)
    w_in = f(inp["ab_w_in"])
    wA = np.zeros((2, 3, 4, 128, 8, 128), np.float32)
    wG = np.zeros((2, 2, 128, 8, 800), np.float32)
    for l in range(2):
        for wi in range(3):
            wA[l, wi] = _chunkw(w_in[l][:, wi * 512:(wi + 1) * 512])
        W = w_in[l].reshape(8, 128, 3104).transpose(1, 0, 2)
        for pc in range(2):
            wG[l, pc, :, :, 0:128] = W[:, :, 1536 + pc * 128:1536 + (pc + 1) * 128]
            wG[l, pc, :, :, 128:256] = W[:, :, 1792 + pc * 128:1792 + (pc + 1) * 128]
            wG[l, pc, :, :, 256:512] = W[:, :, 2048 + pc * 256:2048 + (pc + 1) * 256]
            wG[l, pc, :, :, 512:768] = W[:, :, 2560 + pc * 256:2560 + (pc + 1) * 256]
            wG[l, pc, :, :, 768:800] = W[:, :, 3072:3104]
    S["wA"], S["wG"] = wA, wG
    w2x = np.zeros((2, 128, 2, 256), np.float32)
    w2x[:, 96:112, 0, :] = f(inp["gla_w2_fwd"])
    w2x[:, 112:128, 1, :] = f(inp["gla_w2_bwd"])
    S["w2x"] = w2x
    b2 = np.stack([f(inp["gla_b2_fwd"]), f(inp["gla_b2_bwd"])], axis=1)
    S["b2x"] = np.ascontiguousarray(b2.reshape(2, 2, 2, 128).transpose(3, 0, 1, 2))
    S["glag"] = np.ascontiguousarray(f(inp["gla_norm_g"]).reshape(2, 4, 128).transpose(2, 0, 1))
    S["wOab"] = np.stack([_chunkw(f(inp["ab_w_out"])[l]) for l in range(2)])
    rpb = f(inp["na_rpb"])
    dr_b, dc_b, ok_b = _t2_index()
    T2 = np.zeros((2, 4, 128, 2, 7, 128), np.float32)
    T2i = np.zeros((2, 4, 128, 2, 7, 128), np.float32)
    for l in range(2):
        for h in range(8):
            g = rpb[l, h][dr_b, dc_b]
            T2[l, h // 2, :, h % 2] = np.where(ok_b, g, np.float32(-30000.0))
            T2i[l, h // 2, :, h % 2] = np.where(ok_b & (dr_b >= 3) & (dr_b <= 10), g, np.float32(-30000.0))
    S["T2"] = T2
    S["T2i"] = T2i
    gq = f(inp["gqa_w_in"])
    S["wQ"] = np.stack([_chunkw(gq[l][:, 0:1024]) for l in range(2)])
    wK = np.zeros((2, 4, 128, 8, 128), np.float32)
    wV = np.zeros((2, 4, 128, 8, 64), np.float32)
    for l in range(2):
        W = gq[l].reshape(8, 128, 1536).transpose(1, 0, 2)
        for g in range(4):
            kcols = W[:, :, 1024 + g * 64:1024 + (g + 1) * 64]
            wK[l, g, :, :, 0:64] = kcols
            wK[l, g, :, :, 64:128] = kcols
            wV[l, g] = W[:, :, 1280 + g * 64:1280 + (g + 1) * 64]
    S["wK"], S["wV"] = wK, wV
    S["wOc"] = np.stack([_chunkw(f(inp["gqa_w_out"])[l]) for l in range(2)])
    qn = f(inp["gqa_q_norm_g"])
    kn = f(inp["gqa_k_norm_g"])
    S["qng"] = np.ascontiguousarray(np.concatenate([qn, qn], axis=1).T)
    S["kng"] = np.ascontiguousarray(np.concatenate([kn, kn], axis=1).T)
    up = f(inp["ffn_w_up"])
    wUp = np.zeros((4, NJ, 2, 128, 8, 128), np.float32)
    for l in range(4):
        c = _chunkw(up[l])
        wUp[l, :, 0] = c[0:NJ]
        wUp[l, :, 1] = c[NJ:2 * NJ]
    S["wUp"] = wUp
    dn = f(inp["ffn_w_down"])
    S["wDn"] = np.ascontiguousarray(dn.reshape(4, NJ, 128, 8, 128).transpose(0, 3, 2, 1, 4))
    cw = f(inp["ffn_conv_w"])
    cbias = f(inp["ffn_conv_b"])
    cvw = np.zeros((128, 4, 44, 4), np.float32)
    for t in range(3):
        cvw[:, :, :, t] = cw[:, t, :].reshape(4, 44, 128).transpose(2, 0, 1)
    cvw[:, :, :, 3] = cbias.reshape(4, 44, 128).transpose(2, 0, 1)
    S["convw"] = cvw
    c, mk, cosT, sinT = _consts()
    S["consts"], S["masks"], S["cosT"], S["sinT"] = c, mk, cosT, sinT
    return S


def _prep_core(inp, i):
    f = lambda a: np.asarray(a, dtype=np.float32)
    M = {}
    xp, xs = f(inp["x_prompt"]), f(inp["x_sample"])
    x = np.concatenate([xp[2 * i], xp[2 * i + 1], xs[i]], axis=0)
    M["xT"] = np.ascontiguousarray(x.T)
    cv = np.stack([f(inp["c_ctx"]), f(inp["c"])[i]], axis=1)
    M["cv"] = np.ascontiguousarray(cv.reshape(8, 128, 2).transpose(1, 0, 2))
    nk, nv = f(inp["cache_na_k"])[i], f(inp["cache_na_v"])[i]
    M["cnk"] = np.ascontiguousarray(nk.reshape(2, 4, 2, 256, 64).transpose(0, 1, 2, 4, 3).reshape(2, 4, 128, 256))
    M["cnv"] = np.ascontiguousarray(nv.reshape(2, 4, 2, 2, 128, 64).transpose(0, 1, 4, 3, 2, 5))
    sf, sb_ = f(inp["state_gla_fwd"])[i], f(inp["state_gla_bwd"])[i]
    lay = lambda s: np.ascontiguousarray(s.reshape(2, 2, 2, 64, 128).transpose(0, 2, 3, 1, 4).reshape(2, 128, 2, 128))
    M["sgf"], M["sgb"] = lay(sf), lay(sb_)
    gk, gv = f(inp["cache_gqa_k"])[i], f(inp["cache_gqa_v"])[i]
    gkT = gk.transpose(0, 1, 3, 2)
    M["cgk"] = np.ascontiguousarray(np.concatenate([gkT, gkT], axis=2))
    M["cgv"] = np.ascontiguousarray(gv.reshape(2, 4, 2, 128, 64).transpose(0, 1, 3, 2, 4))
    return M


_NC_CACHE = {}


def kernel(**inputs):
    if "nc" not in _NC_CACHE:
        _NC_CACHE["nc"] = build()
    nc = _NC_CACHE["nc"]
    S = _prep_shared(inputs)
    in_maps = []
    for i in range(8):
        m = dict(S)
        m.update(_prep_core(inputs, i))
        in_maps.append(m)
    res = run_bass_kernel_spmd(nc, in_maps, core_ids=list(range(8)))
    R = res.results
    y_prompt = np.zeros((16, 256, 1024), np.float32)
    y_sample = np.zeros((8, 2048, 1024), np.float32)
    na_k = np.zeros((16, 2, 8, 256, 64), np.float32)
    na_v = np.zeros((16, 2, 8, 256, 64), np.float32)
    g_f = np.zeros((16, 2, 4, 64, 128), np.float32)
    g_b = np.zeros((16, 2, 4, 64, 128), np.float32)
    q_k = np.zeros((16, 2, 4, 256, 64), np.float32)
    q_v = np.zeros((16, 2, 4, 256, 64), np.float32)
    for i in range(8):
        r = R[i]
        y = np.asarray(r["yT"]).T
        y_prompt[2 * i] = y[0:256]
        y_prompt[2 * i + 1] = y[256:512]
        y_sample[i] = y[512:]
        nak = np.asarray(r["o_nak"]).reshape(2, 4, 2, 64, 2, 256)
        na_k[2 * i:2 * i + 2] = nak.transpose(4, 0, 1, 2, 5, 3).reshape(2, 2, 8, 256, 64)
        nav = np.asarray(r["o_nav"]).reshape(2, 4, 2, 2, 128, 2, 64)
        na_v[2 * i:2 * i + 2] = nav.transpose(2, 0, 1, 5, 3, 4, 6).reshape(2, 2, 8, 256, 64)
        for nm, dst in (("o_gf", g_f), ("o_gb", g_b)):
            a = np.asarray(r[nm]).reshape(2, 2, 2, 64, 2, 128)
            dst[2 * i:2 * i + 2] = a.transpose(1, 0, 4, 2, 3, 5).reshape(2, 2, 4, 64, 128)
        gk = np.asarray(r["o_gk"]).reshape(2, 4, 64, 2, 256)
        q_k[2 * i:2 * i + 2] = gk.transpose(3, 0, 1, 4, 2)
        gv = np.asarray(r["o_gv"]).reshape(2, 4, 2, 2, 128, 64)
        q_v[2 * i:2 * i + 2] = gv.transpose(2, 0, 1, 3, 4, 5).reshape(2, 2, 4, 256, 64)
    return (y_prompt, y_sample, na_k, na_v, g_f, g_b, q_k, q_v)
```

```python
import concourse.bass as bass
import concourse.mybir as mybir
from contextlib import ExitStack

F32 = mybir.dt.float32
BF16 = mybir.dt.bfloat16
AF = mybir.ActivationFunctionType
ALU = mybir.AluOpType

ENGS = ("pe", "act", "dve", "pool", "sp")
DMA_RING = 8


class _Op:
    __slots__ = ("eng", "fn", "deps", "dma", "signal", "seq", "ring", "ringval", "idx")

    def __init__(self, eng, fn, dma):
        self.eng = eng
        self.fn = fn
        self.dma = dma
        self.deps = []
        self.signal = False
        self.seq = 0
        self.ring = None
        self.ringval = 0


class Prog:
    def __init__(self, nc):
        self.nc = nc
        self.ops = []
        self.state = {}
        self.ndma = {e: 0 for e in ENGS}
        self.last_op = {e: None for e in ENGS}
        self.dma_ops = {e: [] for e in ENGS}
        self._cap = None

    def capture(self, thunk):
        assert self._cap is None
        self._cap = []
        thunk()
        lst, self._cap = self._cap, None
        return lst

    @staticmethod
    def merge(la, lb):
        out = []
        ia = ib = 0
        while ia < len(la) or ib < len(lb):
            fa = ia / max(1, len(la))
            fb = ib / max(1, len(lb))
            if ib >= len(lb) or (ia < len(la) and fa <= fb):
                out.append(la[ia])
                ia += 1
            else:
                out.append(lb[ib])
                ib += 1
        return out

    def replay(self, lst):
        for it in lst:
            self.op(*it)

    def replay_merged(self, la, lb):
        self.replay(self.merge(la, lb))

    @staticmethod
    def _conf(a, b):
        n = min(len(a), len(b))
        return a[:n] == b[:n]

    def _collect(self, op, keys, is_write):
        for k in keys:
            name, sub = k[0], tuple(k[1:])
            tab = self.state.setdefault(name, {})
            for s2, st in tab.items():
                if self._conf(sub, s2):
                    if st[0] is not None:
                        op.deps.append(st[0])
                    if is_write:
                        op.deps.extend(st[1])

    def _update(self, op, reads, writes):
        for k in reads:
            name, sub = k[0], tuple(k[1:])
            tab = self.state[name]
            st = tab.setdefault(sub, [None, []])
            st[1].append(op)
        for k in writes:
            name, sub = k[0], tuple(k[1:])
            tab = self.state[name]
            for s2 in [s for s in tab if len(s) >= len(sub) and s[:len(sub)] == sub]:
                del tab[s2]
            tab[sub] = [op, []]

    def op(self, eng, fn, R=(), W=(), dma=False, extra=()):
        if self._cap is not None:
            self._cap.append((eng, fn, list(R), list(W), dma, list(extra)))
            return None
        o = _Op(eng, fn, dma)
        o.idx = len(self.ops)
        W = list(W) + [k for k in R if k[0] == "ps"]
        R = [k for k in R if k[0] != "ps"]
        self._collect(o, R, False)
        self._collect(o, W, True)
        o.deps.extend(extra)
        self._update(o, R, W)
        if dma:
            j = self.ndma[eng]
            self.ndma[eng] += 1
            o.ring = j % DMA_RING
            o.ringval = 16 * (j // DMA_RING + 1)
            if j >= DMA_RING:
                o.deps.append(self.dma_ops[eng][j - DMA_RING])
            self.dma_ops[eng].append(o)
        self.ops.append(o)
        self.last_op[eng] = o
        return o

    def dma(self, eng, out, in_, R=(), W=()):
        return self.op(eng, lambda e: e.dma_start(out=out, in_=in_), R, W, dma=True)

    def barrier(self):
        pend = [o for o in self.last_op.values() if o is not None]
        for e in ENGS:
            pend.extend(self.dma_ops[e][-DMA_RING:])
        for e in ENGS:
            if e == "sp" and False:
                continue
            self.op(e, None, extra=pend)

    def emit(self):
        nc = self.nc
        for o in self.ops:
            for d in o.deps:
                if not d.dma:
                    if d.eng == "pe" and o.eng == "pe" and not o.dma:
                        continue
                    d.signal = True
        cnt = {e: 0 for e in ENGS}
        for o in self.ops:
            if o.dma:
                continue
            if o.fn is None:
                continue
            if o.signal:
                cnt[o.eng] += 1
                o.seq = cnt[o.eng]
        with ExitStack() as es:
            esem = {e: es.enter_context(nc.semaphore("s_" + e)) for e in ENGS}
            rsem = {e: [es.enter_context(nc.semaphore("r_%s%d" % (e, i))) for i in range(DMA_RING)]
                    for e in ENGS if self.ndma[e] > 0}
            block = es.enter_context(nc.Block())
            per = {e: [o for o in self.ops if o.eng == e] for e in ENGS}
            handles = {"pe": "tensor", "act": "scalar", "dve": "vector", "pool": "gpsimd", "sp": "sync"}

            def make(e):
                def body(eng):
                    waited = {}
                    for o in per[e]:
                        for d in o.deps:
                            if d.dma:
                                key = ("r", d.eng, d.ring)
                                sem, val = rsem[d.eng][d.ring], d.ringval
                            else:
                                if d.eng == "pe" and e == "pe" and not o.dma:
                                    continue
                                if d.fn is None:
                                    continue
                                key = ("e", d.eng)
                                sem, val = esem[d.eng], d.seq
                            if waited.get(key, 0) >= val:
                                continue
                            waited[key] = val
                            eng.wait_ge(sem, val)
                        if o.fn is None:
                            continue
                        ins = o.fn(eng)
                        if o.dma:
                            ins.then_inc(rsem[e][o.ring], 16)
                        elif o.signal:
                            ins.then_inc(esem[e], 1)
                    for d in self.dma_ops[e][-DMA_RING:]:
                        key = ("r", e, d.ring)
                        if waited.get(key, 0) >= d.ringval:
                            continue
                        waited[key] = d.ringval
                        eng.wait_ge(rsem[e][d.ring], d.ringval)
                return body

            for e in ENGS:
                if per[e]:
                    getattr(block, handles[e])(make(e))


import numpy as np
import ml_dtypes
from concourse.bass_utils import run_bass_kernel_spmd

D = 1024
NT = 2560
NTILE = 20
DEPTH = 4
DFF = 2816
NJ = 22
EPS = 1e-6
SEQS = [(0, 256), (256, 512), (512, 2560)]
SEQ_STARTS = {0, 256, 512}
SEQ_ENDS = {256, 512, 2560}
ARENA_BYTES = 112 * 1024


def _prod(s):
    r = 1
    for v in s:
        r *= v
    return r


class Arena:
    def __init__(self, t, nbytes):
        self.t = t
        self.cap = nbytes
        self.off = 0
        self.gen = 0

    def reset(self):
        self.off = 0
        self.gen += 1

    def alloc(self, name, shape, dt):
        esz = 2 if dt == BF16 else 4
        nb = _prod(shape) * esz
        nb_al = (nb + 31) // 32 * 32
        assert self.off + nb_al <= self.cap, (name, self.off, nb_al, self.cap)
        ap = self.t[:, self.off // 2:(self.off + nb) // 2]
        self.off += nb_al
        if dt == F32:
            ap = ap.bitcast(F32)
        if len(shape) == 2:
            ap = ap.rearrange("p (a b) -> p a b", a=shape[0])
        elif len(shape) == 3:
            ap = ap.rearrange("p (a b c) -> p a b c", a=shape[0], b=shape[1])
        elif len(shape) == 4:
            ap = ap.rearrange("p (a b c d) -> p a b c d", a=shape[0], b=shape[1], c=shape[2])
        return ap, "%s@%d" % (name, self.gen)


def vof(t):
    return 0 if t < 512 else 1


def split_blocks(t0, t1, maxw=512):
    out = []
    a = t0
    while a < t1:
        b = min(t1, a + maxw)
        if a < 512 < b:
            b = 512
        out.append((a, b - a))
        a = b
    return out


def build(stop=None, sub=None):
    nc = bass.Bass("TRN2", target_bir_lowering=False)
    din = lambda n, s: nc.dram_tensor(n, list(s), F32, kind="ExternalInput").ap()
    dout = lambda n, s: nc.dram_tensor(n, list(s), F32, kind="ExternalOutput").ap()
    I = {}
    for n, s in [("xT", (D, NT)), ("cv", (128, 8, 2)), ("adaW", (4, 24, 128, 8, 256)), ("adaB", (128, 4, 48, 2)),
                 ("gmix", (128, 4, 8)), ("gffn", (128, 4, 8)), ("gfin", (128, 8)),
                 ("wA", (2, 3, 4, 128, 8, 128)), ("wG", (2, 2, 128, 8, 800)), ("w2x", (2, 128, 2, 256)), ("b2x", (128, 2, 2, 2)),
                 ("glag", (128, 2, 4)), ("wOab", (2, 8, 128, 8, 128)), ("T2", (2, 4, 128, 2, 7, 128)), ("T2i", (2, 4, 128, 2, 7, 128)),
                 ("cnk", (2, 4, 128, 256)), ("cnv", (2, 4, 128, 2, 2, 64)), ("sgf", (2, 128, 2, 128)), ("sgb", (2, 128, 2, 128)),
                 ("wQ", (2, 8, 128, 8, 128)), ("wK", (2, 4, 128, 8, 128)), ("wV", (2, 4, 128, 8, 64)), ("wOc", (2, 8, 128, 8, 128)),
                 ("qng", (128, 2)), ("kng", (128, 2)), ("cgk", (2, 4, 128, 256)), ("cgv", (2, 4, 128, 2, 64)),
                 ("cosT", (128, 2048)), ("sinT", (128, 2048)),
                 ("wUp", (4, NJ, 2, 128, 8, 128)), ("wDn", (4, 8, 128, NJ, 128)), ("convw", (128, 4, 44, 4)),
                 ("consts", (128, 5, 128)), ("masks", (128, 2, 4, 128))]:
        I[n] = din(n, s)
    O = {}
    for n, s in [("yT", (D, NT)), ("o_nak", (2, 4, 128, 512)), ("o_nav", (2, 4, 4, 128, 128)),
                 ("o_gf", (2, 2, 128, 2, 128)), ("o_gb", (2, 2, 128, 2, 128)),
                 ("o_gk", (2, 4, 64, 512)), ("o_gv", (2, 4, 4, 128, 64))]:
        O[n] = dout(n, s)

    with ExitStack() as es:
        sb = lambda n, s, d: es.enter_context(nc.sbuf_tensor(n, list(s), d))
        xT = sb("xT_sb", (128, 8, NT), F32)
        arena_t = sb("arena", (128, ARENA_BYTES // 2), BF16)
        cbf = sb("cbf", (128, 5, 128), BF16)
        mbf = sb("mbf", (128, 2, 4, 128), BF16)
        modT = sb("modT", (128, 4, 48, 2), F32)
        adab = sb("adab", (128, 4, 48, 2), F32)
        gmix = sb("gmix_sb", (128, 4, 8), F32)
        gffn = sb("gffn_sb", (128, 4, 8), F32)
        gfin = sb("gfin_sb", (128, 8), F32)
        gsc = sb("gsc", (128, 9, 8, 2), F32)
        cvs = sb("cvs", (128, 8, 2), F32)
        cvb = sb("cvb", (128, 8, 2), BF16)
        convw = sb("convw_sb", (128, 4, 44, 4), F32)
        glag = sb("glag_sb", (128, 2, 4), F32)
        qng = sb("qng_sb", (128, 2), F32)
        kng = sb("kng_sb", (128, 2), F32)
        w2x = sb("w2x_sb", (128, 2, 2, 256), BF16)
        negb = sb("negb", (128, 2, 2, 2), F32)
        zero1 = sb("zero1", (128, 1), F32)
        psum = es.enter_context(nc.psum_tensor("psum", [128, 8, 512], F32))
        P = Prog(nc)
        A = Arena(arena_t, ARENA_BYTES)
        ident = cbf[:, 0, :]
        ones = cbf[:, 1, :]
        blockones = cbf[:, 2, :]
        rperm = cbf[:, 3, :]

        def PS(b):
            return ("ps", b)

        P.dma("sp", xT[:], I["xT"].rearrange("(k p) t -> p k t", p=128), W=[("xT",)])
        P.dma("pool", cbf[:], I["consts"], W=[("cbf",)])
        P.dma("pool", mbf[:], I["masks"], W=[("mbf",)])
        P.dma("sp", adab[:], I["adaB"], W=[("adab",)])
        P.dma("sp", gmix[:], I["gmix"], W=[("gmix",)])
        P.dma("sp", gffn[:], I["gffn"], W=[("gffn",)])
        P.dma("sp", gfin[:], I["gfin"], W=[("gfin",)])
        P.dma("sp", cvs[:], I["cv"], W=[("cvs",)])
        P.dma("sp", convw[:], I["convw"], W=[("convw",)])
        P.dma("sp", glag[:], I["glag"], W=[("glag",)])
        P.dma("sp", qng[:], I["qng"], W=[("qng",)])
        P.dma("sp", kng[:], I["kng"], W=[("kng",)])
        P.dma("pool", w2x[:], I["w2x"].rearrange("l r d c -> r l d c"), W=[("w2x",)])
        P.dma("sp", negb[:], I["b2x"], W=[("negb",)])
        P.op("dve", lambda e: e.tensor_scalar(out=negb[:], in0=negb[:], scalar1=-1.0, scalar2=None, op0=ALU.mult), R=[("negb",)], W=[("negb",)])
        P.op("pool", lambda e: e.memset(zero1[:], 0.0), W=[("zero1",)])
        P.op("act", lambda e: e.activation(out=cvb[:], in_=cvs[:], func=AF.Silu), R=[("cvs",)], W=[("cvb",)])

        def mod_dma(l, pc, buf):
            P.dma("pool", buf[0], I["adaW"][l, pc], W=[(buf[1],)])

        def mod_compute(l, pc, buf, bank):
            ab, abk = buf
            for fi in range(2):
                for k in range(8):
                    P.op("pe", lambda e, fi=fi, k=k: e.matmul(
                        psum[:, bank, fi * 2:fi * 2 + 2], lhsT=ab[:, k, fi * 128:(fi + 1) * 128], rhs=cvb[:, k, :],
                        start=(k == 0), stop=(k == 7)), R=[(abk,), ("cvb",)], W=[PS(bank)])
            f0 = pc * 2
            P.op("dve", lambda e: e.tensor_tensor(
                out=modT[:, l, f0:f0 + 2, :], in0=psum[:, bank, 0:4].rearrange("p (a b) -> p a b", a=2),
                in1=adab[:, l, f0:f0 + 2, :], op=ALU.add), R=[PS(bank), ("adab",)], W=[("modT", l, pc)])

        def mod_finish(l):
            for which in range(2):
                g = gmix if which == 0 else gffn
                sc0 = 8 + 24 * which
                for v in range(2):
                    P.op("dve", lambda e, which=which, g=g, sc0=sc0, v=v: e.scalar_tensor_tensor(
                        out=gsc[:, l * 2 + which, :, v], in0=modT[:, l, sc0:sc0 + 8, v], scalar=1.0, in1=g[:, l, :],
                        op0=ALU.add, op1=ALU.mult), R=[("modT", l), ("gmix",), ("gffn",)], W=[("gsc", l, which, v)])

        A.reset()
        NB_ADA = 3
        adabuf = [A.alloc("ada%d" % i, (8, 256), BF16) for i in range(NB_ADA)]
        for pc in range(24):
            mod_dma(0, pc, adabuf[pc % NB_ADA])
            mod_compute(0, pc, adabuf[pc % NB_ADA], pc % 2)
        mod_finish(0)
        for v in range(2):
            P.op("dve", lambda e, v=v: e.tensor_copy(out=gsc[:, 8, :, v], in_=gfin[:]), R=[("gfin",)], W=[("gsc", 8, 0, v)])
        P.barrier()

        def norm_block(nidx, l, which, t0, w, dst, dcol, dkey, tmp, final=False, xsrc=None, xkey=None):
            v = vof(t0)
            sqb, sqk = tmp["sq"]
            rs, rsk = tmp["rs"]
            tt, ttk = tmp["tt"]
            bank = tmp["bank"]
            if xsrc is None:
                xsrc = lambda k: xT[:, k, t0:t0 + w]
            xk_ = (lambda k: ("xT", k)) if xkey is None else (lambda k: xkey)
            for k in range(8):
                P.op("act", lambda e, k=k: e.activation(out=sqb[:, k % 2, 0:w], in_=xsrc(k), func=AF.Square),
                     R=[xk_(k)], W=[(sqk, k % 2)])
                P.op("pe", lambda e, k=k: e.matmul(psum[:, bank, 0:w], lhsT=ones, rhs=sqb[:, k % 2, 0:w], start=(k == 0), stop=(k == 7)),
                     R=[(sqk, k % 2), ("cbf",)], W=[PS(bank)])
            P.op("act", lambda e: e.activation(out=rs[:, 0:w], in_=psum[:, bank, 0:w], func=AF.Ln, scale=1.0 / D, bias=zero1_eps[:]),
                 R=[PS(bank), ("eps",)], W=[(rsk,)])
            P.op("act", lambda e: e.activation(out=rs[:, 0:w], in_=rs[:, 0:w], func=AF.Exp, scale=-0.5), R=[(rsk,)], W=[(rsk,)])
            for k in range(8):
                if final:
                    P.op("dve", lambda e, k=k: e.scalar_tensor_tensor(out=dst[:, k, dcol:dcol + w], in0=xsrc(k),
                         scalar=gsc[:, nidx, k, v:v + 1], in1=rs[:, 0:w], op0=ALU.mult, op1=ALU.mult),
                         R=[xk_(k), (rsk,), ("gsc",)], W=[(dkey, k)])
                else:
                    P.op("dve", lambda e, k=k: e.scalar_tensor_tensor(out=tt[:, k % 2, 0:w], in0=xsrc(k),
                         scalar=gsc[:, nidx, k, v:v + 1], in1=rs[:, 0:w], op0=ALU.mult, op1=ALU.mult),
                         R=[xk_(k), (rsk,), ("gsc",)], W=[(ttk, k % 2)])
                    sh = modT[:, l, 24 * which + k, v:v + 1]
                    P.op("act", lambda e, k=k, sh=sh: e.activation(out=dst[:, k, dcol:dcol + w], in_=tt[:, k % 2, 0:w], func=AF.Identity, bias=sh, scale=1.0),
                         R=[(ttk, k % 2), ("modT", l)], W=[(dkey, k)])

        eps_t = sb("eps_t", (128, 1), F32)
        eps64_t = sb("eps64_t", (128, 1), F32)
        zero1_eps = eps_t
        P.op("pool", lambda e: e.memset(eps_t[:], EPS), W=[("eps",)])

        def norm_tmp(bank):
            return {"sq": A.alloc("nsq", (2, 512), BF16), "rs": A.alloc("nrs", (512,), F32),
                    "tt": A.alloc("ntt", (2, 512), F32), "bank": bank}

        def out_proj_residual(l, wsrc_fn, attn, attn_key, nk, gate_f0):
            wb = [A.alloc("wo%d" % i, (nk, 128), BF16) for i in range(2)]
            bi = 0
            P.dma("pool", wb[0][0], wsrc_fn(0), W=[(wb[0][1],)])
            for d in range(8):
                w, wk = wb[d % 2]
                if d + 1 < 8:
                    P.dma("pool", wb[(d + 1) % 2][0], wsrc_fn(d + 1), W=[(wb[(d + 1) % 2][1],)])
                for (t0, wd) in split_blocks(0, NT):
                    bank = 6 + bi % 2
                    bi += 1
                    for k in range(nk):
                        P.op("pe", lambda e, w=w, k=k, t0=t0, wd=wd, bank=bank: e.matmul(psum[:, bank, 0:wd], lhsT=w[:, k, :], rhs=attn[:, k, t0:t0 + wd],
                             start=(k == 0), stop=(k == nk - 1)), R=[(wk,), (attn_key,)], W=[PS(bank)])
                    v = vof(t0)
                    xk = [("xT", d, tt) for tt in range(t0 // 128, (t0 + wd + 127) // 128)]
                    P.op("dve", lambda e, d=d, t0=t0, wd=wd, bank=bank, v=v: e.scalar_tensor_tensor(
                        out=xT[:, d, t0:t0 + wd], in0=psum[:, bank, 0:wd], scalar=modT[:, l, gate_f0 + d, v:v + 1], in1=xT[:, d, t0:t0 + wd],
                        op0=ALU.mult, op1=ALU.add), R=[PS(bank), ("modT", l)] + xk, W=xk)

        def softmax_finish(bank, w, dst_ap, dkeys, rr, rrk):
            P.op("dve", lambda e: e.reciprocal(out=rr[64:128, 0:w], in_=psum[64:128, bank, 0:w]), R=[PS(bank)], W=[(rrk,)])
            P.op("dve", lambda e: e.tensor_tensor(out=dst_ap, in0=psum[0:64, bank, 0:w], in1=rr[64:128, 0:w], op=ALU.mult),
                 R=[PS(bank), (rrk,)], W=dkeys)

        def prompt_attention(qT, qkey, kT, kkey, vtm, vkey, base, dst_fn, tmpP, rrs, cnt, qkeyf=None, k128=False, escale=0.125):
            for sq in range(2):
                c0 = sq * 256
                u = cnt[0]
                cnt[0] += 1
                sbank = [0 + 3 * (u % 2), 1 + 3 * (u % 2)]
                obank = 2 + 3 * (u % 2)
                pt, ptk = tmpP[u % 2]
                for kt in range(2):
                    kc = c0 + kt * 128
                    r0_, r1_ = (0, 128) if k128 else (base, base + 64)
                    P.op("pe", lambda e, kc=kc, c0=c0, b=sbank[kt], r0_=r0_, r1_=r1_: e.matmul(psum[:, b, 0:256], lhsT=kT[r0_:r1_, kc:kc + 128],
                         rhs=qT[r0_:r1_, c0:c0 + 256], start=True, stop=True), R=[(kkey,), ((qkey,) if qkeyf is None else qkeyf(c0))], W=[PS(sbank[kt])])
                    P.op("act", lambda e, kt=kt, b=sbank[kt], pt=pt: e.activation(out=pt[:, kt, 0:256], in_=psum[:, b, 0:256], func=AF.Exp, scale=escale),
                         R=[PS(sbank[kt])], W=[(ptk, kt)])
                for kt in range(2):
                    P.op("pe", lambda e, kt=kt, sq=sq, pt=pt, obank=obank: e.matmul(psum[:, obank, 0:256], lhsT=vtm(sq * 2 + kt), rhs=pt[:, kt, 0:256],
                         start=(kt == 0), stop=(kt == 1)), R=[(ptk, kt), (vkey,)], W=[PS(obank)])
                rr, rrk = rrs[u % 2]
                dst, dkeys = dst_fn(c0, 256)
                softmax_finish(obank, 256, dst, dkeys, rr, rrk)

        def ab_layer(l):
            j = l // 2
            A.reset()
            hT, hk = A.alloc("hT", (8, NT), BF16)
            mark = A.off
            nt = norm_tmp(7)
            for (t0, w) in split_blocks(0, NT):
                norm_block(l * 2, l, 0, t0, w, hT, t0, hk, nt)
            P.barrier()
            if sub is not None and sub <= 1:
                return
            tiles_of = lambda a_, b_: list(range(a_ // 128, b_ // 128))
            psb = psum[:, 7, :].bitcast(BF16)
            def gla_part(pc):
                A.off = mark
                A.gen += 1
                og, ogk = A.alloc("og", (2, NT), BF16)
                wg, wgk = A.alloc("wg", (8, 800), BF16)
                P.dma("pool", wg, I["wG"][j, pc], W=[(wgk,)])
                Sbs, Sbsk = A.alloc("Sbs", (NTILE, 128), BF16)
                S32 = [A.alloc("S32_%d" % d, (128,), F32) for d in range(2)]
                Sfb, Sfbk = A.alloc("Sfb", (128,), BF16)
                lrx, lrxk = A.alloc("lrx", (128,), BF16)
                spb, spk = A.alloc("sp", (2, 128), F32)
                cb, cbk = A.alloc("cc", (2, 128), F32)
                En, Enk = A.alloc("En", (2, 128), F32)
                Ep, Epk = A.alloc("Ep", (2, 128), F32)
                ktm, ktmk = A.alloc("ktm", (128,), BF16)
                m1, m1k = A.alloc("m1", (2, 128), F32)
                m2, m2k = A.alloc("m2", (2, 128), F32)
                att, attk = A.alloc("att", (2, 128), BF16)
                osq, osqk = A.alloc("osq", (256,), BF16)
                ors, orsk = A.alloc("ors", (256,), F32)
                ot, otk = A.alloc("ot", (256,), F32)
                stmp, stmpk = A.alloc("stmp", (128,), F32)
                stg, stgk = A.alloc("stg", (128,), F32)
                ebt2 = [A.alloc("eb%d" % p_, (2,), F32) for p_ in range(2)]
                kt2 = [A.alloc("kt%d" % p_, (2, 128), BF16) for p_ in range(2)]
                vtm2 = [A.alloc("vtm%d" % p_, (256,), BF16) for p_ in range(2)]
                srg2 = [A.alloc("srg%d" % p_, (2, 128), F32) for p_ in range(2)]
                qpad2 = [[[A.alloc("qp%d%d%d" % (p_, d_, i_), (128,), BF16) for i_ in range(2)] for d_ in range(2)] for p_ in range(2)]
                for p_ in range(2):
                    for d_ in range(2):
                        for i_ in range(2):
                            P.op("pool", lambda e, p_=p_, d_=d_, i_=i_: e.memset(qpad2[p_][d_][i_][0][(1 - i_) * 64:(2 - i_) * 64, :], 0.0), W=[("qpz", p_, d_, i_)])

                def gate_stage(c0, dirs, pp):
                    ebt, ebk = ebt2[pp]
                    for k in range(8):
                        P.op("pe", lambda e, k=k: e.matmul(psum[:, 1, 0:128], lhsT=wg[:, k, 672:800], rhs=hT[:, k, c0:c0 + 128],
                             start=(k == 0), stop=(k == 7)), R=[(wgk,), (hk, k)], W=[PS(1)])
                    P.op("act", lambda e: e.activation(out=lrx, in_=psum[:, 1, 0:128], func=AF.Identity), R=[PS(1)], W=[(lrxk,)])
                    for d in dirs:
                        P.op("pe", lambda e, d=d: e.matmul(psum[:, 2, d * 128:(d + 1) * 128], lhsT=w2x[:, j, d, pc * 128:(pc + 1) * 128],
                             rhs=lrx, start=True, stop=True), R=[(lrxk,), ("w2x",)], W=[PS(2)])
                    for d in dirs:
                        P.op("act", lambda e, d=d: e.activation(out=spb[:, d, :], in_=psum[:, 2, d * 128:(d + 1) * 128], func=AF.Exp, scale=-1.0, bias=negb[:, j, d, pc:pc + 1]),
                             R=[PS(2), ("negb",)], W=[(spk, d)])
                        P.op("act", lambda e, d=d: e.activation(out=spb[:, d, :], in_=spb[:, d, :], func=AF.Ln, bias=1.0, scale=1.0), R=[(spk, d)], W=[(spk, d)])
                        P.op("dve", lambda e, d=d: e.tensor_tensor_scan(out=cb[:, d, :], data0=ones_f[:, 0:128], data1=spb[:, d, :],
                             initial=0.0, op0=ALU.mult, op1=ALU.add), R=[(spk, d), ("ones_f",)], W=[(cbk, d)])
                        P.op("act", lambda e, d=d: e.activation(out=ebt[:, d:d + 1], in_=cb[:, d, 127:128], func=AF.Exp, scale=-1.0 / 16), R=[(cbk, d)], W=[(ebk, d)])
                        if d == 1:
                            P.op("dve", lambda e: e.scalar_tensor_tensor(out=cb[:, 1, :], in0=cb[:, 1, :], scalar=cb[:, 1, 127:128],
                                 in1=spb[:, 1, :], op0=ALU.subtract, op1=ALU.subtract), R=[(cbk, 1), (spk, 1)], W=[(cbk, 1)])

                def v_stage(c0, pp):
                    vtm, vtmk = vtm2[pp]
                    for k in range(8):
                        P.op("pe", lambda e, k=k: e.matmul(psum[:, 3, 0:256], lhsT=hT[:, k, c0:c0 + 128], rhs=wg[:, k, 256:512], start=(k == 0), stop=(k == 7)),
                             R=[(wgk,), (hk, k)], W=[PS(3)])
                    P.op("act", lambda e: e.activation(out=vtm, in_=psum[:, 3, 0:256], func=AF.Identity), R=[PS(3)], W=[(vtmk,)])

                def state_update(d, pp):
                    S, Sk = S32[d]
                    ebt, ebk = ebt2[pp]
                    kt, ktk = kt2[pp]
                    vtm, vtmk = vtm2[pp]
                    P.op("pe", lambda e: e.transpose(out=psb[:, 0:128], in_=kt[:, d, :], identity=ident), R=[(ktk, d), ("cbf",)], W=[PS(7)])
                    P.op("act", lambda e: e.activation(out=ktm, in_=psb[:, 0:128], func=AF.Identity), R=[PS(7)], W=[(ktmk,)])
                    P.op("pe", lambda e: e.matmul(psum[:, 5, 256:512], lhsT=ktm, rhs=vtm, start=True, stop=True), R=[(ktmk,), (vtmk,)], W=[PS(5)])
                    for i in range(2):
                        r0 = i * 64
                        P.op("dve", lambda e, i=i, r0=r0: e.tensor_tensor(out=stmp[r0:r0 + 64, :], in0=psum[r0:r0 + 64, 5, 256 + i * 128:256 + (i + 1) * 128],
                             in1=S[r0:r0 + 64, :], op=ALU.add), R=[PS(5), (Sk,)], W=[(stmpk, i)])
                        P.op("dve", lambda e, i=i, r0=r0: e.tensor_scalar(out=S[r0:r0 + 64, :], in0=stmp[r0:r0 + 64, :], scalar1=ebt[r0:r0 + 64, d:d + 1], scalar2=None,
                             op0=ALU.mult), R=[(stmpk, i), (ebk, d)], W=[(Sk,)])

                def state_init(d, si):
                    S, Sk = S32[d]
                    if si < 2:
                        P.op("pool", lambda e: e.memset(S, 0.0), W=[(Sk,)])
                    else:
                        P.dma("sp", S, I["sgb" if d == 1 else "sgf"][j, :, pc, :], W=[(Sk,)])

                def state_out(d, si):
                    S, Sk = S32[d]
                    if si < 2:
                        P.op("act", lambda e: e.activation(out=stg, in_=S, func=AF.Identity), R=[(Sk,)], W=[(stgk,)])
                        P.dma("sp", O["o_gb" if d == 1 else "o_gf"][j, si, :, pc, :], stg, R=[(stgk,)])

                flatB = []
                for si, (s0, s1) in enumerate(SEQS):
                    ts_ = list(reversed(tiles_of(s0, s1)))
                    for q_, t in enumerate(ts_):
                        flatB.append((si, t, q_ == 0, q_ == len(ts_) - 1))

                def b_stage1(idx):
                    si, t, first, last = flatB[idx]
                    pp, c0 = idx % 2, t * 128
                    kt, ktk = kt2[pp]
                    for k in range(8):
                        P.op("pe", lambda e, k=k: e.matmul(psum[:, 0, 0:128], lhsT=wg[:, k, 128:256], rhs=hT[:, k, c0:c0 + 128],
                             start=(k == 0), stop=(k == 7)), R=[(wgk,), (hk, k)], W=[PS(0)])
                    gate_stage(c0, [1], pp)
                    P.op("act", lambda e: e.activation(out=En[:, 1, :], in_=cb[:, 1, :], func=AF.Exp, scale=-1.0 / 16), R=[(cbk, 1)], W=[(Enk, 1)])
                    P.op("dve", lambda e: e.tensor_tensor(out=kt[:, 1, :], in0=psum[:, 0, 0:128], in1=En[:, 1, :], op=ALU.mult),
                         R=[PS(0), (Enk, 1)], W=[(ktk, 1)])
                    v_stage(c0, pp)

                def b_stage2(idx):
                    si, t, first, last = flatB[idx]
                    pp = idx % 2
                    S, Sk = S32[1]
                    if first:
                        state_init(1, si)
                    P.op("act", lambda e: e.activation(out=Sbs[:, t, :], in_=S, func=AF.Identity), R=[(Sk,)], W=[(Sbsk, t)])
                    state_update(1, pp)
                    if last:
                        state_out(1, si)

                for idx in range(len(flatB) + 1):
                    la = P.capture(lambda: b_stage1(idx)) if idx < len(flatB) else []
                    lb = P.capture(lambda: b_stage2(idx - 1)) if idx >= 1 else []
                    P.replay_merged(la, lb)
                if sub is not None and sub <= 2:
                    P.barrier()
                    return
                flatF = []
                for si, (s0, s1) in enumerate(SEQS):
                    ts_ = tiles_of(s0, s1)
                    for q_, t in enumerate(ts_):
                        flatF.append((si, t, q_ == 0, q_ == len(ts_) - 1))

                def f_stage1(idx):
                    si, t, first, last = flatF[idx]
                    pp, c0 = idx % 2, t * 128
                    kt, ktk = kt2[pp]
                    srg, srgk = srg2[pp]
                    qp = qpad2[pp]
                    for qk in range(2):
                        for k in range(8):
                            P.op("pe", lambda e, k=k, qk=qk: e.matmul(psum[:, 0, qk * 128:(qk + 1) * 128],
                                 lhsT=wg[:, k, qk * 128:(qk + 1) * 128], rhs=hT[:, k, c0:c0 + 128], start=(k == 0), stop=(k == 7)),
                                 R=[(wgk,), (hk, k)], W=[PS(0)])
                    gate_stage(c0, [0, 1], pp)
                    P.op("act", lambda e: e.activation(out=En, in_=cb, func=AF.Exp, scale=-1.0 / 16), R=[(cbk,)], W=[(Enk,)])
                    P.op("act", lambda e: e.activation(out=Ep, in_=cb, func=AF.Exp, scale=1.0 / 16), R=[(cbk,)], W=[(Epk,)])
                    qps = psum[:, 0, 0:128]
                    kps = psum[:, 0, 128:256]
                    for i_ in range(2):
                        r0 = i_ * 64
                        P.op("dve", lambda e, i_=i_, r0=r0: e.scalar_tensor_tensor(out=qp[0][i_][0][r0:r0 + 64, :], in0=qps[r0:r0 + 64, :], scalar=0.125, in1=En[r0:r0 + 64, 0, :],
                             op0=ALU.mult, op1=ALU.mult), R=[PS(0), (Enk,)], W=[(qp[0][i_][1],)])
                    P.op("dve", lambda e: e.tensor_tensor(out=kt[:, 0, :], in0=kps, in1=Ep[:, 0, :], op=ALU.mult), R=[PS(0), (Epk,)], W=[(ktk, 0)])
                    for i_ in range(2):
                        r0 = i_ * 64
                        P.op("dve", lambda e, i_=i_, r0=r0: e.scalar_tensor_tensor(out=qp[1][i_][0][r0:r0 + 64, :], in0=qps[r0:r0 + 64, :], scalar=0.125, in1=Ep[r0:r0 + 64, 1, :],
                             op0=ALU.mult, op1=ALU.mult), R=[PS(0), (Epk,)], W=[(qp[1][i_][1],)])
                    P.op("dve", lambda e: e.tensor_tensor(out=kt[:, 1, :], in0=kps, in1=En[:, 1, :], op=ALU.mult), R=[PS(0), (Enk,)], W=[(ktk, 1)])
                    v_stage(c0, pp)
                    for i in range(2):
                        for k in range(8):
                            P.op("pe", lambda e, k=k, i=i: e.matmul(psum[:, 4, i * 128:(i + 1) * 128], lhsT=wg[:, k, 512 + i * 128:512 + (i + 1) * 128],
                                 rhs=hT[:, k, c0:c0 + 128], start=(k == 0), stop=(k == 7)), R=[(wgk,), (hk, k)], W=[PS(4)])
                    P.op("act", lambda e: e.activation(out=srg, in_=psum[:, 4, 0:256].rearrange("p (a b) -> p a b", a=2), func=AF.Exp, scale=-1.0), R=[PS(4)], W=[(srgk,)])
                    P.op("act", lambda e: e.activation(out=srg, in_=srg, func=AF.Ln, bias=1.0, scale=1.0), R=[(srgk,)], W=[(srgk,)])
                    P.op("act", lambda e: e.activation(out=srg, in_=srg, func=AF.Exp, scale=-1.0), R=[(srgk,)], W=[(srgk,)])
                    P.op("dve", lambda e: e.tensor_tensor(out=srg, in0=psum[:, 4, 0:256].rearrange("p (a b) -> p a b", a=2), in1=srg, op=ALU.mult), R=[PS(4), (srgk,)], W=[(srgk,)])

                def f_stage2(idx):
                    si, t, first, last = flatF[idx]
                    pp, c0 = idx % 2, t * 128
                    kt, ktk = kt2[pp]
                    srg, srgk = srg2[pp]
                    qp = qpad2[pp]
                    vtm, vtmk = vtm2[pp]
                    S, Sk = S32[0]
                    if first:
                        state_init(0, si)
                    P.op("act", lambda e: e.activation(out=Sfb, in_=S, func=AF.Identity), R=[(Sk,)], W=[(Sfbk,)])
                    for i in range(2):
                        P.op("pe", lambda e, i=i: e.matmul(psum[:, 5, i * 128:(i + 1) * 128], lhsT=kt[:, 0, :], rhs=qp[0][i][0],
                             start=True, stop=True), R=[(qp[0][i][1],), ("qpz",), (ktk, 0)], W=[PS(5)])
                        P.op("pe", lambda e, i=i: e.matmul(psum[:, 6, i * 128:(i + 1) * 128], lhsT=kt[:, 1, :], rhs=qp[1][i][0],
                             start=True, stop=True), R=[(qp[1][i][1],), ("qpz",), (ktk, 1)], W=[PS(6)])
                    P.op("dve", lambda e: e.tensor_tensor(out=m1, in0=psum[:, 5, 0:256].rearrange("p (a b) -> p a b", a=2), in1=mbf[:, 0, 0:2, :], op=ALU.mult), R=[PS(5), ("mbf",)], W=[(m1k,)])
                    P.op("dve", lambda e: e.tensor_tensor(out=m2, in0=psum[:, 6, 0:256].rearrange("p (a b) -> p a b", a=2), in1=mbf[:, 1, 0:2, :], op=ALU.mult), R=[PS(6), ("mbf",)], W=[(m2k,)])
                    P.op("pool", lambda e: e.tensor_tensor(out=att, in0=m1, in1=m2, op=ALU.add), R=[(m1k,), (m2k,)], W=[(attk,)])
                    for i in range(2):
                        oc = 256 + i * 128
                        P.op("pe", lambda e, i=i, oc=oc: e.matmul(psum[:, 7, oc:oc + 128], lhsT=vtm[:, i * 128:(i + 1) * 128], rhs=att[:, i, :], start=True, stop=False),
                             R=[(vtmk,), (attk,)], W=[PS(7)])
                        P.op("pe", lambda e, i=i, oc=oc: e.matmul(psum[:, 7, oc:oc + 128], lhsT=Sfb, rhs=qp[0][i][0], start=False, stop=False),
                             R=[(Sfbk,), (qp[0][i][1],)], W=[PS(7)])
                        P.op("pe", lambda e, i=i, oc=oc: e.matmul(psum[:, 7, oc:oc + 128], lhsT=Sbs[:, t, :], rhs=qp[1][i][0], start=False, stop=True),
                             R=[(Sbsk, t), (qp[1][i][1],)], W=[PS(7)])
                    P.op("act", lambda e: e.activation(out=osq, in_=psum[:, 7, 256:512], func=AF.Square), R=[PS(7)], W=[(osqk,)])
                    P.op("pe", lambda e: e.matmul(psum[:, 6, 256:512], lhsT=ones, rhs=osq, start=True, stop=True), R=[(osqk,), ("cbf",)], W=[PS(6)])
                    P.op("act", lambda e: e.activation(out=ors, in_=psum[:, 6, 256:512], func=AF.Ln, scale=1.0 / 128, bias=eps_t[:]), R=[PS(6), ("eps",)], W=[(orsk,)])
                    P.op("act", lambda e: e.activation(out=ors, in_=ors, func=AF.Exp, scale=-0.5), R=[(orsk,)], W=[(orsk,)])
                    P.op("dve", lambda e: e.tensor_tensor(out=ot, in0=psum[:, 7, 256:512], in1=ors, op=ALU.mult), R=[PS(7), (orsk,)], W=[(otk,)])
                    for i in range(2):
                        hh = 2 * pc + i
                        P.op("dve", lambda e, i=i, hh=hh: e.scalar_tensor_tensor(out=og[:, i, c0:c0 + 128], in0=ot[:, i * 128:(i + 1) * 128], scalar=glag[:, j, hh:hh + 1],
                             in1=srg[:, i, :], op0=ALU.mult, op1=ALU.mult), R=[(otk,), (srgk,), ("glag",)], W=[(ogk, i)])
                    state_update(0, pp)
                    if last:
                        state_out(0, si)

                for idx in range(len(flatF) + 1):
                    la = P.capture(lambda: f_stage1(idx)) if idx < len(flatF) else []
                    lb = P.capture(lambda: f_stage2(idx - 1)) if idx >= 1 else []
                    P.replay_merged(la, lb)
                P.barrier()
                out_proj_residual(l, lambda d: I["wOab"][j, d, :, 4 + 2 * pc:6 + 2 * pc, :], og, ogk, 2, 16)
                P.barrier()

            for pc_ in range(2):
                gla_part(pc_)
                if sub is not None and sub <= 3 + pc_:
                    return
            A.off = mark
            A.gen += 1
            oa, oak = A.alloc("oa", (1, NT), BF16)
            wq3 = [A.alloc("wa%d" % i, (8, 128), BF16) for i in range(3)]
            qA, qAk = A.alloc("qA", (NT,), BF16)
            kAh = [A.alloc("kA%d" % i, (NT,), BF16) for i in range(2)]
            kAk = "kAall"
            vA, vAk = A.alloc("vA", (NTILE, 2, 128), BF16)
            kst, kstk = A.alloc("kst", (512,), F32)
            vst, vstk = A.alloc("vst", (4, 128), F32)
            T2b, T2k = A.alloc("T2b", (2, 7, 128), BF16)
            T2ib, T2ik = A.alloc("T2ib", (2, 7, 128), BF16)
            kcxh = [A.alloc("kcx%d" % i, (256,), BF16) for i in range(2)]
            kcxk = "kcxall"
            vcx, vcxk = A.alloc("vcx", (2, 2, 128), BF16)
            ptI = [A.alloc("ptI%d" % i, (5, 128), BF16) for i in range(2)]
            ptE = [A.alloc("ptE%d" % i, (4, 128), BF16) for i in range(2)]
            ptC = [A.alloc("ptC%d" % i, (2, 128), BF16) for i in range(2)]
            ptP = [A.alloc("ptP%d" % i, (2, 256), BF16) for i in range(2)]
            tb = [A.alloc("tb%d" % i, (5, 128), F32) for i in range(2)]
            rrs = [A.alloc("rr%d" % i, (256,), F32) for i in range(2)]
            pL = [A.alloc("pL%d" % i, (4, 128), BF16) for i in range(2)]
            pX = [A.alloc("pX%d" % i, (3, 128), BF16) for i in range(2)]
            wmark = A.off
            P.op("pool", lambda e: e.memset(vA[:, :, :, 64:128], 1.0), W=[(vAk, "ones")])
            P.op("pool", lambda e: e.memset(kAh[0][0][64:128, :], 0.0), W=[("kAz", 0)])
            P.op("pool", lambda e: e.memset(kAh[1][0][0:64, :], 0.0), W=[("kAz", 1)])
            P.op("pool", lambda e: e.memset(kcxh[0][0][64:128, :], 0.0), W=[("kAz", 2)])
            P.op("pool", lambda e: e.memset(kcxh[1][0][0:64, :], 0.0), W=[("kAz", 3)])
            P.op("pool", lambda e: e.memset(vcx[:, :, :, 64:128], 1.0), W=[(vcxk, "ones")])
            cnt = [0]
            for hp in range(4):
                for wi in range(3):
                    P.dma("pool", wq3[wi][0], I["wA"][j, wi, hp], W=[(wq3[wi][1],)])
                import os
                CUT = os.environ.get("CUT", "")
                if "a" not in CUT:
                    P.dma("pool", T2b, I["T2"][j, hp], W=[(T2k,)])
                    P.dma("pool", T2ib, I["T2i"][j, hp], W=[(T2ik,)])
                    P.dma("pool", kcxh[0][0][0:64, :], I["cnk"][j, hp, 0:64, :], W=[(kcxk, 0)])
                    P.dma("pool", kcxh[1][0][64:128, :], I["cnk"][j, hp, 64:128, :], W=[(kcxk, 1)])
                    P.dma("pool", vcx[:, :, :, 0:64], I["cnv"][j, hp], W=[(vcxk, "v")])
                bi = 0
                for wi, (dst, dk_) in enumerate([(qA, qAk), (None, kAk)] if "b" not in CUT else []):
                    for (t0, wd) in split_blocks(0, NT):
                        bank = 6 + (bi % 2)
                        bi += 1
                        for k in range(8):
                            P.op("pe", lambda e, wi=wi, k=k, t0=t0, wd=wd, bank=bank: e.matmul(psum[:, bank, 0:wd], lhsT=wq3[wi][0][:, k, :], rhs=hT[:, k, t0:t0 + wd],
                                 start=(k == 0), stop=(k == 7)), R=[(wq3[wi][1],), (hk, k)], W=[PS(bank)])
                        if wi == 0:
                            P.op("act", lambda e, dst=dst, t0=t0, wd=wd, bank=bank: e.activation(out=dst[:, t0:t0 + wd], in_=psum[:, bank, 0:wd], func=AF.Identity, scale=0.125),
                                 R=[PS(bank)], W=[(dk_, t0 // 512)])
                        else:
                            for hf in range(2):
                                r0 = hf * 64
                                P.op("act", lambda e, hf=hf, r0=r0, t0=t0, wd=wd, bank=bank: e.activation(out=kAh[hf][0][r0:r0 + 64, t0:t0 + wd], in_=psum[r0:r0 + 64, bank, 0:wd], func=AF.Identity),
                                     R=[PS(bank)], W=[(dk_, t0 // 512, hf)])
                        if wi == 1 and t0 == 0 and "d" not in CUT:
                            P.op("dve", lambda e, bank=bank: e.tensor_copy(out=kst, in_=psum[:, bank, 0:512]), R=[PS(bank)], W=[(kstk,)])
                            if "e" not in CUT:
                                P.dma("sp", O["o_nak"][j, hp], kst, R=[(kstk,)])
                for t in range(NTILE if "c" not in CUT else 0):
                    bank = 6 + (t % 2)
                    for k in range(8):
                        P.op("pe", lambda e, k=k, t=t, bank=bank: e.matmul(psum[:, bank, 0:128], lhsT=hT[:, k, t * 128:(t + 1) * 128], rhs=wq3[2][0][:, k, :],
                             start=(k == 0), stop=(k == 7)), R=[(wq3[2][1],), (hk, k)], W=[PS(bank)])
                    P.op("act", lambda e, t=t, bank=bank: e.activation(out=vA[:, t, :, 0:64], in_=psum[:, bank, 0:128].rearrange("p (a b) -> p a b", a=2), func=AF.Identity),
                         R=[PS(bank)], W=[(vAk, "v", t)])
                    if t < 4:
                        P.op("dve", lambda e, t=t, bank=bank: e.tensor_copy(out=vst[:, t, :], in_=psum[:, bank, 0:128]), R=[PS(bank)], W=[(vstk, t)])
                P.dma("sp", O["o_nav"][j, hp].rearrange("t p c -> p t c"), vst, R=[(vstk,)])
                if sub == 5:
                    P.barrier()
                    return
                def na_head(i):
                    base = i * 64
                    kAi = kAh[i][0]
                    kci = kcxh[i][0]
                    prompt_attention(qA, qAk, kAi, kAk, lambda tt, i=i: vA[:, tt, i, :], vAk, base,
                                     lambda c0, w, base=base: (oa[base:base + 64, 0, c0:c0 + w], [(oak, base, c0)]), ptP, rrs, cnt, k128=True, escale=1.0)
                    nn = 16 if sub is None else {6: 0, 7: 1, 8: 3, 9: 16}.get(sub, 16)
                    u0 = cnt[0]
                    cnt[0] += nn

                    def info(n):
                        u = u0 + n
                        par = u % 2
                        interior = 2 <= n <= 13
                        if interior:
                            ms = list(range(n - 2, n + 3))
                        elif n < 2:
                            ms = [0, 1, 2, 3]
                        else:
                            ms = [12, 13, 14, 15]
                        e0 = (2 * (ms[0] - n) + 7 - 1) // 2
                        return par, 0 + 3 * par, 1 + 3 * par, 2 + 3 * par, 512 + n * 128, interior, ms, e0

                    def st1(n):
                        par, sb_, cb_, ob_, q0, interior, ms, e0 = info(n)
                        nm = len(ms)
                        Tt, Ttk = (T2ib, T2ik) if interior else (T2b, T2k)
                        for ci, m in enumerate(ms):
                            k0 = 512 + m * 128
                            bk_, c_ = (sb_, ci * 128) if ci < 4 else (cb_, 256)
                            P.op("pe", lambda e, k0=k0, bk_=bk_, c_=c_: e.matmul(psum[:, bk_, c_:c_ + 128], lhsT=kAi[:, k0:k0 + 128],
                                 rhs=qA[:, q0:q0 + 128], start=True, stop=False), R=[(kAk,), ("kAz",), (qAk,)], W=[PS(bk_)])
                            P.op("pe", lambda e, ci=ci, bk_=bk_, c_=c_: e.matmul(psum[:, bk_, c_:c_ + 128], lhsT=ident,
                                 rhs=Tt[:, i, e0 + ci, :], start=False, stop=True), R=[(Ttk,), ("cbf",)], W=[PS(bk_)])
                        for cc in range(2):
                            P.op("pe", lambda e, cc=cc: e.matmul(psum[:, cb_, cc * 128:(cc + 1) * 128], lhsT=kci[:, cc * 128:(cc + 1) * 128],
                                 rhs=qA[:, q0:q0 + 128], start=True, stop=True), R=[(kcxk,), ("kAz",), (qAk,)], W=[PS(cb_)])
                        n4 = min(nm, 4)
                        nx = 3 if nm == 5 else 2
                        pl, plk = pL[par]
                        px, pxk = pX[par]
                        P.op("act", lambda e: e.activation(out=pl[:, 0:n4, :], in_=psum[:, sb_, 0:n4 * 128].rearrange("p (a b) -> p a b", a=n4), func=AF.Exp),
                             R=[PS(sb_)], W=[(plk,)])
                        P.op("act", lambda e: e.activation(out=px[:, 0:nx, :], in_=psum[:, cb_, 0:nx * 128].rearrange("p (a b) -> p a b", a=nx), func=AF.Exp),
                             R=[PS(cb_)], W=[(pxk,)])

                    def st2(n):
                        par, sb_, cb_, ob_, q0, interior, ms, e0 = info(n)
                        pl, plk = pL[par]
                        px, pxk = pX[par]
                        for ci, m in enumerate(ms):
                            tt = 4 + m
                            rhs_ = pl[:, ci, :] if ci < 4 else px[:, 2, :]
                            P.op("pe", lambda e, ci=ci, tt=tt, rhs_=rhs_: e.matmul(psum[:, ob_, 0:128], lhsT=vA[:, tt, i, :], rhs=rhs_, start=(ci == 0), stop=False),
                                 R=[(vAk,), (plk,), (pxk,)], W=[PS(ob_)])
                        for cc in range(2):
                            P.op("pe", lambda e, cc=cc: e.matmul(psum[:, ob_, 0:128], lhsT=vcx[:, cc, i, :], rhs=px[:, cc, :], start=False, stop=(cc == 1)),
                                 R=[(vcxk,), (pxk,)], W=[PS(ob_)])
                        rr, rrk = rrs[par]
                        softmax_finish(ob_, 128, oa[base:base + 64, 0, q0:q0 + 128], [(oak, base, q0)], rr, rrk)

                    for n in range(nn + 1):
                        if n < nn:
                            st1(n)
                        if n >= 1:
                            st2(n - 1)
                for i_ in range(2):
                    na_head(i_)
                if sub is not None and sub >= 6:
                    P.barrier()
                    return
                A.off = wmark
                out_proj_residual(l, lambda d, hp=hp: I["wOab"][j, d, :, hp:hp + 1, :], oa, oak, 1, 16)
            P.barrier()

        def gqa_layer(l):
            j = l // 2
            A.reset()
            hT, hk = A.alloc("hT", (8, NT), BF16)
            oT, oTk = A.alloc("attn", (2, NT), BF16)
            nt = norm_tmp(7)
            for (t0, w) in split_blocks(0, NT):
                norm_block(l * 2, l, 0, t0, w, hT, t0, hk, nt)
            wq2 = [A.alloc("wq%d" % i, (8, 128), BF16) for i in range(2)]
            wkb, wkk = A.alloc("wk", (8, 128), BF16)
            wvb, wvk = A.alloc("wv", (8, 64), BF16)
            kGh = [A.alloc("kG%d" % i, (NT + 256,), BF16) for i in range(2)]
            kGk = "kGall"
            vG, vGk = A.alloc("vG", (NTILE + 2, 128), BF16)
            cosb = [A.alloc("cos%d" % i, (512,), F32) for i in range(2)]
            sinb = [A.alloc("sin%d" % i, (512,), F32) for i in range(2)]
            sqb, sqk = nt["sq"]
            rsb, rsk = A.alloc("grs", (2, 512), F32)
            qnb, qnk = nt["tt"]
            qbb, qbk = A.alloc("gqnb", (2, 512), BF16)
            t1_, t1k_ = A.alloc("gt1", (512,), F32)
            t2_, t2k_ = A.alloc("gt2", (512,), F32)
            kst, kstk = t1_, t1k_
            vst, vstk = A.alloc("vst", (4, 64), F32)
            ptS = [A.alloc("ptS%d" % i, (2, 512), BF16) for i in range(3)]
            ptP = [A.alloc("ptP%d" % i, (2, 256), BF16) for i in range(2)]
            rrs = [A.alloc("rr%d" % i, (512,), F32) for i in range(1)] * 2
            P.op("pool", lambda e: e.memset(vG[:, :, 64:128], 1.0), W=[(vGk, "ones")])
            P.op("pool", lambda e: e.memset(kGh[0][0][64:128, :], 0.0), W=[("kGz", 0)])
            P.op("pool", lambda e: e.memset(kGh[1][0][0:64, :], 0.0), W=[("kGz", 1)])
            cnt = [0]
            blocks = split_blocks(0, NT)

            def group(g):
                for c2 in range(2):
                    P.dma("pool", wq2[c2][0], I["wQ"][j, 2 * g + c2], W=[(wq2[c2][1],)])
                P.dma("pool", wkb, I["wK"][j, g], W=[(wkk,)])
                P.dma("pool", wvb, I["wV"][j, g], W=[(wvk,)])
                P.dma("pool", kGh[0][0][0:64, NT:NT + 256], I["cgk"][j, g, 0:64, :], W=[(kGk, "ctx", 0)])
                P.dma("pool", kGh[1][0][64:128, NT:NT + 256], I["cgk"][j, g, 64:128, :], W=[(kGk, "ctx", 1)])
                P.dma("pool", vG[:, NTILE:NTILE + 2, 0:64], I["cgv"][j, g], W=[(vGk, "ctx")])
                units = [(bi, which) for bi in range(len(blocks)) for which in range(3)]
                nu = len(units)

                def uinfo(u):
                    bi, which = units[u]
                    t0, wd = blocks[bi]
                    if which < 2:
                        return t0, wd, wq2[which], qng, oT[:, which, t0:t0 + wd], [(oTk, which, t0 // 512)], which
                    return t0, wd, (wkb, wkk), kng, None, [(kGk, "tok", t0 // 512)], which

                def stA(u):
                    t0, wd, (ws, wsk), gvec, dst, dkeys, which = uinfo(u)
                    pb, s = u % 3, u % 2
                    if which == 0 and t0 >= 512:
                        bsel = (t0 // 512) % 2
                        P.dma("sp", cosb[bsel][0], I["cosT"][:, t0 - 512:t0], W=[(cosb[bsel][1],)])
                        P.dma("sp", sinb[bsel][0], I["sinT"][:, t0 - 512:t0], W=[(sinb[bsel][1],)])
                    for k in range(8):
                        P.op("pe", lambda e, k=k: e.matmul(psum[:, pb, 0:wd], lhsT=ws[:, k, :], rhs=hT[:, k, t0:t0 + wd],
                             start=(k == 0), stop=(k == 7)), R=[(wsk,), (hk, k)], W=[PS(pb)])
                    P.op("act", lambda e: e.activation(out=sqb[:, s, 0:wd], in_=psum[:, pb, 0:wd], func=AF.Square), R=[PS(pb)], W=[(sqk, s)])

                def stB(u):
                    t0, wd, (ws, wsk), gvec, dst, dkeys, which = uinfo(u)
                    pb, s = u % 3, u % 2
                    P.op("pe", lambda e: e.matmul(psum[:, 3, 0:wd], lhsT=blockones, rhs=sqb[:, s, 0:wd], start=True, stop=True), R=[(sqk, s), ("cbf",)], W=[PS(3)])
                    P.op("act", lambda e: e.activation(out=rsb[:, s, 0:wd], in_=psum[:, 3, 0:wd], func=AF.Ln, scale=1.0 / 64, bias=eps_t[:]), R=[PS(3), ("eps",)], W=[(rsk, s)])
                    P.op("act", lambda e: e.activation(out=rsb[:, s, 0:wd], in_=rsb[:, s, 0:wd], func=AF.Exp, scale=-0.5), R=[(rsk, s)], W=[(rsk, s)])
                    if t0 < 512:
                        if which < 2:
                            P.op("dve", lambda e: e.scalar_tensor_tensor(out=dst, in0=psum[:, pb, 0:wd], scalar=gvec[:, j:j + 1], in1=rsb[:, s, 0:wd], op0=ALU.mult, op1=ALU.mult),
                                 R=[PS(pb), (rsk, s), ("qkng",)], W=dkeys)
                        else:
                            for hf in range(2):
                                r0 = hf * 64
                                P.op("dve", lambda e, hf=hf, r0=r0: e.scalar_tensor_tensor(out=kGh[hf][0][r0:r0 + 64, t0:t0 + wd], in0=psum[r0:r0 + 64, pb, 0:wd], scalar=gvec[r0:r0 + 64, j:j + 1],
                                     in1=rsb[r0:r0 + 64, s, 0:wd], op0=ALU.mult, op1=ALU.mult), R=[PS(pb), (rsk, s), ("qkng",)], W=dkeys)
                        if which == 2:
                            P.op("dve", lambda e: e.scalar_tensor_tensor(out=kst[:, 0:wd], in0=psum[:, pb, 0:wd], scalar=gvec[:, j:j + 1], in1=rsb[:, s, 0:wd], op0=ALU.mult, op1=ALU.mult),
                                 R=[PS(pb), (rsk, s), ("qkng",)], W=[(kstk,)])
                            P.dma("sp", O["o_gk"][j, g], kst[0:64, :], R=[(kstk,)])
                        return
                    P.op("dve", lambda e: e.scalar_tensor_tensor(out=qnb[:, s, 0:wd], in0=psum[:, pb, 0:wd], scalar=gvec[:, j:j + 1], in1=rsb[:, s, 0:wd], op0=ALU.mult, op1=ALU.mult),
                         R=[PS(pb), (rsk, s), ("qkng",)], W=[(qnk, s)])
                    P.op("act", lambda e: e.activation(out=qbb[:, s, 0:wd], in_=qnb[:, s, 0:wd], func=AF.Identity), R=[(qnk, s)], W=[(qbk, s)])

                def stC(u):
                    t0, wd, (ws, wsk), gvec, dst, dkeys, which = uinfo(u)
                    if t0 < 512:
                        return
                    s = u % 2
                    bsel = (t0 // 512) % 2
                    cs, csk = cosb[bsel]
                    sn, snk = sinb[bsel]
                    P.op("pe", lambda e: e.matmul(psum[:, 4, 0:wd], lhsT=rperm, rhs=qbb[:, s, 0:wd], start=True, stop=True), R=[(qbk, s), ("cbf",)], W=[PS(4)])
                    P.op("dve", lambda e: e.tensor_tensor(out=t1_[:, 0:wd], in0=qnb[:, s, 0:wd], in1=cs[:, 0:wd], op=ALU.mult), R=[(qnk, s), (csk,)], W=[(t1k_,)])
                    P.op("dve", lambda e: e.tensor_tensor(out=t2_[:, 0:wd], in0=psum[:, 4, 0:wd], in1=sn[:, 0:wd], op=ALU.mult), R=[PS(4), (snk,)], W=[(t2k_,)])
                    if which < 2:
                        P.op("pool", lambda e: e.tensor_tensor(out=dst, in0=t1_[:, 0:wd], in1=t2_[:, 0:wd], op=ALU.add), R=[(t1k_,), (t2k_,)], W=dkeys)
                    else:
                        for hf in range(2):
                            r0 = hf * 64
                            P.op("pool", lambda e, hf=hf, r0=r0: e.tensor_tensor(out=kGh[hf][0][r0:r0 + 64, t0:t0 + wd], in0=t1_[r0:r0 + 64, 0:wd], in1=t2_[r0:r0 + 64, 0:wd], op=ALU.add),
                                 R=[(t1k_,), (t2k_,)], W=dkeys)

                for i in range(nu + 2):
                    if i < nu:
                        stA(i)
                    if 0 <= i - 1 < nu:
                        stB(i - 1)
                    if 0 <= i - 2 < nu:
                        stC(i - 2)
                for t in range(NTILE):
                    pb = 6 + (t % 2)
                    for k in range(8):
                        P.op("pe", lambda e, k=k, t=t, pb=pb: e.matmul(psum[:, pb, 0:64], lhsT=hT[:, k, t * 128:(t + 1) * 128], rhs=wvb[:, k, :],
                             start=(k == 0), stop=(k == 7)), R=[(wvk,), (hk, k)], W=[PS(pb)])
                    P.op("act", lambda e, t=t, pb=pb: e.activation(out=vG[:, t, 0:64], in_=psum[:, pb, 0:64], func=AF.Identity), R=[PS(pb)], W=[(vGk, "v", t)])
                    if t < 4:
                        P.op("dve", lambda e, t=t, pb=pb: e.tensor_copy(out=vst[:, t, :], in_=psum[:, pb, 0:64]), R=[PS(pb)], W=[(vstk, t)])
                P.dma("sp", O["o_gv"][j, g].rearrange("t p c -> p t c"), vst, R=[(vstk,)])
                for hq in range(4):
                    ch = hq // 2
                    base = (hq % 2) * 64
                    qv = oT[:, ch, :]
                    prompt_attention(qv, oTk, kGh[hq % 2][0], kGk, lambda tt: vG[:, tt, :], vGk, base,
                                     lambda c0, w, ch=ch, base=base: (oT[base:base + 64, ch, c0:c0 + w], [(oTk, ch, 0, base, c0)]), ptP, rrs, cnt,
                                     qkeyf=lambda c0, ch=ch, base=base: (oTk, ch, 0), k128=True)
                chunks = [(512 + m * 128, 4 + m) for m in range(16)] + [(NT, NTILE), (NT + 128, NTILE + 1)]
                items = [(hq, qb, pi) for hq in range(4) for qb in range(4) for pi in range(9)]
                LA = 2

                def emit_qk(kk):
                    hq, qb, pi = items[kk]
                    ch, base, q0 = hq // 2, (hq % 2) * 64, 512 + qb * 512
                    slot = kk % 3
                    pt, ptk = ptS[kk % 3]
                    for e_ in range(2):
                        kc, vt = chunks[2 * pi + e_]
                        P.op("pe", lambda e, e_=e_, kc=kc: e.matmul(psum[:, 2 * slot + e_, :], lhsT=kGh[hq % 2][0][:, kc:kc + 128],
                             rhs=oT[:, ch, q0:q0 + 512], start=True, stop=True), R=[(kGk,), ("kGz",), (oTk, ch, qb + 1)], W=[PS(2 * slot + e_)])
                    P.op("act", lambda e: e.activation(out=pt, in_=psum[:, 2 * slot:2 * slot + 2, :], func=AF.Exp, scale=0.125),
                         R=[PS(2 * slot), PS(2 * slot + 1)], W=[(ptk,)])

                def emit_pv(kk):
                    hq, qb, pi = items[kk]
                    ch, base, q0 = hq // 2, (hq % 2) * 64, 512 + qb * 512
                    u = hq * 4 + qb
                    ob_ = 6 + (u % 2)
                    pt, ptk = ptS[kk % 3]
                    for e_ in range(2):
                        kc, vt = chunks[2 * pi + e_]
                        P.op("pe", lambda e, e_=e_, vt=vt: e.matmul(psum[:, ob_, :], lhsT=vG[:, vt, :], rhs=pt[:, e_, :], start=(pi == 0 and e_ == 0), stop=(pi == 8 and e_ == 1)),
                             R=[(vGk,), (ptk,)], W=[PS(ob_)])
                    if pi == 8:
                        rr, rrk = rrs[u % 2]
                        softmax_finish(ob_, 512, oT[base:base + 64, ch, q0:q0 + 512], [(oTk, ch, qb + 1, base)], rr, rrk)

                for kk in range(len(items) + LA):
                    if kk < len(items):
                        emit_qk(kk)
                    if kk >= LA:
                        emit_pv(kk - LA)
                gmark = A.off
                out_proj_residual(l, lambda d, g=g: I["wOc"][j, d, :, 2 * g:2 * g + 2, :], oT, oTk, 2, 16)
                A.off = gmark

            for g_ in range(4):
                group(g_)
            P.barrier()

        def ffn_layer(l):
            A.reset()
            NG = 4
            GW = NT // NG
            hG2 = [A.alloc("hG%d" % i, (8, GW + 2), BF16) for i in range(2)]
            act, actk = A.alloc("act", (NJ, GW), BF16)
            wup = [A.alloc("wup%d" % i, (2, 8, 128), BF16) for i in range(3)]
            wdn = [A.alloc("wdn%d" % i, (NJ, 128), BF16) for i in range(2)]
            usb = [[A.alloc("u%d%d" % (i, vg), (GW + 2,), F32) for vg in range(2)] for i in range(2)]
            Ab = [[A.alloc("A%d%d" % (i, vg), (GW,), F32) for vg in range(2)] for i in range(2)]
            Bg, Bgk = A.alloc("Bg", (GW,), F32)
            nt = norm_tmp(7)
            adaF = [A.alloc("adaF%d" % i, (8, 256), BF16) for i in range(2)] if l + 1 < DEPTH else None
            xsave, xsk = A.alloc("xsave", (8, 4), F32)
            for g in range(1, NG):
                P.op("dve", lambda e, g=g: e.tensor_copy(out=xsave[:, :, g:g + 1], in_=xT[:, :, g * GW - 1:g * GW]), R=[("xT",)], W=[(xsk, g)])
            pbi = [0]
            up_items = [(g, jj) for g in range(NG) for jj in range(NJ)]
            dn_items = [(g, d) for g in range(NG) for d in range(8)]
            up_next = [0]
            dn_next = [0]

            def up_prefetch(upto):
                while up_next[0] <= min(upto, len(up_items) - 1):
                    i = up_next[0]
                    wu, wuk = wup[i % 3]
                    P.dma("pool", wu, I["wUp"][l, up_items[i][1]].rearrange("v p k c -> p v k c"), W=[(wuk,)])
                    up_next[0] += 1

            def dn_prefetch(upto):
                while dn_next[0] <= min(upto, len(dn_items) - 1):
                    i = dn_next[0]
                    wd_, wdk = wdn[i % 2]
                    P.dma("pool", wd_, I["wDn"][l, dn_items[i][1]], W=[(wdk,)])
                    dn_next[0] += 1

            def ffn_norm(g):
                hG, hGk = hG2[g % 2]
                t0g, t1g = g * GW, (g + 1) * GW
                lo = t0g - (0 if t0g in SEQ_STARTS else 1)
                hi = t1g + (0 if t1g in SEQ_ENDS else 1)
                if lo < t0g:
                    norm_block(l * 2 + 1, l, 1, lo, 1, hG, 0, hGk, nt, xsrc=lambda k: xsave[:, k, g:g + 1], xkey=(xsk, g))
                for (t0, w) in split_blocks(t0g, hi):
                    norm_block(l * 2 + 1, l, 1, t0, w, hG, t0 - lo, hGk, nt)

            def ffn_group(g):
                hG, hGk = hG2[g % 2]
                t0g, t1g = g * GW, (g + 1) * GW
                lo = t0g - (0 if t0g in SEQ_STARTS else 1)
                hi = t1g + (0 if t1g in SEQ_ENDS else 1)
                segs = []
                for (s0, s1) in SEQS:
                    a, b = max(s0, t0g), min(s1, t1g)
                    if a < b:
                        segs.append((a, b, s0, s1))
                blocks = split_blocks(lo, hi, 321)

                def ffn_tail(jq):
                    av, avk = Ab[jq % 2][0]
                    ag, agk = Ab[jq % 2][1]
                    P.op("act", lambda e: e.activation(out=Bg, in_=ag, func=AF.Silu), R=[(agk,)], W=[(Bgk,)])
                    P.op("pool", lambda e: e.tensor_tensor(out=act[:, jq, :], in0=av, in1=Bg, op=ALU.mult), R=[(avk,), (Bgk,)], W=[(actk, jq)])

                for jj in range(NJ):
                    ui = g * NJ + jj
                    if adaF is not None:
                        if ui % 3 == 0 and ui // 3 < 24:
                            mod_dma(l + 1, ui // 3, adaF[(ui // 3) % 2])
                        if ui % 3 == 2 and ui // 3 < 24:
                            mod_compute(l + 1, ui // 3, adaF[(ui // 3) % 2], 6)
                    up_prefetch(ui + 2)
                    if jj == NJ - 3:
                        dn_prefetch(g * 8)
                    wu, wuk = wup[ui % 3]
                    for vg in range(2):
                        ub, ubk = usb[jj % 2][vg]
                        ab, abk = Ab[jj % 2][vg]
                        chn = vg * NJ + jj
                        for (t0, w) in blocks:
                            bank = pbi[0] % 6
                            pbi[0] += 1
                            for k in range(8):
                                P.op("pe", lambda e, wu=wu, vg=vg, k=k, t0=t0, w=w, bank=bank: e.matmul(psum[:, bank, 0:w], lhsT=wu[:, vg, k, :], rhs=hG[:, k, t0 - lo:t0 - lo + w],
                                     start=(k == 0), stop=(k == 7)), R=[(wuk,), (hGk, k)], W=[PS(bank)])
                            P.op("act", lambda e, ub=ub, t0=t0, w=w, bank=bank: e.activation(out=ub[:, t0 - lo:t0 - lo + w], in_=psum[:, bank, 0:w], func=AF.Identity),
                                 R=[PS(bank)], W=[(ubk, t0)])
                            a0, a1 = max(t0, t0g), min(t0 + w, t1g)
                            if a0 < a1 and vg == 1:
                                P.op("act", lambda e, ab=ab, a0=a0, a1=a1, t0=t0, bank=bank, chn=chn: e.activation(out=ab[:, a0 - t0g:a1 - t0g], in_=psum[:, bank, a0 - t0:a1 - t0],
                                     func=AF.Identity, scale=convw[:, l, chn, 1:2], bias=convw[:, l, chn, 3:4]), R=[PS(bank), ("convw",)], W=[(abk, a0)])
                        if vg == 0:
                            P.op("pool", lambda e, ab=ab, ub=ub, chn=chn: e.tensor_scalar(out=ab[:, 0:GW], in0=ub[:, t0g - lo:t0g - lo + GW],
                                 scalar1=convw[:, l, chn, 1:2], scalar2=convw[:, l, chn, 3:4], op0=ALU.mult, op1=ALU.add), R=[(ubk,), ("convw",)], W=[(abk,)])
                        segs_ = segs
                        if g == 0 and segs[0][:2] == (0, 256) and segs[1][:2] == (256, 512):
                            A3 = ab[:, 0:512].rearrange("p (s c) -> p s c", s=2)
                            U3 = ub[:, 0:512].rearrange("p (s c) -> p s c", s=2)
                            P.op("dve", lambda e, A3=A3, U3=U3, chn=chn: e.scalar_tensor_tensor(out=A3[:, :, 1:256], in0=U3[:, :, 0:255],
                                 scalar=convw[:, l, chn, 0:1], in1=A3[:, :, 1:256], op0=ALU.mult, op1=ALU.add), R=[(ubk,), (abk,), ("convw",)], W=[(abk,)])
                            P.op("dve", lambda e, A3=A3, U3=U3, chn=chn: e.scalar_tensor_tensor(out=A3[:, :, 0:255], in0=U3[:, :, 1:256],
                                 scalar=convw[:, l, chn, 2:3], in1=A3[:, :, 0:255], op0=ALU.mult, op1=ALU.add), R=[(ubk,), (abk,), ("convw",)], W=[(abk,)])
                            segs_ = segs[2:]
                        for (a, b, s0, s1) in segs_:
                            al = a + 1 if a == s0 else a
                            if al < b:
                                P.op("dve", lambda e, ab=ab, ub=ub, al=al, b=b, chn=chn: e.scalar_tensor_tensor(out=ab[:, al - t0g:b - t0g], in0=ub[:, al - 1 - lo:b - 1 - lo],
                                     scalar=convw[:, l, chn, 0:1], in1=ab[:, al - t0g:b - t0g], op0=ALU.mult, op1=ALU.add), R=[(ubk,), (abk,), ("convw",)], W=[(abk,)])
                            br = b - 1 if b == s1 else b
                            if a < br:
                                P.op("dve", lambda e, ab=ab, ub=ub, a=a, br=br, chn=chn: e.scalar_tensor_tensor(out=ab[:, a - t0g:br - t0g], in0=ub[:, a + 1 - lo:br + 1 - lo],
                                     scalar=convw[:, l, chn, 2:3], in1=ab[:, a - t0g:br - t0g], op0=ALU.mult, op1=ALU.add), R=[(ubk,), (abk,), ("convw",)], W=[(abk,)])
                    if jj >= 1:
                        ffn_tail(jj - 1)
                ffn_tail(NJ - 1)
                if g + 1 < NG:
                    ffn_norm(g + 1)
                for d in range(8):
                    di = g * 8 + d
                    dn_prefetch(di + 1)
                    wd_, wdk = wdn[di % 2]
                    for (t0, w) in split_blocks(t0g, t1g, 320):
                        bank = pbi[0] % 6
                        pbi[0] += 1
                        for jj in range(NJ):
                            P.op("pe", lambda e, wd_=wd_, jj=jj, t0=t0, w=w, bank=bank: e.matmul(psum[:, bank, 0:w], lhsT=wd_[:, jj, :], rhs=act[:, jj, t0 - t0g:t0 - t0g + w],
                                 start=(jj == 0), stop=(jj == NJ - 1)), R=[(wdk,), (actk, jj)], W=[PS(bank)])
                        v = vof(t0)
                        xk = [("xT", d, tt) for tt in range(t0 // 128, (t0 + w + 127) // 128)]
                        P.op("dve", lambda e, d=d, t0=t0, w=w, bank=bank, v=v: e.scalar_tensor_tensor(out=xT[:, d, t0:t0 + w], in0=psum[:, bank, 0:w],
                             scalar=modT[:, l, 40 + d, v:v + 1], in1=xT[:, d, t0:t0 + w], op0=ALU.mult, op1=ALU.add), R=[PS(bank), ("modT", l)] + xk, W=xk)

            ffn_norm(0)
            for g_ in range(NG):
                ffn_group(g_)
            if adaF is not None:
                mod_finish(l + 1)
            P.barrier()

        ones_f = sb("ones_f", (128, 128), F32)
        P.op("pool", lambda e: e.memset(ones_f[:], 1.0), W=[("ones_f",)])

        phases = []
        for l in range(DEPTH):
            phases.append(("mix", l))
            phases.append(("ffn", l))
        nph = len(phases) if stop is None else stop
        for (kind, l) in phases[:nph]:
            if kind == "ffn":
                ffn_layer(l)
            elif l % 2 == 0:
                ab_layer(l)
            else:
                gqa_layer(l)
        A.reset()
        yst = [A.alloc("yst%d" % i, (8, 512), F32) for i in range(2)]
        nt = norm_tmp(7)
        for bi, (t0, w) in enumerate(split_blocks(0, NT)):
            ys, ysk = yst[bi % 2]
            norm_block(8, 0, 0, t0, w, ys, 0, ysk, nt, final=True)
            P.dma("sp", O["yT"].rearrange("(k p) t -> p k t", p=128)[:, :, t0:t0 + w], ys[:, :, 0:w], R=[(ysk,)])
        P.emit()
    return nc


def _consts():
    c = np.zeros((128, 5, 128), np.float32)
    c[:, 0, :] = np.eye(128, dtype=np.float32)
    c[:, 1, :] = 1.0
    c[0:64, 2, 0:64] = 1.0
    c[64:128, 2, 64:128] = 1.0
    for m in range(128):
        r = m % 32
        k = m + 16 if r < 16 else m - 16
        c[k, 3, m] = 1.0
    jj = np.arange(128)[:, None]
    ii = np.arange(128)[None, :]
    mk = np.zeros((128, 2, 4, 128), np.float32)
    mk[:, 0] = (jj <= ii).astype(np.float32)[:, None, :]
    mk[:, 1] = (jj >= ii).astype(np.float32)[:, None, :]
    p = np.arange(128)
    dd = p % 64
    axis = (dd >= 32).astype(np.int64)
    r = dd % 32
    fi = r % 16
    inv = (np.float32(10000.0) ** (-(np.arange(0, 32, 2, dtype=np.float32)) / np.float32(32))).astype(np.float32)
    t = np.arange(2048)
    pos = np.where(axis[:, None] == 0, (t // 64)[None, :], (t % 64)[None, :]).astype(np.float32)
    ang = (pos * inv[fi][:, None]).astype(np.float32)
    cosT = np.cos(ang).astype(np.float32)
    sinT = np.sin(ang).astype(np.float32)
    sinT = np.where((r < 16)[:, None], -sinT, sinT).astype(np.float32)
    return c, mk, cosT, sinT


def _t2_index():
    p = np.arange(128)
    kb, kc = p // 64, p % 64
    qq = np.arange(128)
    qrow, qc = qq // 64, qq % 64
    e = np.arange(7)
    d = 2 * e + 1
    dr = d[None, :, None] + kb[:, None, None] - qrow[None, None, :]
    dc = kc[:, None] - qc[None, :] + 15
    cs = np.clip(qc - 8, 0, 48)
    ok = (kc[:, None] >= cs[None, :]) & (kc[:, None] < cs[None, :] + 16)
    dcc = np.clip(dc, 0, 30)
    dr_b = np.broadcast_to(dr, (128, 7, 128))
    dc_b = np.broadcast_to(dcc[:, None, :], (128, 7, 128))
    ok_b = np.broadcast_to(ok[:, None, :], (128, 7, 128))
    return dr_b, dc_b, ok_b


def _chunkw(w, ncols_per_chunk=128):
    K, C = w.shape
    return np.ascontiguousarray(w.reshape(K // 128, 128, C // ncols_per_chunk, ncols_per_chunk).transpose(2, 1, 0, 3))


def _prep_shared(inp):
    f = lambda a: np.asarray(a, dtype=np.float32)
    S = {}
    ada_w = f(inp["ada_w"])
    S["adaW"] = np.ascontiguousarray(ada_w.reshape(4, 8, 128, 24, 256).transpose(0, 3, 2, 1, 4))
    ada_b = f(inp["ada_b"])
    ab = ada_b.reshape(4, 48, 128).transpose(2, 0, 1)
    S["adaB"] = np.ascontiguousarray(np.repeat(ab[:, :, :, None], 2, axis=3))
    S["gmix"] = np.ascontiguousarray(f(inp["norm_mix_g"]).reshape(4, 8, 128).transpose(2, 0, 1))
    S["gffn"] = np.ascontiguousarray(f(inp["norm_ffn_g"]).reshape(4, 8, 128).transpose(2, 0, 1))
    S["gfin"] = np.ascontiguousarray(f(inp["final_norm_g"]).reshape(8, 128).T)
    w_in = f(inp["ab_w_in"])
    wA = np.zeros((2, 3, 4, 128, 8, 128), np.float32)
    wG = np.zeros((2, 2, 128, 8, 800), np.float32)
    for l in range(2):
        for wi in range(3):
            wA[l, wi] = _chunkw(w_in[l][:, wi * 512:(wi + 1) * 512])
        W = w_in[l].reshape(8, 128, 3104).transpose(1, 0, 2)
        for pc in range(2):
            wG[l, pc, :, :, 0:128] = W[:, :, 1536 + pc * 128:1536 + (pc + 1) * 128]
            wG[l, pc, :, :, 128:256] = W[:, :, 1792 + pc * 128:1792 + (pc + 1) * 128]
            wG[l, pc, :, :, 256:512] = W[:, :, 2048 + pc * 256:2048 + (pc + 1) * 256]
            wG[l, pc, :, :, 512:768] = W[:, :, 2560 + pc * 256:2560 + (pc + 1) * 256]
            wG[l, pc, :, :, 768:800] = W[:, :, 3072:3104]
    S["wA"], S["wG"] = wA, wG
    w2x = np.zeros((2, 128, 2, 256), np.float32)
    w2x[:, 96:112, 0, :] = f(inp["gla_w2_fwd"])
    w2x[:, 112:128, 1, :] = f(inp["gla_w2_bwd"])
    S["w2x"] = w2x
    b2 = np.stack([f(inp["gla_b2_fwd"]), f(inp["gla_b2_bwd"])], axis=1)
    S["b2x"] = np.ascontiguousarray(b2.reshape(2, 2, 2, 128).transpose(3, 0, 1, 2))
    S["glag"] = np.ascontiguousarray(f(inp["gla_norm_g"]).reshape(2, 4, 128).transpose(2, 0, 1))
    S["wOab"] = np.stack([_chunkw(f(inp["ab_w_out"])[l]) for l in range(2)])
    rpb = f(inp["na_rpb"])
    dr_b, dc_b, ok_b = _t2_index()
    T2 = np.zeros((2, 4, 128, 2, 7, 128), np.float32)
    T2i = np.zeros((2, 4, 128, 2, 7, 128), np.float32)
    for l in range(2):
        for h in range(8):
            g = rpb[l, h][dr_b, dc_b]
            T2[l, h // 2, :, h % 2] = np.where(ok_b, g, np.float32(-30000.0))
            T2i[l, h // 2, :, h % 2] = np.where(ok_b & (dr_b >= 3) & (dr_b <= 10), g, np.float32(-30000.0))
    S["T2"] = T2
    S["T2i"] = T2i
    gq = f(inp["gqa_w_in"])
    S["wQ"] = np.stack([_chunkw(gq[l][:, 0:1024]) for l in range(2)])
    wK = np.zeros((2, 4, 128, 8, 128), np.float32)
    wV = np.zeros((2, 4, 128, 8, 64), np.float32)
    for l in range(2):
        W = gq[l].reshape(8, 128, 1536).transpose(1, 0, 2)
        for g in range(4):
            kcols = W[:, :, 1024 + g * 64:1024 + (g + 1) * 64]
            wK[l, g, :, :, 0:64] = kcols
            wK[l, g, :, :, 64:128] = kcols
            wV[l, g] = W[:, :, 1280 + g * 64:1280 + (g + 1) * 64]
    S["wK"], S["wV"] = wK, wV
    S["wOc"] = np.stack([_chunkw(f(inp["gqa_w_out"])[l]) for l in range(2)])
    qn = f(inp["gqa_q_norm_g"])
    kn = f(inp["gqa_k_norm_g"])
    S["qng"] = np.ascontiguousarray(np.concatenate([qn, qn], axis=1).T)
    S["kng"] = np.ascontiguousarray(np.concatenate([kn, kn], axis=1).T)
    up = f(inp["ffn_w_up"])
    wUp = np.zeros((4, NJ, 2, 128, 8, 128), np.float32)
    for l in range(4):
        c = _chunkw(up[l])
        wUp[l, :, 0] = c[0:NJ]
        wUp[l, :, 1] = c[NJ:2 * NJ]
    S["wUp"] = wUp
    dn = f(inp["ffn_w_down"])
    S["wDn"] = np.ascontiguousarray(dn.reshape(4, NJ, 128, 8, 128).transpose(0, 3, 2, 1, 4))
    cw = f(inp["ffn_conv_w"])
    cbias = f(inp["ffn_conv_b"])
    cvw = np.zeros((128, 4, 44, 4), np.float32)
    for t in range(3):
        cvw[:, :, :, t] = cw[:, t, :].reshape(4, 44, 128).transpose(2, 0, 1)
    cvw[:, :, :, 3] = cbias.reshape(4, 44, 128).transpose(2, 0, 1)
    S["convw"] = cvw
    c, mk, cosT, sinT = _consts()
    S["consts"], S["masks"], S["cosT"], S["sinT"] = c, mk, cosT, sinT
    return S


def _prep_core(inp, i):
    f = lambda a: np.asarray(a, dtype=np.float32)
    M = {}
    xp, xs = f(inp["x_prompt"]), f(inp["x_sample"])
    x = np.concatenate([xp[2 * i], xp[2 * i + 1], xs[i]], axis=0)
    M["xT"] = np.ascontiguousarray(x.T)
    cv = np.stack([f(inp["c_ctx"]), f(inp["c"])[i]], axis=1)
    M["cv"] = np.ascontiguousarray(cv.reshape(8, 128, 2).transpose(1, 0, 2))
    nk, nv = f(inp["cache_na_k"])[i], f(inp["cache_na_v"])[i]
    M["cnk"] = np.ascontiguousarray(nk.reshape(2, 4, 2, 256, 64).transpose(0, 1, 2, 4, 3).reshape(2, 4, 128, 256))
    M["cnv"] = np.ascontiguousarray(nv.reshape(2, 4, 2, 2, 128, 64).transpose(0, 1, 4, 3, 2, 5))
    sf, sb_ = f(inp["state_gla_fwd"])[i], f(inp["state_gla_bwd"])[i]
    lay = lambda s: np.ascontiguousarray(s.reshape(2, 2, 2, 64, 128).transpose(0, 2, 3, 1, 4).reshape(2, 128, 2, 128))
    M["sgf"], M["sgb"] = lay(sf), lay(sb_)
    gk, gv = f(inp["cache_gqa_k"])[i], f(inp["cache_gqa_v"])[i]
    gkT = gk.transpose(0, 1, 3, 2)
    M["cgk"] = np.ascontiguousarray(np.concatenate([gkT, gkT], axis=2))
    M["cgv"] = np.ascontiguousarray(gv.reshape(2, 4, 2, 128, 64).transpose(0, 1, 3, 2, 4))
    return M


_NC_CACHE = {}


def kernel(**inputs):
    if "nc" not in _NC_CACHE:
        _NC_CACHE["nc"] = build()
    nc = _NC_CACHE["nc"]
    S = _prep_shared(inputs)
    in_maps = []
    for i in range(8):
        m = dict(S)
        m.update(_prep_core(inputs, i))
        in_maps.append(m)
    res = run_bass_kernel_spmd(nc, in_maps, core_ids=list(range(8)))
    R = res.results
    y_prompt = np.zeros((16, 256, 1024), np.float32)
    y_sample = np.zeros((8, 2048, 1024), np.float32)
    na_k = np.zeros((16, 2, 8, 256, 64), np.float32)
    na_v = np.zeros((16, 2, 8, 256, 64), np.float32)
    g_f = np.zeros((16, 2, 4, 64, 128), np.float32)
    g_b = np.zeros((16, 2, 4, 64, 128), np.float32)
    q_k = np.zeros((16, 2, 4, 256, 64), np.float32)
    q_v = np.zeros((16, 2, 4, 256, 64), np.float32)
    for i in range(8):
        r = R[i]
        y = np.asarray(r["yT"]).T
        y_prompt[2 * i] = y[0:256]
        y_prompt[2 * i + 1] = y[256:512]
        y_sample[i] = y[512:]
        nak = np.asarray(r["o_nak"]).reshape(2, 4, 2, 64, 2, 256)
        na_k[2 * i:2 * i + 2] = nak.transpose(4, 0, 1, 2, 5, 3).reshape(2, 2, 8, 256, 64)
        nav = np.asarray(r["o_nav"]).reshape(2, 4, 2, 2, 128, 2, 64)
        na_v[2 * i:2 * i + 2] = nav.transpose(2, 0, 1, 5, 3, 4, 6).reshape(2, 2, 8, 256, 64)
        for nm, dst in (("o_gf", g_f), ("o_gb", g_b)):
            a = np.asarray(r[nm]).reshape(2, 2, 2, 64, 2, 128)
            dst[2 * i:2 * i + 2] = a.transpose(1, 0, 4, 2, 3, 5).reshape(2, 2, 4, 64, 128)
        gk = np.asarray(r["o_gk"]).reshape(2, 4, 64, 2, 256)
        q_k[2 * i:2 * i + 2] = gk.transpose(3, 0, 1, 4, 2)
        gv = np.asarray(r["o_gv"]).reshape(2, 4, 2, 2, 128, 64)
        q_v[2 * i:2 * i + 2] = gv.transpose(2, 0, 1, 3, 4, 5).reshape(2, 2, 4, 256, 64)
    return (y_prompt, y_sample, na_k, na_v, g_f, g_b, q_k, q_v)
```

```python
import concourse.bass as bass
import concourse.mybir as mybir
from contextlib import ExitStack

F32 = mybir.dt.float32
BF16 = mybir.dt.bfloat16
AF = mybir.ActivationFunctionType
ALU = mybir.AluOpType

ENGS = ("pe", "act", "dve", "pool", "sp")
DMA_RING = 8


class _Op:
    __slots__ = ("eng", "fn", "deps", "dma", "signal", "seq", "ring", "ringval", "idx")

    def __init__(self, eng, fn, dma):
        self.eng = eng
        self.fn = fn
        self.dma = dma
        self.deps = []
        self.signal = False
        self.seq = 0
        self.ring = None
        self.ringval = 0


class Prog:
    def __init__(self, nc):
        self.nc = nc
        self.ops = []
        self.state = {}
        self.ndma = {e: 0 for e in ENGS}
        self.last_op = {e: None for e in ENGS}
        self.dma_ops = {e: [] for e in ENGS}
        self._cap = None

    def capture(self, thunk):
        assert self._cap is None
        self._cap = []
        thunk()
        lst, self._cap = self._cap, None
        return lst

    @staticmethod
    def merge(la, lb):
        out = []
        ia = ib = 0
        while ia < len(la) or ib < len(lb):
            fa = ia / max(1, len(la))
            fb = ib / max(1, len(lb))
            if ib >= len(lb) or (ia < len(la) and fa <= fb):
                out.append(la[ia])
                ia += 1
            else:
                out.append(lb[ib])
                ib += 1
        return out

    def replay(self, lst):
        for it in lst:
            self.op(*it)

    def replay_merged(self, la, lb):
        self.replay(self.merge(la, lb))

    @staticmethod
    def _conf(a, b):
        n = min(len(a), len(b))
        return a[:n] == b[:n]

    def _collect(self, op, keys, is_write):
        for k in keys:
            name, sub = k[0], tuple(k[1:])
            tab = self.state.setdefault(name, {})
            for s2, st in tab.items():
                if self._conf(sub, s2):
                    if st[0] is not None:
                        op.deps.append(st[0])
                    if is_write:
                        op.deps.extend(st[1])

    def _update(self, op, reads, writes):
        for k in reads:
            name, sub = k[0], tuple(k[1:])
            tab = self.state[name]
            st = tab.setdefault(sub, [None, []])
            st[1].append(op)
        for k in writes:
            name, sub = k[0], tuple(k[1:])
            tab = self.state[name]
            for s2 in [s for s in tab if len(s) >= len(sub) and s[:len(sub)] == sub]:
                del tab[s2]
            tab[sub] = [op, []]

    def op(self, eng, fn, R=(), W=(), dma=False, extra=()):
        if self._cap is not None:
            self._cap.append((eng, fn, list(R), list(W), dma, list(extra)))
            return None
        o = _Op(eng, fn, dma)
        o.idx = len(self.ops)
        W = list(W) + [k for k in R if k[0] == "ps"]
        R = [k for k in R if k[0] != "ps"]
        self._collect(o, R, False)
        self._collect(o, W, True)
        o.deps.extend(extra)
        self._update(o, R, W)
        if dma:
            j = self.ndma[eng]
            self.ndma[eng] += 1
            o.ring = j % DMA_RING
            o.ringval = 16 * (j // DMA_RING + 1)
            if j >= DMA_RING:
                o.deps.append(self.dma_ops[eng][j - DMA_RING])
            self.dma_ops[eng].append(o)
        self.ops.append(o)
        self.last_op[eng] = o
        return o

    def dma(self, eng, out, in_, R=(), W=()):
        return self.op(eng, lambda e: e.dma_start(out=out, in_=in_), R, W, dma=True)

    def barrier(self):
        pend = [o for o in self.last_op.values() if o is not None]
        for e in ENGS:
            pend.extend(self.dma_ops[e][-DMA_RING:])
        for e in ENGS:
            if e == "sp" and False:
                continue
            self.op(e, None, extra=pend)

    def emit(self):
        nc = self.nc
        for o in self.ops:
            for d in o.deps:
                if not d.dma:
                    if d.eng == "pe" and o.eng == "pe" and not o.dma:
                        continue
                    d.signal = True
        cnt = {e: 0 for e in ENGS}
        for o in self.ops:
            if o.dma:
                continue
            if o.fn is None:
                continue
            if o.signal:
                cnt[o.eng] += 1
                o.seq = cnt[o.eng]
        with ExitStack() as es:
            esem = {e: es.enter_context(nc.semaphore("s_" + e)) for e in ENGS}
            rsem = {e: [es.enter_context(nc.semaphore("r_%s%d" % (e, i))) for i in range(DMA_RING)]
                    for e in ENGS if self.ndma[e] > 0}
            block = es.enter_context(nc.Block())
            per = {e: [o for o in self.ops if o.eng == e] for e in ENGS}
            handles = {"pe": "tensor", "act": "scalar", "dve": "vector", "pool": "gpsimd", "sp": "sync"}

            def make(e):
                def body(eng):
                    waited = {}
                    for o in per[e]:
                        for d in o.deps:
                            if d.dma:
                                key = ("r", d.eng, d.ring)
                                sem, val = rsem[d.eng][d.ring], d.ringval
                            else:
                                if d.eng == "pe" and e == "pe" and not o.dma:
                                    continue
                                if d.fn is None:
                                    continue
                                key = ("e", d.eng)
                                sem, val = esem[d.eng], d.seq
                            if waited.get(key, 0) >= val:
                                continue
                            waited[key] = val
                            eng.wait_ge(sem, val)
                        if o.fn is None:
                            continue
                        ins = o.fn(eng)
                        if o.dma:
                            ins.then_inc(rsem[e][o.ring], 16)
                        elif o.signal:
                            ins.then_inc(esem[e], 1)
                    for d in self.dma_ops[e][-DMA_RING:]:
                        key = ("r", e, d.ring)
                        if waited.get(key, 0) >= d.ringval:
                            continue
                        waited[key] = d.ringval
                        eng.wait_ge(rsem[e][d.ring], d.ringval)
                return body

            for e in ENGS:
                if per[e]:
                    getattr(block, handles[e])(make(e))


import numpy as np
import ml_dtypes
from concourse.bass_utils import run_bass_kernel_spmd

D = 1024
NT = 2560
NTILE = 20
DEPTH = 4
DFF = 2816
NJ = 22
EPS = 1e-6
SEQS = [(0, 256), (256, 512), (512, 2560)]
SEQ_STARTS = {0, 256, 512}
SEQ_ENDS = {256, 512, 2560}
ARENA_BYTES = 112 * 1024


def _prod(s):
    r = 1
    for v in s:
        r *= v
    return r


class Arena:
    def __init__(self, t, nbytes):
        self.t = t
        self.cap = nbytes
        self.off = 0
        self.gen = 0

    def reset(self):
        self.off = 0
        self.gen += 1

    def alloc(self, name, shape, dt):
        esz = 2 if dt == BF16 else 4
        nb = _prod(shape) * esz
        nb_al = (nb + 31) // 32 * 32
        assert self.off + nb_al <= self.cap, (name, self.off, nb_al, self.cap)
        ap = self.t[:, self.off // 2:(self.off + nb) // 2]
        self.off += nb_al
        if dt == F32:
            ap = ap.bitcast(F32)
        if len(shape) == 2:
            ap = ap.rearrange("p (a b) -> p a b", a=shape[0])
        elif len(shape) == 3:
            ap = ap.rearrange("p (a b c) -> p a b c", a=shape[0], b=shape[1])
        elif len(shape) == 4:
            ap = ap.rearrange("p (a b c d) -> p a b c d", a=shape[0], b=shape[1], c=shape[2])
        return ap, "%s@%d" % (name, self.gen)


def vof(t):
    return 0 if t < 512 else 1


def split_blocks(t0, t1, maxw=512):
    out = []
    a = t0
    while a < t1:
        b = min(t1, a + maxw)
        if a < 512 < b:
            b = 512
        out.append((a, b - a))
        a = b
    return out


def build(stop=None, sub=None):
    nc = bass.Bass("TRN2", target_bir_lowering=False)
    din = lambda n, s: nc.dram_tensor(n, list(s), F32, kind="ExternalInput").ap()
    dout = lambda n, s: nc.dram_tensor(n, list(s), F32, kind="ExternalOutput").ap()
    I = {}
    for n, s in [("xT", (D, NT)), ("cv", (128, 8, 2)), ("adaW", (4, 24, 128, 8, 256)), ("adaB", (128, 4, 48, 2)),
                 ("gmix", (128, 4, 8)), ("gffn", (128, 4, 8)), ("gfin", (128, 8)),
                 ("wA", (2, 3, 4, 128, 8, 128)), ("wG", (2, 2, 128, 8, 800)), ("w2x", (2, 128, 2, 256)), ("b2x", (128, 2, 2, 2)),
                 ("glag", (128, 2, 4)), ("wOab", (2, 8, 128, 8, 128)), ("T2", (2, 4, 128, 2, 7, 128)), ("T2i", (2, 4, 128, 2, 7, 128)),
                 ("cnk", (2, 4, 128, 256)), ("cnv", (2, 4, 128, 2, 2, 64)), ("sgf", (2, 128, 2, 128)), ("sgb", (2, 128, 2, 128)),
                 ("wQ", (2, 8, 128, 8, 128)), ("wK", (2, 4, 128, 8, 128)), ("wV", (2, 4, 128, 8, 64)), ("wOc", (2, 8, 128, 8, 128)),
                 ("qng", (128, 2)), ("kng", (128, 2)), ("cgk", (2, 4, 128, 256)), ("cgv", (2, 4, 128, 2, 64)),
                 ("cosT", (128, 2048)), ("sinT", (128, 2048)),
                 ("wUp", (4, NJ, 2, 128, 8, 128)), ("wDn", (4, 8, 128, NJ, 128)), ("convw", (128, 4, 44, 4)),
                 ("consts", (128, 5, 128)), ("masks", (128, 2, 4, 128))]:
        I[n] = din(n, s)
    O = {}
    for n, s in [("yT", (D, NT)), ("o_nak", (2, 4, 128, 512)), ("o_nav", (2, 4, 4, 128, 128)),
                 ("o_gf", (2, 2, 128, 2, 128)), ("o_gb", (2, 2, 128, 2, 128)),
                 ("o_gk", (2, 4, 64, 512)), ("o_gv", (2, 4, 4, 128, 64))]:
        O[n] = dout(n, s)

    with ExitStack() as es:
        sb = lambda n, s, d: es.enter_context(nc.sbuf_tensor(n, list(s), d))
        xT = sb("xT_sb", (128, 8, NT), F32)
        arena_t = sb("arena", (128, ARENA_BYTES // 2), BF16)
        cbf = sb("cbf", (128, 5, 128), BF16)
        mbf = sb("mbf", (128, 2, 4, 128), BF16)
        modT = sb("modT", (128, 4, 48, 2), F32)
        adab = sb("adab", (128, 4, 48, 2), F32)
        gmix = sb("gmix_sb", (128, 4, 8), F32)
        gffn = sb("gffn_sb", (128, 4, 8), F32)
        gfin = sb("gfin_sb", (128, 8), F32)
        gsc = sb("gsc", (128, 9, 8, 2), F32)
        cvs = sb("cvs", (128, 8, 2), F32)
        cvb = sb("cvb", (128, 8, 2), BF16)
        convw = sb("convw_sb", (128, 4, 44, 4), F32)
        glag = sb("glag_sb", (128, 2, 4), F32)
        qng = sb("qng_sb", (128, 2), F32)
        kng = sb("kng_sb", (128, 2), F32)
        w2x = sb("w2x_sb", (128, 2, 2, 256), BF16)
        negb = sb("negb", (128, 2, 2, 2), F32)
        zero1 = sb("zero1", (128, 1), F32)
        psum = es.enter_context(nc.psum_tensor("psum", [128, 8, 512], F32))
        P = Prog(nc)
        A = Arena(arena_t, ARENA_BYTES)
        ident = cbf[:, 0, :]
        ones = cbf[:, 1, :]
        blockones = cbf[:, 2, :]
        rperm = cbf[:, 3, :]

        def PS(b):
            return ("ps", b)

        P.dma("sp", xT[:], I["xT"].rearrange("(k p) t -> p k t", p=128), W=[("xT",)])
        P.dma("pool", cbf[:], I["consts"], W=[("cbf",)])
        P.dma("pool", mbf[:], I["masks"], W=[("mbf",)])
        P.dma("sp", adab[:], I["adaB"], W=[("adab",)])
        P.dma("sp", gmix[:], I["gmix"], W=[("gmix",)])
        P.dma("sp", gffn[:], I["gffn"], W=[("gffn",)])
        P.dma("sp", gfin[:], I["gfin"], W=[("gfin",)])
        P.dma("sp", cvs[:], I["cv"], W=[("cvs",)])
        P.dma("sp", convw[:], I["convw"], W=[("convw",)])
        P.dma("sp", glag[:], I["glag"], W=[("glag",)])
        P.dma("sp", qng[:], I["qng"], W=[("qng",)])
        P.dma("sp", kng[:], I["kng"], W=[("kng",)])
        P.dma("pool", w2x[:], I["w2x"].rearrange("l r d c -> r l d c"), W=[("w2x",)])
        P.dma("sp", negb[:], I["b2x"], W=[("negb",)])
        P.op("dve", lambda e: e.tensor_scalar(out=negb[:], in0=negb[:], scalar1=-1.0, scalar2=None, op0=ALU.mult), R=[("negb",)], W=[("negb",)])
        P.op("pool", lambda e: e.memset(zero1[:], 0.0), W=[("zero1",)])
        P.op("act", lambda e: e.activation(out=cvb[:], in_=cvs[:], func=AF.Silu), R=[("cvs",)], W=[("cvb",)])

        def mod_dma(l, pc, buf):
            P.dma("pool", buf[0], I["adaW"][l, pc], W=[(buf[1],)])

        def mod_compute(l, pc, buf, bank):
            ab, abk = buf
            for fi in range(2):
                for k in range(8):
                    P.op("pe", lambda e, fi=fi, k=k: e.matmul(
                        psum[:, bank, fi * 2:fi * 2 + 2], lhsT=ab[:, k, fi * 128:(fi + 1) * 128], rhs=cvb[:, k, :],
                        start=(k == 0), stop=(k == 7)), R=[(abk,), ("cvb",)], W=[PS(bank)])
            f0 = pc * 2
            P.op("dve", lambda e: e.tensor_tensor(
                out=modT[:, l, f0:f0 + 2, :], in0=psum[:, bank, 0:4].rearrange("p (a b) -> p a b", a=2),
                in1=adab[:, l, f0:f0 + 2, :], op=ALU.add), R=[PS(bank), ("adab",)], W=[("modT", l, pc)])

        def mod_finish(l):
            for which in range(2):
                g = gmix if which == 0 else gffn
                sc0 = 8 + 24 * which
                for v in range(2):
                    P.op("dve", lambda e, which=which, g=g, sc0=sc0, v=v: e.scalar_tensor_tensor(
                        out=gsc[:, l * 2 + which, :, v], in0=modT[:, l, sc0:sc0 + 8, v], scalar=1.0, in1=g[:, l, :],
                        op0=ALU.add, op1=ALU.mult), R=[("modT", l), ("gmix",), ("gffn",)], W=[("gsc", l, which, v)])

        A.reset()
        NB_ADA = 6
        adabuf = [A.alloc("ada%d" % i, (8, 256), BF16) for i in range(NB_ADA)]
        for pc in range(24):
            mod_dma(0, pc, adabuf[pc % NB_ADA])
            mod_compute(0, pc, adabuf[pc % NB_ADA], pc % 2)
        mod_finish(0)
        for v in range(2):
            P.op("dve", lambda e, v=v: e.tensor_copy(out=gsc[:, 8, :, v], in_=gfin[:]), R=[("gfin",)], W=[("gsc", 8, 0, v)])
        P.barrier()

        def norm_block(nidx, l, which, t0, w, dst, dcol, dkey, tmp, final=False, xsrc=None, xkey=None):
            v = vof(t0)
            sqb, sqk = tmp["sq"]
            rs, rsk = tmp["rs"]
            tt, ttk = tmp["tt"]
            bank = tmp["bank"]
            if xsrc is None:
                xsrc = lambda k: xT[:, k, t0:t0 + w]
            xk_ = (lambda k: ("xT", k)) if xkey is None else (lambda k: xkey)
            for k in range(8):
                P.op("act", lambda e, k=k: e.activation(out=sqb[:, k % 2, 0:w], in_=xsrc(k), func=AF.Square),
                     R=[xk_(k)], W=[(sqk, k % 2)])
                P.op("pe", lambda e, k=k: e.matmul(psum[:, bank, 0:w], lhsT=ones, rhs=sqb[:, k % 2, 0:w], start=(k == 0), stop=(k == 7)),
                     R=[(sqk, k % 2), ("cbf",)], W=[PS(bank)])
            P.op("act", lambda e: e.activation(out=rs[:, 0:w], in_=psum[:, bank, 0:w], func=AF.Ln, scale=1.0 / D, bias=zero1_eps[:]),
                 R=[PS(bank), ("eps",)], W=[(rsk,)])
            P.op("act", lambda e: e.activation(out=rs[:, 0:w], in_=rs[:, 0:w], func=AF.Exp, scale=-0.5), R=[(rsk,)], W=[(rsk,)])
            for k in range(8):
                if final:
                    P.op("dve", lambda e, k=k: e.scalar_tensor_tensor(out=dst[:, k, dcol:dcol + w], in0=xsrc(k),
                         scalar=gsc[:, nidx, k, v:v + 1], in1=rs[:, 0:w], op0=ALU.mult, op1=ALU.mult),
                         R=[xk_(k), (rsk,), ("gsc",)], W=[(dkey, k)])
                else:
                    P.op("dve", lambda e, k=k: e.scalar_tensor_tensor(out=tt[:, k % 2, 0:w], in0=xsrc(k),
                         scalar=gsc[:, nidx, k, v:v + 1], in1=rs[:, 0:w], op0=ALU.mult, op1=ALU.mult),
                         R=[xk_(k), (rsk,), ("gsc",)], W=[(ttk, k % 2)])
                    sh = modT[:, l, 24 * which + k, v:v + 1]
                    P.op("act", lambda e, k=k, sh=sh: e.activation(out=dst[:, k, dcol:dcol + w], in_=tt[:, k % 2, 0:w], func=AF.Identity, bias=sh, scale=1.0),
                         R=[(ttk, k % 2), ("modT", l)], W=[(dkey, k)])

        eps_t = sb("eps_t", (128, 1), F32)
        eps64_t = sb("eps64_t", (128, 1), F32)
        zero1_eps = eps_t
        P.op("pool", lambda e: e.memset(eps_t[:], EPS), W=[("eps",)])

        def norm_tmp(bank):
            return {"sq": A.alloc("nsq", (2, 512), BF16), "rs": A.alloc("nrs", (512,), F32),
                    "tt": A.alloc("ntt", (2, 512), F32), "bank": bank}

        def out_proj_residual(l, wsrc_fn, attn, attn_key, nk, gate_f0):
            wb = [A.alloc("wo%d" % i, (nk, 128), BF16) for i in range(2)]
            bi = 0
            P.dma("pool", wb[0][0], wsrc_fn(0), W=[(wb[0][1],)])
            for d in range(8):
                w, wk = wb[d % 2]
                if d + 1 < 8:
                    P.dma("pool", wb[(d + 1) % 2][0], wsrc_fn(d + 1), W=[(wb[(d + 1) % 2][1],)])
                for (t0, wd) in split_blocks(0, NT):
                    bank = 6 + bi % 2
                    bi += 1
                    for k in range(nk):
                        P.op("pe", lambda e, w=w, k=k, t0=t0, wd=wd, bank=bank: e.matmul(psum[:, bank, 0:wd], lhsT=w[:, k, :], rhs=attn[:, k, t0:t0 + wd],
                             start=(k == 0), stop=(k == nk - 1)), R=[(wk,), (attn_key,)], W=[PS(bank)])
                    v = vof(t0)
                    xk = [("xT", d, tt) for tt in range(t0 // 128, (t0 + wd + 127) // 128)]
                    P.op("dve", lambda e, d=d, t0=t0, wd=wd, bank=bank, v=v: e.scalar_tensor_tensor(
                        out=xT[:, d, t0:t0 + wd], in0=psum[:, bank, 0:wd], scalar=modT[:, l, gate_f0 + d, v:v + 1], in1=xT[:, d, t0:t0 + wd],
                        op0=ALU.mult, op1=ALU.add), R=[PS(bank), ("modT", l)] + xk, W=xk)

        def softmax_finish(bank, w, dst_ap, dkeys, rr, rrk):
            P.op("dve", lambda e: e.reciprocal(out=rr[64:128, 0:w], in_=psum[64:128, bank, 0:w]), R=[PS(bank)], W=[(rrk,)])
            P.op("dve", lambda e: e.tensor_tensor(out=dst_ap, in0=psum[0:64, bank, 0:w], in1=rr[64:128, 0:w], op=ALU.mult),
                 R=[PS(bank), (rrk,)], W=dkeys)

        def prompt_attention(qT, qkey, kT, kkey, vtm, vkey, base, dst_fn, tmpP, rrs, cnt, qkeyf=None, k128=False, escale=0.125):
            for sq in range(2):
                c0 = sq * 256
                u = cnt[0]
                cnt[0] += 1
                sbank = [0 + 3 * (u % 2), 1 + 3 * (u % 2)]
                obank = 2 + 3 * (u % 2)
                pt, ptk = tmpP[u % 2]
                for kt in range(2):
                    kc = c0 + kt * 128
                    r0_, r1_ = (0, 128) if k128 else (base, base + 64)
                    P.op("pe", lambda e, kc=kc, c0=c0, b=sbank[kt], r0_=r0_, r1_=r1_: e.matmul(psum[:, b, 0:256], lhsT=kT[r0_:r1_, kc:kc + 128],
                         rhs=qT[r0_:r1_, c0:c0 + 256], start=True, stop=True), R=[(kkey,), ((qkey,) if qkeyf is None else qkeyf(c0))], W=[PS(sbank[kt])])
                    P.op("act", lambda e, kt=kt, b=sbank[kt], pt=pt: e.activation(out=pt[:, kt, 0:256], in_=psum[:, b, 0:256], func=AF.Exp, scale=escale),
                         R=[PS(sbank[kt])], W=[(ptk, kt)])
                for kt in range(2):
                    P.op("pe", lambda e, kt=kt, sq=sq, pt=pt, obank=obank: e.matmul(psum[:, obank, 0:256], lhsT=vtm(sq * 2 + kt), rhs=pt[:, kt, 0:256],
                         start=(kt == 0), stop=(kt == 1)), R=[(ptk, kt), (vkey,)], W=[PS(obank)])
                rr, rrk = rrs[u % 2]
                dst, dkeys = dst_fn(c0, 256)
                softmax_finish(obank, 256, dst, dkeys, rr, rrk)

        def ab_layer(l):
            j = l // 2
            A.reset()
            hT, hk = A.alloc("hT", (8, NT), BF16)
            mark = A.off
            nt = norm_tmp(7)
            for (t0, w) in split_blocks(0, NT):
                norm_block(l * 2, l, 0, t0, w, hT, t0, hk, nt)
            P.barrier()
            if sub is not None and sub <= 1:
                return
            tiles_of = lambda a_, b_: list(range(a_ // 128, b_ // 128))
            psb = psum[:, 7, :].bitcast(BF16)
            def gla_part(pc):
                A.off = mark
                A.gen += 1
                og, ogk = A.alloc("og", (2, NT), BF16)
                wg, wgk = A.alloc("wg", (8, 800), BF16)
                P.dma("pool", wg, I["wG"][j, pc], W=[(wgk,)])
                Sbs, Sbsk = A.alloc("Sbs", (NTILE, 128), BF16)
                S32 = [A.alloc("S32_%d" % d, (128,), F32) for d in range(2)]
                Sfb, Sfbk = A.alloc("Sfb", (128,), BF16)
                lrx, lrxk = A.alloc("lrx", (128,), BF16)
                spb, spk = A.alloc("sp", (2, 128), F32)
                cb, cbk = A.alloc("cc", (2, 128), F32)
                En, Enk = A.alloc("En", (2, 128), F32)
                Ep, Epk = A.alloc("Ep", (2, 128), F32)
                ktm, ktmk = A.alloc("ktm", (128,), BF16)
                m1, m1k = A.alloc("m1", (2, 128), F32)
                m2, m2k = A.alloc("m2", (2, 128), F32)
                att, attk = A.alloc("att", (2, 128), BF16)
                osq, osqk = A.alloc("osq", (256,), BF16)
                ors, orsk = A.alloc("ors", (256,), F32)
                ot, otk = A.alloc("ot", (256,), F32)
                stmp, stmpk = A.alloc("stmp", (128,), F32)
                stg, stgk = A.alloc("stg", (128,), F32)
                ebt2 = [A.alloc("eb%d" % p_, (2,), F32) for p_ in range(2)]
                kt2 = [A.alloc("kt%d" % p_, (2, 128), BF16) for p_ in range(2)]
                vtm2 = [A.alloc("vtm%d" % p_, (256,), BF16) for p_ in range(2)]
                srg2 = [A.alloc("srg%d" % p_, (2, 128), F32) for p_ in range(2)]
                qpad2 = [[[A.alloc("qp%d%d%d" % (p_, d_, i_), (128,), BF16) for i_ in range(2)] for d_ in range(2)] for p_ in range(2)]
                for p_ in range(2):
                    for d_ in range(2):
                        for i_ in range(2):
                            P.op("pool", lambda e, p_=p_, d_=d_, i_=i_: e.memset(qpad2[p_][d_][i_][0][(1 - i_) * 64:(2 - i_) * 64, :], 0.0), W=[("qpz", p_, d_, i_)])

                def gate_stage(c0, dirs, pp):
                    ebt, ebk = ebt2[pp]
                    for k in range(8):
                        P.op("pe", lambda e, k=k: e.matmul(psum[:, 1, 0:128], lhsT=wg[:, k, 672:800], rhs=hT[:, k, c0:c0 + 128],
                             start=(k == 0), stop=(k == 7)), R=[(wgk,), (hk, k)], W=[PS(1)])
                    P.op("act", lambda e: e.activation(out=lrx, in_=psum[:, 1, 0:128], func=AF.Identity), R=[PS(1)], W=[(lrxk,)])
                    for d in dirs:
                        P.op("pe", lambda e, d=d: e.matmul(psum[:, 2, d * 128:(d + 1) * 128], lhsT=w2x[:, j, d, pc * 128:(pc + 1) * 128],
                             rhs=lrx, start=True, stop=True), R=[(lrxk,), ("w2x",)], W=[PS(2)])
                    for d in dirs:
                        P.op("act", lambda e, d=d: e.activation(out=spb[:, d, :], in_=psum[:, 2, d * 128:(d + 1) * 128], func=AF.Exp, scale=-1.0, bias=negb[:, j, d, pc:pc + 1]),
                             R=[PS(2), ("negb",)], W=[(spk, d)])
                        P.op("act", lambda e, d=d: e.activation(out=spb[:, d, :], in_=spb[:, d, :], func=AF.Ln, bias=1.0, scale=1.0), R=[(spk, d)], W=[(spk, d)])
                        P.op("dve", lambda e, d=d: e.tensor_tensor_scan(out=cb[:, d, :], data0=ones_f[:, 0:128], data1=spb[:, d, :],
                             initial=0.0, op0=ALU.mult, op1=ALU.add), R=[(spk, d), ("ones_f",)], W=[(cbk, d)])
                        P.op("act", lambda e, d=d: e.activation(out=ebt[:, d:d + 1], in_=cb[:, d, 127:128], func=AF.Exp, scale=-1.0 / 16), R=[(cbk, d)], W=[(ebk, d)])
                        if d == 1:
                            P.op("dve", lambda e: e.scalar_tensor_tensor(out=cb[:, 1, :], in0=cb[:, 1, :], scalar=cb[:, 1, 127:128],
                                 in1=spb[:, 1, :], op0=ALU.subtract, op1=ALU.subtract), R=[(cbk, 1), (spk, 1)], W=[(cbk, 1)])

                def v_stage(c0, pp):
                    vtm, vtmk = vtm2[pp]
                    for k in range(8):
                        P.op("pe", lambda e, k=k: e.matmul(psum[:, 3, 0:256], lhsT=hT[:, k, c0:c0 + 128], rhs=wg[:, k, 256:512], start=(k == 0), stop=(k == 7)),
                             R=[(wgk,), (hk, k)], W=[PS(3)])
                    P.op("act", lambda e: e.activation(out=vtm, in_=psum[:, 3, 0:256], func=AF.Identity), R=[PS(3)], W=[(vtmk,)])

                def state_update(d, pp):
                    S, Sk = S32[d]
                    ebt, ebk = ebt2[pp]
                    kt, ktk = kt2[pp]
                    vtm, vtmk = vtm2[pp]
                    P.op("pe", lambda e: e.transpose(out=psb[:, 0:128], in_=kt[:, d, :], identity=ident), R=[(ktk, d), ("cbf",)], W=[PS(7)])
                    P.op("act", lambda e: e.activation(out=ktm, in_=psb[:, 0:128], func=AF.Identity), R=[PS(7)], W=[(ktmk,)])
                    P.op("pe", lambda e: e.matmul(psum[:, 5, 256:512], lhsT=ktm, rhs=vtm, start=True, stop=True), R=[(ktmk,), (vtmk,)], W=[PS(5)])
                    for i in range(2):
                        r0 = i * 64
                        P.op("dve", lambda e, i=i, r0=r0: e.tensor_tensor(out=stmp[r0:r0 + 64, :], in0=psum[r0:r0 + 64, 5, 256 + i * 128:256 + (i + 1) * 128],
                             in1=S[r0:r0 + 64, :], op=ALU.add), R=[PS(5), (Sk,)], W=[(stmpk, i)])
                        P.op("dve", lambda e, i=i, r0=r0: e.tensor_scalar(out=S[r0:r0 + 64, :], in0=stmp[r0:r0 + 64, :], scalar1=ebt[r0:r0 + 64, d:d + 1], scalar2=None,
                             op0=ALU.mult), R=[(stmpk, i), (ebk, d)], W=[(Sk,)])

                def state_init(d, si):
                    S, Sk = S32[d]
                    if si < 2:
                        P.op("pool", lambda e: e.memset(S, 0.0), W=[(Sk,)])
                    else:
                        P.dma("sp", S, I["sgb" if d == 1 else "sgf"][j, :, pc, :], W=[(Sk,)])

                def state_out(d, si):
                    S, Sk = S32[d]
                    if si < 2:
                        P.op("act", lambda e: e.activation(out=stg, in_=S, func=AF.Identity), R=[(Sk,)], W=[(stgk,)])
                        P.dma("sp", O["o_gb" if d == 1 else "o_gf"][j, si, :, pc, :], stg, R=[(stgk,)])

                flatB = []
                for si, (s0, s1) in enumerate(SEQS):
                    ts_ = list(reversed(tiles_of(s0, s1)))
                    for q_, t in enumerate(ts_):
                        flatB.append((si, t, q_ == 0, q_ == len(ts_) - 1))

                def b_stage1(idx):
                    si, t, first, last = flatB[idx]
                    pp, c0 = idx % 2, t * 128
                    kt, ktk = kt2[pp]
                    for k in range(8):
                        P.op("pe", lambda e, k=k: e.matmul(psum[:, 0, 0:128], lhsT=wg[:, k, 128:256], rhs=hT[:, k, c0:c0 + 128],
                             start=(k == 0), stop=(k == 7)), R=[(wgk,), (hk, k)], W=[PS(0)])
                    gate_stage(c0, [1], pp)
                    P.op("act", lambda e: e.activation(out=En[:, 1, :], in_=cb[:, 1, :], func=AF.Exp, scale=-1.0 / 16), R=[(cbk, 1)], W=[(Enk, 1)])
                    P.op("dve", lambda e: e.tensor_tensor(out=kt[:, 1, :], in0=psum[:, 0, 0:128], in1=En[:, 1, :], op=ALU.mult),
                         R=[PS(0), (Enk, 1)], W=[(ktk, 1)])
                    v_stage(c0, pp)

                def b_stage2(idx):
                    si, t, first, last = flatB[idx]
                    pp = idx % 2
                    S, Sk = S32[1]
                    if first:
                        state_init(1, si)
                    P.op("act", lambda e: e.activation(out=Sbs[:, t, :], in_=S, func=AF.Identity), R=[(Sk,)], W=[(Sbsk, t)])
                    state_update(1, pp)
                    if last:
                        state_out(1, si)

                for idx in range(len(flatB) + 1):
                    la = P.capture(lambda: b_stage1(idx)) if idx < len(flatB) else []
                    lb = P.capture(lambda: b_stage2(idx - 1)) if idx >= 1 else []
                    P.replay_merged(la, lb)
                if sub is not None and sub <= 2:
                    P.barrier()
                    return
                flatF = []
                for si, (s0, s1) in enumerate(SEQS):
                    ts_ = tiles_of(s0, s1)
                    for q_, t in enumerate(ts_):
                        flatF.append((si, t, q_ == 0, q_ == len(ts_) - 1))

                def f_stage1(idx):
                    si, t, first, last = flatF[idx]
                    pp, c0 = idx % 2, t * 128
                    kt, ktk = kt2[pp]
                    srg, srgk = srg2[pp]
                    qp = qpad2[pp]
                    for qk in range(2):
                        for k in range(8):
                            P.op("pe", lambda e, k=k, qk=qk: e.matmul(psum[:, 0, qk * 128:(qk + 1) * 128],
                                 lhsT=wg[:, k, qk * 128:(qk + 1) * 128], rhs=hT[:, k, c0:c0 + 128], start=(k == 0), stop=(k == 7)),
                                 R=[(wgk,), (hk, k)], W=[PS(0)])
                    gate_stage(c0, [0, 1], pp)
                    P.op("act", lambda e: e.activation(out=En, in_=cb, func=AF.Exp, scale=-1.0 / 16), R=[(cbk,)], W=[(Enk,)])
                    P.op("act", lambda e: e.activation(out=Ep, in_=cb, func=AF.Exp, scale=1.0 / 16), R=[(cbk,)], W=[(Epk,)])
                    qps = psum[:, 0, 0:128]
                    kps = psum[:, 0, 128:256]
                    for i_ in range(2):
                        r0 = i_ * 64
                        P.op("dve", lambda e, i_=i_, r0=r0: e.scalar_tensor_tensor(out=qp[0][i_][0][r0:r0 + 64, :], in0=qps[r0:r0 + 64, :], scalar=0.125, in1=En[r0:r0 + 64, 0, :],
                             op0=ALU.mult, op1=ALU.mult), R=[PS(0), (Enk,)], W=[(qp[0][i_][1],)])
                    P.op("dve", lambda e: e.tensor_tensor(out=kt[:, 0, :], in0=kps, in1=Ep[:, 0, :], op=ALU.mult), R=[PS(0), (Epk,)], W=[(ktk, 0)])
                    for i_ in range(2):
                        r0 = i_ * 64
                        P.op("dve", lambda e, i_=i_, r0=r0: e.scalar_tensor_tensor(out=qp[1][i_][0][r0:r0 + 64, :], in0=qps[r0:r0 + 64, :], scalar=0.125, in1=Ep[r0:r0 + 64, 1, :],
                             op0=ALU.mult, op1=ALU.mult), R=[PS(0), (Epk,)], W=[(qp[1][i_][1],)])
                    P.op("dve", lambda e: e.tensor_tensor(out=kt[:, 1, :], in0=kps, in1=En[:, 1, :], op=ALU.mult), R=[PS(0), (Enk,)], W=[(ktk, 1)])
                    v_stage(c0, pp)
                    for i in range(2):
                        for k in range(8):
                            P.op("pe", lambda e, k=k, i=i: e.matmul(psum[:, 4, i * 128:(i + 1) * 128], lhsT=wg[:, k, 512 + i * 128:512 + (i + 1) * 128],
                                 rhs=hT[:, k, c0:c0 + 128], start=(k == 0), stop=(k == 7)), R=[(wgk,), (hk, k)], W=[PS(4)])
                    P.op("act", lambda e: e.activation(out=srg, in_=psum[:, 4, 0:256].rearrange("p (a b) -> p a b", a=2), func=AF.Exp, scale=-1.0), R=[PS(4)], W=[(srgk,)])
                    P.op("act", lambda e: e.activation(out=srg, in_=srg, func=AF.Ln, bias=1.0, scale=1.0), R=[(srgk,)], W=[(srgk,)])
                    P.op("act", lambda e: e.activation(out=srg, in_=srg, func=AF.Exp, scale=-1.0), R=[(srgk,)], W=[(srgk,)])
                    P.op("dve", lambda e: e.tensor_tensor(out=srg, in0=psum[:, 4, 0:256].rearrange("p (a b) -> p a b", a=2), in1=srg, op=ALU.mult), R=[PS(4), (srgk,)], W=[(srgk,)])

                def f_stage2(idx):
                    si, t, first, last = flatF[idx]
                    pp, c0 = idx % 2, t * 128
                    kt, ktk = kt2[pp]
                    srg, srgk = srg2[pp]
                    qp = qpad2[pp]
                    vtm, vtmk = vtm2[pp]
                    S, Sk = S32[0]
                    if first:
                        state_init(0, si)
                    P.op("act", lambda e: e.activation(out=Sfb, in_=S, func=AF.Identity), R=[(Sk,)], W=[(Sfbk,)])
                    for i in range(2):
                        P.op("pe", lambda e, i=i: e.matmul(psum[:, 5, i * 128:(i + 1) * 128], lhsT=kt[:, 0, :], rhs=qp[0][i][0],
                             start=True, stop=True), R=[(qp[0][i][1],), ("qpz",), (ktk, 0)], W=[PS(5)])
                        P.op("pe", lambda e, i=i: e.matmul(psum[:, 6, i * 128:(i + 1) * 128], lhsT=kt[:, 1, :], rhs=qp[1][i][0],
                             start=True, stop=True), R=[(qp[1][i][1],), ("qpz",), (ktk, 1)], W=[PS(6)])
                    P.op("dve", lambda e: e.tensor_tensor(out=m1, in0=psum[:, 5, 0:256].rearrange("p (a b) -> p a b", a=2), in1=mbf[:, 0, 0:2, :], op=ALU.mult), R=[PS(5), ("mbf",)], W=[(m1k,)])
                    P.op("dve", lambda e: e.tensor_tensor(out=m2, in0=psum[:, 6, 0:256].rearrange("p (a b) -> p a b", a=2), in1=mbf[:, 1, 0:2, :], op=ALU.mult), R=[PS(6), ("mbf",)], W=[(m2k,)])
                    P.op("pool", lambda e: e.tensor_tensor(out=att, in0=m1, in1=m2, op=ALU.add), R=[(m1k,), (m2k,)], W=[(attk,)])
                    for i in range(2):
                        oc = 256 + i * 128
                        P.op("pe", lambda e, i=i, oc=oc: e.matmul(psum[:, 7, oc:oc + 128], lhsT=vtm[:, i * 128:(i + 1) * 128], rhs=att[:, i, :], start=True, stop=False),
                             R=[(vtmk,), (attk,)], W=[PS(7)])
                        P.op("pe", lambda e, i=i, oc=oc: e.matmul(psum[:, 7, oc:oc + 128], lhsT=Sfb, rhs=qp[0][i][0], start=False, stop=False),
                             R=[(Sfbk,), (qp[0][i][1],)], W=[PS(7)])
                        P.op("pe", lambda e, i=i, oc=oc: e.matmul(psum[:, 7, oc:oc + 128], lhsT=Sbs[:, t, :], rhs=qp[1][i][0], start=False, stop=True),
                             R=[(Sbsk, t), (qp[1][i][1],)], W=[PS(7)])
                    P.op("act", lambda e: e.activation(out=osq, in_=psum[:, 7, 256:512], func=AF.Square), R=[PS(7)], W=[(osqk,)])
                    P.op("pe", lambda e: e.matmul(psum[:, 6, 256:512], lhsT=ones, rhs=osq, start=True, stop=True), R=[(osqk,), ("cbf",)], W=[PS(6)])
                    P.op("act", lambda e: e.activation(out=ors, in_=psum[:, 6, 256:512], func=AF.Ln, scale=1.0 / 128, bias=eps_t[:]), R=[PS(6), ("eps",)], W=[(orsk,)])
                    P.op("act", lambda e: e.activation(out=ors, in_=ors, func=AF.Exp, scale=-0.5), R=[(orsk,)], W=[(orsk,)])
                    P.op("dve", lambda e: e.tensor_tensor(out=ot, in0=psum[:, 7, 256:512], in1=ors, op=ALU.mult), R=[PS(7), (orsk,)], W=[(otk,)])
                    for i in range(2):
                        hh = 2 * pc + i
                        P.op("dve", lambda e, i=i, hh=hh: e.scalar_tensor_tensor(out=og[:, i, c0:c0 + 128], in0=ot[:, i * 128:(i + 1) * 128], scalar=glag[:, j, hh:hh + 1],
                             in1=srg[:, i, :], op0=ALU.mult, op1=ALU.mult), R=[(otk,), (srgk,), ("glag",)], W=[(ogk, i)])
                    state_update(0, pp)
                    if last:
                        state_out(0, si)

                for idx in range(len(flatF) + 1):
                    la = P.capture(lambda: f_stage1(idx)) if idx < len(flatF) else []
                    lb = P.capture(lambda: f_stage2(idx - 1)) if idx >= 1 else []
                    P.replay_merged(la, lb)
                out_proj_residual(l, lambda d: I["wOab"][j, d, :, 4 + 2 * pc:6 + 2 * pc, :], og, ogk, 2, 16)
                P.barrier()

            for pc_ in range(2):
                gla_part(pc_)
                if sub is not None and sub <= 3 + pc_:
                    return
            A.off = mark
            A.gen += 1
            oa, oak = A.alloc("oa", (1, NT), BF16)
            wq3 = [A.alloc("wa%d" % i, (8, 128), BF16) for i in range(3)]
            qA, qAk = A.alloc("qA", (NT,), BF16)
            kAh = [A.alloc("kA%d" % i, (NT,), BF16) for i in range(2)]
            kAk = "kAall"
            vA, vAk = A.alloc("vA", (NTILE, 2, 128), BF16)
            kst, kstk = A.alloc("kst", (512,), F32)
            vst, vstk = A.alloc("vst", (4, 128), F32)
            T2b, T2k = A.alloc("T2b", (2, 7, 128), BF16)
            T2ib, T2ik = A.alloc("T2ib", (2, 7, 128), BF16)
            kcxh = [A.alloc("kcx%d" % i, (256,), BF16) for i in range(2)]
            kcxk = "kcxall"
            vcx, vcxk = A.alloc("vcx", (2, 2, 128), BF16)
            ptI = [A.alloc("ptI%d" % i, (5, 128), BF16) for i in range(2)]
            ptE = [A.alloc("ptE%d" % i, (4, 128), BF16) for i in range(2)]
            ptC = [A.alloc("ptC%d" % i, (2, 128), BF16) for i in range(2)]
            ptP = [A.alloc("ptP%d" % i, (2, 256), BF16) for i in range(2)]
            tb = [A.alloc("tb%d" % i, (5, 128), F32) for i in range(2)]
            rrs = [A.alloc("rr%d" % i, (256,), F32) for i in range(2)]
            pL = [A.alloc("pL%d" % i, (4, 128), BF16) for i in range(2)]
            pX = [A.alloc("pX%d" % i, (3, 128), BF16) for i in range(2)]
            wmark = A.off
            P.op("pool", lambda e: e.memset(vA[:, :, :, 64:128], 1.0), W=[(vAk, "ones")])
            P.op("pool", lambda e: e.memset(kAh[0][0][64:128, :], 0.0), W=[("kAz", 0)])
            P.op("pool", lambda e: e.memset(kAh[1][0][0:64, :], 0.0), W=[("kAz", 1)])
            P.op("pool", lambda e: e.memset(kcxh[0][0][64:128, :], 0.0), W=[("kAz", 2)])
            P.op("pool", lambda e: e.memset(kcxh[1][0][0:64, :], 0.0), W=[("kAz", 3)])
            P.op("pool", lambda e: e.memset(vcx[:, :, :, 64:128], 1.0), W=[(vcxk, "ones")])
            cnt = [0]
            for hp in range(4):
                for wi in range(3):
                    P.dma("pool", wq3[wi][0], I["wA"][j, wi, hp], W=[(wq3[wi][1],)])
                import os
                CUT = os.environ.get("CUT", "")
                if "a" not in CUT:
                    P.dma("pool", T2b, I["T2"][j, hp], W=[(T2k,)])
                    P.dma("pool", T2ib, I["T2i"][j, hp], W=[(T2ik,)])
                    P.dma("pool", kcxh[0][0][0:64, :], I["cnk"][j, hp, 0:64, :], W=[(kcxk, 0)])
                    P.dma("pool", kcxh[1][0][64:128, :], I["cnk"][j, hp, 64:128, :], W=[(kcxk, 1)])
                    P.dma("pool", vcx[:, :, :, 0:64], I["cnv"][j, hp], W=[(vcxk, "v")])
                bi = 0
                for wi, (dst, dk_) in enumerate([(qA, qAk), (None, kAk)] if "b" not in CUT else []):
                    for (t0, wd) in split_blocks(0, NT):
                        bank = 6 + (bi % 2)
                        bi += 1
                        for k in range(8):
                            P.op("pe", lambda e, wi=wi, k=k, t0=t0, wd=wd, bank=bank: e.matmul(psum[:, bank, 0:wd], lhsT=wq3[wi][0][:, k, :], rhs=hT[:, k, t0:t0 + wd],
                                 start=(k == 0), stop=(k == 7)), R=[(wq3[wi][1],), (hk, k)], W=[PS(bank)])
                        if wi == 0:
                            P.op("act", lambda e, dst=dst, t0=t0, wd=wd, bank=bank: e.activation(out=dst[:, t0:t0 + wd], in_=psum[:, bank, 0:wd], func=AF.Identity, scale=0.125),
                                 R=[PS(bank)], W=[(dk_, t0 // 512)])
                        else:
                            for hf in range(2):
                                r0 = hf * 64
                                P.op("act", lambda e, hf=hf, r0=r0, t0=t0, wd=wd, bank=bank: e.activation(out=kAh[hf][0][r0:r0 + 64, t0:t0 + wd], in_=psum[r0:r0 + 64, bank, 0:wd], func=AF.Identity),
                                     R=[PS(bank)], W=[(dk_, t0 // 512, hf)])
                        if wi == 1 and t0 == 0 and "d" not in CUT:
                            P.op("dve", lambda e, bank=bank: e.tensor_copy(out=kst, in_=psum[:, bank, 0:512]), R=[PS(bank)], W=[(kstk,)])
                            if "e" not in CUT:
                                P.dma("sp", O["o_nak"][j, hp], kst, R=[(kstk,)])
                for t in range(NTILE if "c" not in CUT else 0):
                    bank = 6 + (t % 2)
                    for k in range(8):
                        P.op("pe", lambda e, k=k, t=t, bank=bank: e.matmul(psum[:, bank, 0:128], lhsT=hT[:, k, t * 128:(t + 1) * 128], rhs=wq3[2][0][:, k, :],
                             start=(k == 0), stop=(k == 7)), R=[(wq3[2][1],), (hk, k)], W=[PS(bank)])
                    P.op("act", lambda e, t=t, bank=bank: e.activation(out=vA[:, t, :, 0:64], in_=psum[:, bank, 0:128].rearrange("p (a b) -> p a b", a=2), func=AF.Identity),
                         R=[PS(bank)], W=[(vAk, "v", t)])
                    if t < 4:
                        P.op("dve", lambda e, t=t, bank=bank: e.tensor_copy(out=vst[:, t, :], in_=psum[:, bank, 0:128]), R=[PS(bank)], W=[(vstk, t)])
                P.dma("sp", O["o_nav"][j, hp].rearrange("t p c -> p t c"), vst, R=[(vstk,)])
                if sub == 5:
                    P.barrier()
                    return
                def na_head(i):
                    base = i * 64
                    kAi = kAh[i][0]
                    kci = kcxh[i][0]
                    prompt_attention(qA, qAk, kAi, kAk, lambda tt, i=i: vA[:, tt, i, :], vAk, base,
                                     lambda c0, w, base=base: (oa[base:base + 64, 0, c0:c0 + w], [(oak, base, c0)]), ptP, rrs, cnt, k128=True, escale=1.0)
                    nn = 16 if sub is None else {6: 0, 7: 1, 8: 3, 9: 16}.get(sub, 16)
                    u0 = cnt[0]
                    cnt[0] += nn

                    def info(n):
                        u = u0 + n
                        par = u % 2
                        interior = 2 <= n <= 13
                        if interior:
                            ms = list(range(n - 2, n + 3))
                        elif n < 2:
                            ms = [0, 1, 2, 3]
                        else:
                            ms = [12, 13, 14, 15]
                        e0 = (2 * (ms[0] - n) + 7 - 1) // 2
                        return par, 0 + 3 * par, 1 + 3 * par, 2 + 3 * par, 512 + n * 128, interior, ms, e0

                    def st1(n):
                        par, sb_, cb_, ob_, q0, interior, ms, e0 = info(n)
                        nm = len(ms)
                        Tt, Ttk = (T2ib, T2ik) if interior else (T2b, T2k)
                        for ci, m in enumerate(ms):
                            k0 = 512 + m * 128
                            bk_, c_ = (sb_, ci * 128) if ci < 4 else (cb_, 256)
                            P.op("pe", lambda e, k0=k0, bk_=bk_, c_=c_: e.matmul(psum[:, bk_, c_:c_ + 128], lhsT=kAi[:, k0:k0 + 128],
                                 rhs=qA[:, q0:q0 + 128], start=True, stop=False), R=[(kAk,), ("kAz",), (qAk,)], W=[PS(bk_)])
                            P.op("pe", lambda e, ci=ci, bk_=bk_, c_=c_: e.matmul(psum[:, bk_, c_:c_ + 128], lhsT=ident,
                                 rhs=Tt[:, i, e0 + ci, :], start=False, stop=True), R=[(Ttk,), ("cbf",)], W=[PS(bk_)])
                        for cc in range(2):
                            P.op("pe", lambda e, cc=cc: e.matmul(psum[:, cb_, cc * 128:(cc + 1) * 128], lhsT=kci[:, cc * 128:(cc + 1) * 128],
                                 rhs=qA[:, q0:q0 + 128], start=True, stop=True), R=[(kcxk,), ("kAz",), (qAk,)], W=[PS(cb_)])
                        n4 = min(nm, 4)
                        nx = 3 if nm == 5 else 2
                        pl, plk = pL[par]
                        px, pxk = pX[par]
                        P.op("act", lambda e: e.activation(out=pl[:, 0:n4, :], in_=psum[:, sb_, 0:n4 * 128].rearrange("p (a b) -> p a b", a=n4), func=AF.Exp),
                             R=[PS(sb_)], W=[(plk,)])
                        P.op("act", lambda e: e.activation(out=px[:, 0:nx, :], in_=psum[:, cb_, 0:nx * 128].rearrange("p (a b) -> p a b", a=nx), func=AF.Exp),
                             R=[PS(cb_)], W=[(pxk,)])

                    def st2(n):
                        par, sb_, cb_, ob_, q0, interior, ms, e0 = info(n)
                        pl, plk = pL[par]
                        px, pxk = pX[par]
                        for ci, m in enumerate(ms):
                            tt = 4 + m
                            rhs_ = pl[:, ci, :] if ci < 4 else px[:, 2, :]
                            P.op("pe", lambda e, ci=ci, tt=tt, rhs_=rhs_: e.matmul(psum[:, ob_, 0:128], lhsT=vA[:, tt, i, :], rhs=rhs_, start=(ci == 0), stop=False),
                                 R=[(vAk,), (plk,), (pxk,)], W=[PS(ob_)])
                        for cc in range(2):
                            P.op("pe", lambda e, cc=cc: e.matmul(psum[:, ob_, 0:128], lhsT=vcx[:, cc, i, :], rhs=px[:, cc, :], start=False, stop=(cc == 1)),
                                 R=[(vcxk,), (pxk,)], W=[PS(ob_)])
                        rr, rrk = rrs[par]
                        softmax_finish(ob_, 128, oa[base:base + 64, 0, q0:q0 + 128], [(oak, base, q0)], rr, rrk)

                    for n in range(nn + 1):
                        if n < nn:
                            st1(n)
                        if n >= 1:
                            st2(n - 1)
                for i_ in range(2):
                    na_head(i_)
                if sub is not None and sub >= 6:
                    P.barrier()
                    return
                A.off = wmark
                out_proj_residual(l, lambda d, hp=hp: I["wOab"][j, d, :, hp:hp + 1, :], oa, oak, 1, 16)
            P.barrier()

        def gqa_layer(l):
            j = l // 2
            A.reset()
            hT, hk = A.alloc("hT", (8, NT), BF16)
            oT, oTk = A.alloc("attn", (2, NT), BF16)
            nt = norm_tmp(7)
            for (t0, w) in split_blocks(0, NT):
                norm_block(l * 2, l, 0, t0, w, hT, t0, hk, nt)
            wq2 = [A.alloc("wq%d" % i, (8, 128), BF16) for i in range(2)]
            wkb, wkk = A.alloc("wk", (8, 128), BF16)
            wvb, wvk = A.alloc("wv", (8, 64), BF16)
            kGh = [A.alloc("kG%d" % i, (NT + 256,), BF16) for i in range(2)]
            kGk = "kGall"
            vG, vGk = A.alloc("vG", (NTILE + 2, 128), BF16)
            cosb = [A.alloc("cos%d" % i, (512,), F32) for i in range(2)]
            sinb = [A.alloc("sin%d" % i, (512,), F32) for i in range(2)]
            sqb, sqk = nt["sq"]
            rsb, rsk = A.alloc("grs", (2, 512), F32)
            qnb, qnk = nt["tt"]
            qbb, qbk = A.alloc("gqnb", (2, 512), BF16)
            t1_, t1k_ = A.alloc("gt1", (512,), F32)
            t2_, t2k_ = A.alloc("gt2", (512,), F32)
            kst, kstk = t1_, t1k_
            vst, vstk = A.alloc("vst", (4, 64), F32)
            ptS = [A.alloc("ptS%d" % i, (2, 512), BF16) for i in range(3)]
            ptP = [A.alloc("ptP%d" % i, (2, 256), BF16) for i in range(2)]
            rrs = [A.alloc("rr%d" % i, (512,), F32) for i in range(1)] * 2
            P.op("pool", lambda e: e.memset(vG[:, :, 64:128], 1.0), W=[(vGk, "ones")])
            P.op("pool", lambda e: e.memset(kGh[0][0][64:128, :], 0.0), W=[("kGz", 0)])
            P.op("pool", lambda e: e.memset(kGh[1][0][0:64, :], 0.0), W=[("kGz", 1)])
            cnt = [0]
            blocks = split_blocks(0, NT)

            def group(g):
                for c2 in range(2):
                    P.dma("pool", wq2[c2][0], I["wQ"][j, 2 * g + c2], W=[(wq2[c2][1],)])
                P.dma("pool", wkb, I["wK"][j, g], W=[(wkk,)])
                P.dma("pool", wvb, I["wV"][j, g], W=[(wvk,)])
                P.dma("pool", kGh[0][0][0:64, NT:NT + 256], I["cgk"][j, g, 0:64, :], W=[(kGk, "ctx", 0)])
                P.dma("pool", kGh[1][0][64:128, NT:NT + 256], I["cgk"][j, g, 64:128, :], W=[(kGk, "ctx", 1)])
                P.dma("pool", vG[:, NTILE:NTILE + 2, 0:64], I["cgv"][j, g], W=[(vGk, "ctx")])
                units = [(bi, which) for bi in range(len(blocks)) for which in range(3)]
                nu = len(units)

                def uinfo(u):
                    bi, which = units[u]
                    t0, wd = blocks[bi]
                    if which < 2:
                        return t0, wd, wq2[which], qng, oT[:, which, t0:t0 + wd], [(oTk, which, t0 // 512)], which
                    return t0, wd, (wkb, wkk), kng, None, [(kGk, "tok", t0 // 512)], which

                def stA(u):
                    t0, wd, (ws, wsk), gvec, dst, dkeys, which = uinfo(u)
                    pb, s = u % 3, u % 2
                    if which == 0 and t0 >= 512:
                        bsel = (t0 // 512) % 2
                        P.dma("sp", cosb[bsel][0], I["cosT"][:, t0 - 512:t0], W=[(cosb[bsel][1],)])
                        P.dma("sp", sinb[bsel][0], I["sinT"][:, t0 - 512:t0], W=[(sinb[bsel][1],)])
                    for k in range(8):
                        P.op("pe", lambda e, k=k: e.matmul(psum[:, pb, 0:wd], lhsT=ws[:, k, :], rhs=hT[:, k, t0:t0 + wd],
                             start=(k == 0), stop=(k == 7)), R=[(wsk,), (hk, k)], W=[PS(pb)])
                    P.op("act", lambda e: e.activation(out=sqb[:, s, 0:wd], in_=psum[:, pb, 0:wd], func=AF.Square), R=[PS(pb)], W=[(sqk, s)])

                def stB(u):
                    t0, wd, (ws, wsk), gvec, dst, dkeys, which = uinfo(u)
                    pb, s = u % 3, u % 2
                    P.op("pe", lambda e: e.matmul(psum[:, 3, 0:wd], lhsT=blockones, rhs=sqb[:, s, 0:wd], start=True, stop=True), R=[(sqk, s), ("cbf",)], W=[PS(3)])
                    P.op("act", lambda e: e.activation(out=rsb[:, s, 0:wd], in_=psum[:, 3, 0:wd], func=AF.Ln, scale=1.0 / 64, bias=eps_t[:]), R=[PS(3), ("eps",)], W=[(rsk, s)])
                    P.op("act", lambda e: e.activation(out=rsb[:, s, 0:wd], in_=rsb[:, s, 0:wd], func=AF.Exp, scale=-0.5), R=[(rsk, s)], W=[(rsk, s)])
                    if t0 < 512:
                        if which < 2:
                            P.op("dve", lambda e: e.scalar_tensor_tensor(out=dst, in0=psum[:, pb, 0:wd], scalar=gvec[:, j:j + 1], in1=rsb[:, s, 0:wd], op0=ALU.mult, op1=ALU.mult),
                                 R=[PS(pb), (rsk, s), ("qkng",)], W=dkeys)
                        else:
                            for hf in range(2):
                                r0 = hf * 64
                                P.op("dve", lambda e, hf=hf, r0=r0: e.scalar_tensor_tensor(out=kGh[hf][0][r0:r0 + 64, t0:t0 + wd], in0=psum[r0:r0 + 64, pb, 0:wd], scalar=gvec[r0:r0 + 64, j:j + 1],
                                     in1=rsb[r0:r0 + 64, s, 0:wd], op0=ALU.mult, op1=ALU.mult), R=[PS(pb), (rsk, s), ("qkng",)], W=dkeys)
                        if which == 2:
                            P.op("dve", lambda e: e.scalar_tensor_tensor(out=kst[:, 0:wd], in0=psum[:, pb, 0:wd], scalar=gvec[:, j:j + 1], in1=rsb[:, s, 0:wd], op0=ALU.mult, op1=ALU.mult),
                                 R=[PS(pb), (rsk, s), ("qkng",)], W=[(kstk,)])
                            P.dma("sp", O["o_gk"][j, g], kst[0:64, :], R=[(kstk,)])
                        return
                    P.op("dve", lambda e: e.scalar_tensor_tensor(out=qnb[:, s, 0:wd], in0=psum[:, pb, 0:wd], scalar=gvec[:, j:j + 1], in1=rsb[:, s, 0:wd], op0=ALU.mult, op1=ALU.mult),
                         R=[PS(pb), (rsk, s), ("qkng",)], W=[(qnk, s)])
                    P.op("act", lambda e: e.activation(out=qbb[:, s, 0:wd], in_=qnb[:, s, 0:wd], func=AF.Identity), R=[(qnk, s)], W=[(qbk, s)])

                def stC(u):
                    t0, wd, (ws, wsk), gvec, dst, dkeys, which = uinfo(u)
                    if t0 < 512:
                        return
                    s = u % 2
                    bsel = (t0 // 512) % 2
                    cs, csk = cosb[bsel]
                    sn, snk = sinb[bsel]
                    P.op("pe", lambda e: e.matmul(psum[:, 4, 0:wd], lhsT=rperm, rhs=qbb[:, s, 0:wd], start=True, stop=True), R=[(qbk, s), ("cbf",)], W=[PS(4)])
                    P.op("dve", lambda e: e.tensor_tensor(out=t1_[:, 0:wd], in0=qnb[:, s, 0:wd], in1=cs[:, 0:wd], op=ALU.mult), R=[(qnk, s), (csk,)], W=[(t1k_,)])
                    P.op("dve", lambda e: e.tensor_tensor(out=t2_[:, 0:wd], in0=psum[:, 4, 0:wd], in1=sn[:, 0:wd], op=ALU.mult), R=[PS(4), (snk,)], W=[(t2k_,)])
                    if which < 2:
                        P.op("pool", lambda e: e.tensor_tensor(out=dst, in0=t1_[:, 0:wd], in1=t2_[:, 0:wd], op=ALU.add), R=[(t1k_,), (t2k_,)], W=dkeys)
                    else:
                        for hf in range(2):
                            r0 = hf * 64
                            P.op("pool", lambda e, hf=hf, r0=r0: e.tensor_tensor(out=kGh[hf][0][r0:r0 + 64, t0:t0 + wd], in0=t1_[r0:r0 + 64, 0:wd], in1=t2_[r0:r0 + 64, 0:wd], op=ALU.add),
                                 R=[(t1k_,), (t2k_,)], W=dkeys)

                for i in range(nu + 2):
                    if i < nu:
                        stA(i)
                    if 0 <= i - 1 < nu:
                        stB(i - 1)
                    if 0 <= i - 2 < nu:
                        stC(i - 2)
                for t in range(NTILE):
                    pb = 6 + (t % 2)
                    for k in range(8):
                        P.op("pe", lambda e, k=k, t=t, pb=pb: e.matmul(psum[:, pb, 0:64], lhsT=hT[:, k, t * 128:(t + 1) * 128], rhs=wvb[:, k, :],
                             start=(k == 0), stop=(k == 7)), R=[(wvk,), (hk, k)], W=[PS(pb)])
                    P.op("act", lambda e, t=t, pb=pb: e.activation(out=vG[:, t, 0:64], in_=psum[:, pb, 0:64], func=AF.Identity), R=[PS(pb)], W=[(vGk, "v", t)])
                    if t < 4:
                        P.op("dve", lambda e, t=t, pb=pb: e.tensor_copy(out=vst[:, t, :], in_=psum[:, pb, 0:64]), R=[PS(pb)], W=[(vstk, t)])
                P.dma("sp", O["o_gv"][j, g].rearrange("t p c -> p t c"), vst, R=[(vstk,)])
                for hq in range(4):
                    ch = hq // 2
                    base = (hq % 2) * 64
                    qv = oT[:, ch, :]
                    prompt_attention(qv, oTk, kGh[hq % 2][0], kGk, lambda tt: vG[:, tt, :], vGk, base,
                                     lambda c0, w, ch=ch, base=base: (oT[base:base + 64, ch, c0:c0 + w], [(oTk, ch, 0, base, c0)]), ptP, rrs, cnt,
                                     qkeyf=lambda c0, ch=ch, base=base: (oTk, ch, 0), k128=True)
                chunks = [(512 + m * 128, 4 + m) for m in range(16)] + [(NT, NTILE), (NT + 128, NTILE + 1)]
                items = [(hq, qb, pi) for hq in range(4) for qb in range(4) for pi in range(9)]
                LA = 2

                def emit_qk(kk):
                    hq, qb, pi = items[kk]
                    ch, base, q0 = hq // 2, (hq % 2) * 64, 512 + qb * 512
                    slot = kk % 3
                    pt, ptk = ptS[kk % 3]
                    for e_ in range(2):
                        kc, vt = chunks[2 * pi + e_]
                        P.op("pe", lambda e, e_=e_, kc=kc: e.matmul(psum[:, 2 * slot + e_, :], lhsT=kGh[hq % 2][0][:, kc:kc + 128],
                             rhs=oT[:, ch, q0:q0 + 512], start=True, stop=True), R=[(kGk,), ("kGz",), (oTk, ch, qb + 1)], W=[PS(2 * slot + e_)])
                    P.op("act", lambda e: e.activation(out=pt, in_=psum[:, 2 * slot:2 * slot + 2, :], func=AF.Exp, scale=0.125),
                         R=[PS(2 * slot), PS(2 * slot + 1)], W=[(ptk,)])

                def emit_pv(kk):
                    hq, qb, pi = items[kk]
                    ch, base, q0 = hq // 2, (hq % 2) * 64, 512 + qb * 512
                    u = hq * 4 + qb
                    ob_ = 6 + (u % 2)
                    pt, ptk = ptS[kk % 3]
                    for e_ in range(2):
                        kc, vt = chunks[2 * pi + e_]
                        P.op("pe", lambda e, e_=e_, vt=vt: e.matmul(psum[:, ob_, :], lhsT=vG[:, vt, :], rhs=pt[:, e_, :], start=(pi == 0 and e_ == 0), stop=(pi == 8 and e_ == 1)),
                             R=[(vGk,), (ptk,)], W=[PS(ob_)])
                    if pi == 8:
                        rr, rrk = rrs[u % 2]
                        softmax_finish(ob_, 512, oT[base:base + 64, ch, q0:q0 + 512], [(oTk, ch, qb + 1, base)], rr, rrk)

                for kk in range(len(items) + LA):
                    if kk < len(items):
                        emit_qk(kk)
                    if kk >= LA:
                        emit_pv(kk - LA)
                gmark = A.off
                out_proj_residual(l, lambda d, g=g: I["wOc"][j, d, :, 2 * g:2 * g + 2, :], oT, oTk, 2, 16)
                A.off = gmark

            for g_ in range(4):
                group(g_)
            P.barrier()

        def ffn_layer(l):
            A.reset()
            NG = 4
            GW = NT // NG
            hG2 = [A.alloc("hG%d" % i, (8, GW + 2), BF16) for i in range(2)]
            act, actk = A.alloc("act", (NJ, GW), BF16)
            wup = [A.alloc("wup%d" % i, (2, 8, 128), BF16) for i in range(3)]
            wdn = [A.alloc("wdn%d" % i, (NJ, 128), BF16) for i in range(2)]
            usb = [[A.alloc("u%d%d" % (i, vg), (GW + 2,), F32) for vg in range(2)] for i in range(2)]
            Ab = [[A.alloc("A%d%d" % (i, vg), (GW,), F32) for vg in range(2)] for i in range(2)]
            Bg, Bgk = A.alloc("Bg", (GW,), F32)
            nt = norm_tmp(7)
            adaF = [A.alloc("adaF%d" % i, (8, 256), BF16) for i in range(2)] if l + 1 < DEPTH else None
            xsave, xsk = A.alloc("xsave", (8, 4), F32)
            for g in range(1, NG):
                P.op("dve", lambda e, g=g: e.tensor_copy(out=xsave[:, :, g:g + 1], in_=xT[:, :, g * GW - 1:g * GW]), R=[("xT",)], W=[(xsk, g)])
            pbi = [0]
            up_items = [(g, jj) for g in range(NG) for jj in range(NJ)]
            dn_items = [(g, d) for g in range(NG) for d in range(8)]
            up_next = [0]
            dn_next = [0]

            def up_prefetch(upto):
                while up_next[0] <= min(upto, len(up_items) - 1):
                    i = up_next[0]
                    wu, wuk = wup[i % 3]
                    P.dma("pool", wu, I["wUp"][l, up_items[i][1]].rearrange("v p k c -> p v k c"), W=[(wuk,)])
                    up_next[0] += 1

            def dn_prefetch(upto):
                while dn_next[0] <= min(upto, len(dn_items) - 1):
                    i = dn_next[0]
                    wd_, wdk = wdn[i % 2]
                    P.dma("pool", wd_, I["wDn"][l, dn_items[i][1]], W=[(wdk,)])
                    dn_next[0] += 1

            def ffn_norm(g):
                hG, hGk = hG2[g % 2]
                t0g, t1g = g * GW, (g + 1) * GW
                lo = t0g - (0 if t0g in SEQ_STARTS else 1)
                hi = t1g + (0 if t1g in SEQ_ENDS else 1)
                if lo < t0g:
                    norm_block(l * 2 + 1, l, 1, lo, 1, hG, 0, hGk, nt, xsrc=lambda k: xsave[:, k, g:g + 1], xkey=(xsk, g))
                for (t0, w) in split_blocks(t0g, hi):
                    norm_block(l * 2 + 1, l, 1, t0, w, hG, t0 - lo, hGk, nt)

            def ffn_group(g):
                hG, hGk = hG2[g % 2]
                t0g, t1g = g * GW, (g + 1) * GW
                lo = t0g - (0 if t0g in SEQ_STARTS else 1)
                hi = t1g + (0 if t1g in SEQ_ENDS else 1)
                segs = []
                for (s0, s1) in SEQS:
                    a, b = max(s0, t0g), min(s1, t1g)
                    if a < b:
                        segs.append((a, b, s0, s1))
                blocks = split_blocks(lo, hi, 321)

                def ffn_tail(jq):
                    av, avk = Ab[jq % 2][0]
                    ag, agk = Ab[jq % 2][1]
                    P.op("act", lambda e: e.activation(out=Bg, in_=ag, func=AF.Silu), R=[(agk,)], W=[(Bgk,)])
                    P.op("pool", lambda e: e.tensor_tensor(out=act[:, jq, :], in0=av, in1=Bg, op=ALU.mult), R=[(avk,), (Bgk,)], W=[(actk, jq)])

                for jj in range(NJ):
                    ui = g * NJ + jj
                    if adaF is not None:
                        if ui % 3 == 0 and ui // 3 < 24:
                            mod_dma(l + 1, ui // 3, adaF[(ui // 3) % 2])
                        if ui % 3 == 2 and ui // 3 < 24:
                            mod_compute(l + 1, ui // 3, adaF[(ui // 3) % 2], 6)
                    up_prefetch(ui + 2)
                    if jj == NJ - 3:
                        dn_prefetch(g * 8)
                    wu, wuk = wup[ui % 3]
                    for vg in range(2):
                        ub, ubk = usb[jj % 2][vg]
                        ab, abk = Ab[jj % 2][vg]
                        chn = vg * NJ + jj
                        for (t0, w) in blocks:
                            bank = pbi[0] % 6
                            pbi[0] += 1
                            for k in range(8):
                                P.op("pe", lambda e, wu=wu, vg=vg, k=k, t0=t0, w=w, bank=bank: e.matmul(psum[:, bank, 0:w], lhsT=wu[:, vg, k, :], rhs=hG[:, k, t0 - lo:t0 - lo + w],
                                     start=(k == 0), stop=(k == 7)), R=[(wuk,), (hGk, k)], W=[PS(bank)])
                            P.op("act", lambda e, ub=ub, t0=t0, w=w, bank=bank: e.activation(out=ub[:, t0 - lo:t0 - lo + w], in_=psum[:, bank, 0:w], func=AF.Identity),
                                 R=[PS(bank)], W=[(ubk, t0)])
                            a0, a1 = max(t0, t0g), min(t0 + w, t1g)
                            if a0 < a1 and vg == 1:
                                P.op("act", lambda e, ab=ab, a0=a0, a1=a1, t0=t0, bank=bank, chn=chn: e.activation(out=ab[:, a0 - t0g:a1 - t0g], in_=psum[:, bank, a0 - t0:a1 - t0],
                                     func=AF.Identity, scale=convw[:, l, chn, 1:2], bias=convw[:, l, chn, 3:4]), R=[PS(bank), ("convw",)], W=[(abk, a0)])
                        if vg == 0:
                            P.op("pool", lambda e, ab=ab, ub=ub, chn=chn: e.tensor_scalar(out=ab[:, 0:GW], in0=ub[:, t0g - lo:t0g - lo + GW],
                                 scalar1=convw[:, l, chn, 1:2], scalar2=convw[:, l, chn, 3:4], op0=ALU.mult, op1=ALU.add), R=[(ubk,), ("convw",)], W=[(abk,)])
                        segs_ = segs
                        if g == 0 and segs[0][:2] == (0, 256) and segs[1][:2] == (256, 512):
                            A3 = ab[:, 0:512].rearrange("p (s c) -> p s c", s=2)
                            U3 = ub[:, 0:512].rearrange("p (s c) -> p s c", s=2)
                            P.op("dve", lambda e, A3=A3, U3=U3, chn=chn: e.scalar_tensor_tensor(out=A3[:, :, 1:256], in0=U3[:, :, 0:255],
                                 scalar=convw[:, l, chn, 0:1], in1=A3[:, :, 1:256], op0=ALU.mult, op1=ALU.add), R=[(ubk,), (abk,), ("convw",)], W=[(abk,)])
                            P.op("dve", lambda e, A3=A3, U3=U3, chn=chn: e.scalar_tensor_tensor(out=A3[:, :, 0:255], in0=U3[:, :, 1:256],
                                 scalar=convw[:, l, chn, 2:3], in1=A3[:, :, 0:255], op0=ALU.mult, op1=ALU.add), R=[(ubk,), (abk,), ("convw",)], W=[(abk,)])
                            segs_ = segs[2:]
                        for (a, b, s0, s1) in segs_:
                            al = a + 1 if a == s0 else a
                            if al < b:
                                P.op("dve", lambda e, ab=ab, ub=ub, al=al, b=b, chn=chn: e.scalar_tensor_tensor(out=ab[:, al - t0g:b - t0g], in0=ub[:, al - 1 - lo:b - 1 - lo],
                                     scalar=convw[:, l, chn, 0:1], in1=ab[:, al - t0g:b - t0g], op0=ALU.mult, op1=ALU.add), R=[(ubk,), (abk,), ("convw",)], W=[(abk,)])
                            br = b - 1 if b == s1 else b
                            if a < br:
                                P.op("dve", lambda e, ab=ab, ub=ub, a=a, br=br, chn=chn: e.scalar_tensor_tensor(out=ab[:, a - t0g:br - t0g], in0=ub[:, a + 1 - lo:br + 1 - lo],
                                     scalar=convw[:, l, chn, 2:3], in1=ab[:, a - t0g:br - t0g], op0=ALU.mult, op1=ALU.add), R=[(ubk,), (abk,), ("convw",)], W=[(abk,)])
                    if jj >= 1:
                        ffn_tail(jj - 1)
                ffn_tail(NJ - 1)
                if g + 1 < NG:
                    ffn_norm(g + 1)
                for d in range(8):
                    di = g * 8 + d
                    dn_prefetch(di + 1)
                    wd_, wdk = wdn[di % 2]
                    for (t0, w) in split_blocks(t0g, t1g, 320):
                        bank = pbi[0] % 6
                        pbi[0] += 1
                        for jj in range(NJ):
                            P.op("pe", lambda e, wd_=wd_, jj=jj, t0=t0, w=w, bank=bank: e.matmul(psum[:, bank, 0:w], lhsT=wd_[:, jj, :], rhs=act[:, jj, t0 - t0g:t0 - t0g + w],
                                 start=(jj == 0), stop=(jj == NJ - 1)), R=[(wdk,), (actk, jj)], W=[PS(bank)])
                        v = vof(t0)
                        xk = [("xT", d, tt) for tt in range(t0 // 128, (t0 + w + 127) // 128)]
                        P.op("dve", lambda e, d=d, t0=t0, w=w, bank=bank, v=v: e.scalar_tensor_tensor(out=xT[:, d, t0:t0 + w], in0=psum[:, bank, 0:w],
                             scalar=modT[:, l, 40 + d, v:v + 1], in1=xT[:, d, t0:t0 + w], op0=ALU.mult, op1=ALU.add), R=[PS(bank), ("modT", l)] + xk, W=xk)

            ffn_norm(0)
            for g_ in range(NG):
                ffn_group(g_)
            if adaF is not None:
                mod_finish(l + 1)
            P.barrier()

        ones_f = sb("ones_f", (128, 128), F32)
        P.op("pool", lambda e: e.memset(ones_f[:], 1.0), W=[("ones_f",)])

        phases = []
        for l in range(DEPTH):
            phases.append(("mix", l))
            phases.append(("ffn", l))
        nph = len(phases) if stop is None else stop
        for (kind, l) in phases[:nph]:
            if kind == "ffn":
                ffn_layer(l)
            elif l % 2 == 0:
                ab_layer(l)
            else:
                gqa_layer(l)
        A.reset()
        yst = [A.alloc("yst%d" % i, (8, 512), F32) for i in range(2)]
        nt = norm_tmp(7)
        for bi, (t0, w) in enumerate(split_blocks(0, NT)):
            ys, ysk = yst[bi % 2]
            norm_block(8, 0, 0, t0, w, ys, 0, ysk, nt, final=True)
            P.dma("sp", O["yT"].rearrange("(k p) t -> p k t", p=128)[:, :, t0:t0 + w], ys[:, :, 0:w], R=[(ysk,)])
        P.emit()
    return nc


def _consts():
    c = np.zeros((128, 5, 128), np.float32)
    c[:, 0, :] = np.eye(128, dtype=np.float32)
    c[:, 1, :] = 1.0
    c[0:64, 2, 0:64] = 1.0
    c[64:128, 2, 64:128] = 1.0
    for m in range(128):
        r = m % 32
        k = m + 16 if r < 16 else m - 16
        c[k, 3, m] = 1.0
    jj = np.arange(128)[:, None]
    ii = np.arange(128)[None, :]
    mk = np.zeros((128, 2, 4, 128), np.float32)
    mk[:, 0] = (jj <= ii).astype(np.float32)[:, None, :]
    mk[:, 1] = (jj >= ii).astype(np.float32)[:, None, :]
    p = np.arange(128)
    dd = p % 64
    axis = (dd >= 32).astype(np.int64)
    r = dd % 32
    fi = r % 16
    inv = (np.float32(10000.0) ** (-(np.arange(0, 32, 2, dtype=np.float32)) / np.float32(32))).astype(np.float32)
    t = np.arange(2048)
    pos = np.where(axis[:, None] == 0, (t // 64)[None, :], (t % 64)[None, :]).astype(np.float32)
    ang = (pos * inv[fi][:, None]).astype(np.float32)
    cosT = np.cos(ang).astype(np.float32)
    sinT = np.sin(ang).astype(np.float32)
    sinT = np.where((r < 16)[:, None], -sinT, sinT).astype(np.float32)
    return c, mk, cosT, sinT


def _t2_index():
    p = np.arange(128)
    kb, kc = p // 64, p % 64
    qq = np.arange(128)
    qrow, qc = qq // 64, qq % 64
    e = np.arange(7)
    d = 2 * e + 1
    dr = d[None, :, None] + kb[:, None, None] - qrow[None, None, :]
    dc = kc[:, None] - qc[None, :] + 15
    cs = np.clip(qc - 8, 0, 48)
    ok = (kc[:, None] >= cs[None, :]) & (kc[:, None] < cs[None, :] + 16)
    dcc = np.clip(dc, 0, 30)
    dr_b = np.broadcast_to(dr, (128, 7, 128))
    dc_b = np.broadcast_to(dcc[:, None, :], (128, 7, 128))
    ok_b = np.broadcast_to(ok[:, None, :], (128, 7, 128))
    return dr_b, dc_b, ok_b


def _chunkw(w, ncols_per_chunk=128):
    K, C = w.shape
    return np.ascontiguousarray(w.reshape(K // 128, 128, C // ncols_per_chunk, ncols_per_chunk).transpose(2, 1, 0, 3))


def _prep_shared(inp):
    f = lambda a: np.asarray(a, dtype=np.float32)
    S = {}
    ada_w = f(inp["ada_w"])
    S["adaW"] = np.ascontiguousarray(ada_w.reshape(4, 8, 128, 24, 256).transpose(0, 3, 2, 1, 4))
    ada_b = f(inp["ada_b"])
    ab = ada_b.reshape(4, 48, 128).transpose(2, 0, 1)
    S["adaB"] = np.ascontiguousarray(np.repeat(ab[:, :, :, None], 2, axis=3))
    S["gmix"] = np.ascontiguousarray(f(inp["norm_mix_g"]).reshape(4, 8, 128).transpose(2, 0, 1))
    S["gffn"] = np.ascontiguousarray(f(inp["norm_ffn_g"]).reshape(4, 8, 128).transpose(2, 0, 1))
    S["gfin"] = np.ascontiguousarray(f(inp["final_norm_g"]).reshape(8, 128).T)
    w_in = f(inp["ab_w_in"])
    wA = np.zeros((2, 3, 4, 128, 8, 128), np.float32)
    wG = np.zeros((2, 2, 128, 8, 800), np.float32)
    for l in range(2):
        for wi in range(3):
            wA[l, wi] = _chunkw(w_in[l][:, wi * 512:(wi + 1) * 512])
        W = w_in[l].reshape(8, 128, 3104).transpose(1, 0, 2)
        for pc in range(2):
            wG[l, pc, :, :, 0:128] = W[:, :, 1536 + pc * 128:1536 + (pc + 1) * 128]
            wG[l, pc, :, :, 128:256] = W[:, :, 1792 + pc * 128:1792 + (pc + 1) * 128]
            wG[l, pc, :, :, 256:512] = W[:, :, 2048 + pc * 256:2048 + (pc + 1) * 256]
            wG[l, pc, :, :, 512:768] = W[:, :, 2560 + pc * 256:2560 + (pc + 1) * 256]
            wG[l, pc, :, :, 768:800] = W[:, :, 3072:3104]
    S["wA"], S["wG"] = wA, wG
    w2x = np.zeros((2, 128, 2, 256), np.float32)
    w2x[:, 96:112, 0, :] = f(inp["gla_w2_fwd"])
    w2x[:, 112:128, 1, :] = f(inp["gla_w2_bwd"])
    S["w2x"] = w2x
    b2 = np.stack([f(inp["gla_b2_fwd"]), f(inp["gla_b2_bwd"])], axis=1)
    S["b2x"] = np.ascontiguousarray(b2.reshape(2, 2, 2, 128).transpose(3, 0, 1, 2))
    S["glag"] = np.ascontiguousarray(f(inp["gla_norm_g"]).reshape(2, 4, 128).transpose(2, 0, 1))
    S["wOab"] = np.stack([_chunkw(f(inp["ab_w_out"])[l]) for l in range(2)])
    rpb = f(inp["na_rpb"])
    dr_b, dc_b, ok_b = _t2_index()
    T2 = np.zeros((2, 4, 128, 2, 7, 128), np.float32)
    T2i = np.zeros((2, 4, 128, 2, 7, 128), np.float32)
    for l in range(2):
        for h in range(8):
            g = rpb[l, h][dr_b, dc_b]
            T2[l, h // 2, :, h % 2] = np.where(ok_b, g, np.float32(-30000.0))
            T2i[l, h // 2, :, h % 2] = np.where(ok_b & (dr_b >= 3) & (dr_b <= 10), g, np.float32(-30000.0))
    S["T2"] = T2
    S["T2i"] = T2i
    gq = f(inp["gqa_w_in"])
    S["wQ"] = np.stack([_chunkw(gq[l][:, 0:1024]) for l in range(2)])
    wK = np.zeros((2, 4, 128, 8, 128), np.float32)
    wV = np.zeros((2, 4, 128, 8, 64), np.float32)
    for l in range(2):
        W = gq[l].reshape(8, 128, 1536).transpose(1, 0, 2)
        for g in range(4):
            kcols = W[:, :, 1024 + g * 64:1024 + (g + 1) * 64]
            wK[l, g, :, :, 0:64] = kcols
            wK[l, g, :, :, 64:128] = kcols
            wV[l, g] = W[:, :, 1280 + g * 64:1280 + (g + 1) * 64]
    S["wK"], S["wV"] = wK, wV
    S["wOc"] = np.stack([_chunkw(f(inp["gqa_w_out"])[l]) for l in range(2)])
    qn = f(inp["gqa_q_norm_g"])
    kn = f(inp["gqa_k_norm_g"])
    S["qng"] = np.ascontiguousarray(np.concatenate([qn, qn], axis=1).T)
    S["kng"] = np.ascontiguousarray(np.concatenate([kn, kn], axis=1).T)
    up = f(inp["ffn_w_up"])
    wUp = np.zeros((4, NJ, 2, 128, 8, 128), np.float32)
    for l in range(4):
        c = _chunkw(up[l])
        wUp[l, :, 0] = c[0:NJ]
        wUp[l, :, 1] = c[NJ:2 * NJ]
    S["wUp"] = wUp
    dn = f(inp["ffn_w_down"])
    S["wDn"] = np.ascontiguousarray(dn.reshape(4, NJ, 128, 8, 128).transpose(0, 3, 2, 1, 4))
    cw = f(inp["ffn_conv_w"])
    cbias = f(inp["ffn_conv_b"])
    cvw = np.zeros((128, 4, 44, 4), np.float32)
    for t in range(3):
        cvw[:, :, :, t] = cw[:, t, :].reshape(4, 44, 128).transpose(2, 0, 1)
    cvw[:, :, :, 3] = cbias.reshape(4, 44, 128).transpose(2, 0, 1)
    S["convw"] = cvw
    c, mk, cosT, sinT = _consts()
    S["consts"], S["masks"], S["cosT"], S["sinT"] = c, mk, cosT, sinT
    return S


def _prep_core(inp, i):
    f = lambda a: np.asarray(a, dtype=np.float32)
    M = {}
    xp, xs = f(inp["x_prompt"]), f(inp["x_sample"])
    x = np.concatenate([xp[2 * i], xp[2 * i + 1], xs[i]], axis=0)
    M["xT"] = np.ascontiguousarray(x.T)
    cv = np.stack([f(inp["c_ctx"]), f(inp["c"])[i]], axis=1)
    M["cv"] = np.ascontiguousarray(cv.reshape(8, 128, 2).transpose(1, 0, 2))
    nk, nv = f(inp["cache_na_k"])[i], f(inp["cache_na_v"])[i]
    M["cnk"] = np.ascontiguousarray(nk.reshape(2, 4, 2, 256, 64).transpose(0, 1, 2, 4, 3).reshape(2, 4, 128, 256))
    M["cnv"] = np.ascontiguousarray(nv.reshape(2, 4, 2, 2, 128, 64).transpose(0, 1, 4, 3, 2, 5))
    sf, sb_ = f(inp["state_gla_fwd"])[i], f(inp["state_gla_bwd"])[i]
    lay = lambda s: np.ascontiguousarray(s.reshape(2, 2, 2, 64, 128).transpose(0, 2, 3, 1, 4).reshape(2, 128, 2, 128))
    M["sgf"], M["sgb"] = lay(sf), lay(sb_)
    gk, gv = f(inp["cache_gqa_k"])[i], f(inp["cache_gqa_v"])[i]
    gkT = gk.transpose(0, 1, 3, 2)
    M["cgk"] = np.ascontiguousarray(np.concatenate([gkT, gkT], axis=2))
    M["cgv"] = np.ascontiguousarray(gv.reshape(2, 4, 2, 128, 64).transpose(0, 1, 3, 2, 4))
    return M


_NC_CACHE = {}


def kernel(**inputs):
    if "nc" not in _NC_CACHE:
        _NC_CACHE["nc"] = build()
    nc = _NC_CACHE["nc"]
    S = _prep_shared(inputs)
    in_maps = []
    for i in range(8):
        m = dict(S)
        m.update(_prep_core(inputs, i))
        in_maps.append(m)
    res = run_bass_kernel_spmd(nc, in_maps, core_ids=list(range(8)))
    R = res.results
    y_prompt = np.zeros((16, 256, 1024), np.float32)
    y_sample = np.zeros((8, 2048, 1024), np.float32)
    na_k = np.zeros((16, 2, 8, 256, 64), np.float32)
    na_v = np.zeros((16, 2, 8, 256, 64), np.float32)
    g_f = np.zeros((16, 2, 4, 64, 128), np.float32)
    g_b = np.zeros((16, 2, 4, 64, 128), np.float32)
    q_k = np.zeros((16, 2, 4, 256, 64), np.float32)
    q_v = np.zeros((16, 2, 4, 256, 64), np.float32)
    for i in range(8):
        r = R[i]
        y = np.asarray(r["yT"]).T
        y_prompt[2 * i] = y[0:256]
        y_prompt[2 * i + 1] = y[256:512]
        y_sample[i] = y[512:]
        nak = np.asarray(r["o_nak"]).reshape(2, 4, 2, 64, 2, 256)
        na_k[2 * i:2 * i + 2] = nak.transpose(4, 0, 1, 2, 5, 3).reshape(2, 2, 8, 256, 64)
        nav = np.asarray(r["o_nav"]).reshape(2, 4, 2, 2, 128, 2, 64)
        na_v[2 * i:2 * i + 2] = nav.transpose(2, 0, 1, 5, 3, 4, 6).reshape(2, 2, 8, 256, 64)
        for nm, dst in (("o_gf", g_f), ("o_gb", g_b)):
            a = np.asarray(r[nm]).reshape(2, 2, 2, 64, 2, 128)
            dst[2 * i:2 * i + 2] = a.transpose(1, 0, 4, 2, 3, 5).reshape(2, 2, 4, 64, 128)
        gk = np.asarray(r["o_gk"]).reshape(2, 4, 64, 2, 256)
        q_k[2 * i:2 * i + 2] = gk.transpose(3, 0, 1, 4, 2)
        gv = np.asarray(r["o_gv"]).reshape(2, 4, 2, 2, 128, 64)
        q_v[2 * i:2 * i + 2] = gv.transpose(2, 0, 1, 3, 4, 5).reshape(2, 2, 4, 256, 64)
    return (y_prompt, y_sample, na_k, na_v, g_f, g_b, q_k, q_v)
```
